# Optimizing a Trainium2 kernel written in Bass

```python
import math
import jax
import jax.numpy as jnp
from jax import lax
import numpy as np

D_MODEL = 1024
BATCH = 4
SEQ = 8192
DEPTH = 4
DEC_BATCH = 16
DEC_SEQ = 32
PAST_LEN = 2048

CHUNK = 64
Q_BLOCK = 128
EPS = 1e-6
A_HEADS = 4
A_HEAD_DIM = 64
A_VDIM = 2 * A_HEAD_DIM
A_WIDTH = A_HEADS * A_VDIM
ROPE_DIM = A_HEAD_DIM // 4
ROPE_THETA = 500000.0
G_HEADS = 4
G_DK = 64
G_DV = 128
G_WIDTH = G_HEADS * G_DV
G_GATE_RANK = 16
G_GATE_TAU = 16.0
LRU_WIDTH = 512
LRU_BLOCKS = 8
LRU_BLOCK = LRU_WIDTH // LRU_BLOCKS
LRU_CONV = 4
LRU_C = 8.0
D_FF = 2816
FFN_CONV = 3
N_BRANCH = 3
PROJ_SPLITS = (2 * A_HEADS * A_HEAD_DIM, 2 * A_HEADS * A_HEAD_DIM, A_WIDTH, G_HEADS * G_DK, G_HEADS * G_DK, G_WIDTH, G_WIDTH, G_GATE_RANK, LRU_WIDTH, LRU_WIDTH)
PROJ_WIDTH = sum(PROJ_SPLITS)

kernel_name = 'hybrid_stream_encoder_step'


def rmsnorm(x, g):
    xf = x.astype(jnp.float32)
    y = xf * lax.rsqrt(jnp.mean(xf * xf, axis=-1, keepdims=True) + EPS) * g.astype(jnp.float32)
    return y.astype(x.dtype)


def head_rmsnorm(o, g):
    return o * lax.rsqrt(jnp.mean(o * o, axis=-1, keepdims=True) + EPS) * g.astype(jnp.float32)


def split_cols(z, widths):
    outs = []
    start = 0
    for w in widths:
        outs.append(z[..., start:start + w])
        start += w
    return outs


def causal_dwconv(x, buf, w, b):
    width = w.shape[0]
    t = x.shape[1]
    xp = jnp.concatenate([buf.astype(x.dtype), x], axis=1)
    y = b
    for j in range(width):
        y = y + xp[:, j:j + t] * w[j]
    return y, xp[:, t:]


def rope(x, pos):
    half = ROPE_DIM // 2
    inv = ROPE_THETA ** (-jnp.arange(half, dtype=jnp.float32) / half)
    ang = pos[:, None] * inv[None, :]
    cos = jnp.cos(ang)[:, None, None, :]
    sin = jnp.sin(ang)[:, None, None, :]
    x1 = x[..., :half]
    x2 = x[..., half:ROPE_DIM]
    return jnp.concatenate([x1 * cos - x2 * sin, x2 * cos + x1 * sin, x[..., ROPE_DIM:]], axis=-1)


def diff_scores(q, k, v, mask, lam):
    s = jnp.einsum('bqhcd,bkhcd->bhcqk', q, k) * (A_HEAD_DIM ** -0.5)
    s = jnp.where(mask, s, -jnp.inf)
    p = jax.nn.softmax(s, axis=-1)
    p = p[:, :, 0] - lam * p[:, :, 1]
    return jnp.einsum('bhqk,bkhe->bqhe', p, v)


def diff_attention(q, k, v, q_pos, k_pos, lam):
    b, t, h, _, d = q.shape
    qc = q_pos // CHUNK
    kc = k_pos // CHUNK
    if t > Q_BLOCK and t % Q_BLOCK == 0:
        nb = t // Q_BLOCK
        qb = q.reshape(b, nb, Q_BLOCK, h, 2, d).swapaxes(0, 1)
        qcb = qc.reshape(nb, Q_BLOCK)

        def block(args):
            qi, qci = args
            return diff_scores(qi, k, v, kc[None, :] <= qci[:, None], lam)

        o = lax.map(block, (qb, qcb))
        return o.swapaxes(0, 1).reshape(b, t, h, v.shape[-1])
    return diff_scores(q, k, v, kc[None, :] <= qc[:, None], lam)


def gla_chunk(s0, q, k, v, la):
    L = q.shape[1]
    b_cum = jnp.cumsum(la, axis=1)
    inter = jnp.einsum('bthk,bhkv->bthv', q * jnp.exp(b_cum), s0)
    tri = jnp.tril(jnp.ones((L, L), dtype=bool))
    dif = b_cum[:, :, None] - b_cum[:, None, :]
    decay = jnp.exp(jnp.where(tri[None, :, :, None, None], dif, -jnp.inf))
    att = jnp.einsum('bthk,btshk,bshk->bhts', q, decay, k)
    intra = jnp.einsum('bhts,bshv->bthv', att, v)
    b_last = b_cum[:, -1]
    s_new = jnp.exp(b_last)[..., None] * s0 + jnp.einsum('bshk,bshv->bhkv', k * jnp.exp(b_last[:, None] - b_cum), v)
    return s_new, inter + intra


def gla(q, k, v, la, s0):
    b, t, h, _ = q.shape
    L = min(CHUNK, t)
    nc = t // L

    def to_blocks(a):
        return a.reshape(b, nc, L, h, a.shape[-1]).swapaxes(0, 1)

    def step(s, inp):
        qc, kc, vc, lc = inp
        return gla_chunk(s, qc, kc, vc, lc)

    s_fin, o = lax.scan(step, s0, (to_blocks(q), to_blocks(k), to_blocks(v), to_blocks(la)))
    return s_fin, o.swapaxes(0, 1).reshape(b, t, h, v.shape[-1])


def _lin_combine(e1, e2):
    a1, b1 = e1
    a2, b2 = e2
    return a1 * a2, a2 * b1 + b2


def rg_lru(xc, h0, wa, ba, wx, bx, lam):
    b, t, w = xc.shape
    xf = xc.astype(jnp.float32)
    xb = xf.reshape(b, t, LRU_BLOCKS, LRU_BLOCK)
    r = jax.nn.sigmoid(jnp.einsum('btni,nij->btnj', xb, wa.astype(jnp.float32)).reshape(b, t, w) + ba.astype(jnp.float32))
    i = jax.nn.sigmoid(jnp.einsum('btni,nij->btnj', xb, wx.astype(jnp.float32)).reshape(b, t, w) + bx.astype(jnp.float32))
    log_a = -LRU_C * r * jax.nn.softplus(-lam.astype(jnp.float32))
    a = jnp.exp(log_a)
    u = jnp.sqrt(-jnp.expm1(2.0 * log_a)) * (i * xf)
    a_cum, h_zero = lax.associative_scan(_lin_combine, (a, u), axis=1)
    h = a_cum * h0[:, None, :] + h_zero
    return h, h[:, -1]


def layer(x, l, k_past, v_past, s_gla, lru_buf, h_lru, ffn_buf, params):
    (norm_mix, w_in, lambda_qk, attn_subln, w_gla_gate2, b_gla_gate, gla_norm, lru_conv_w, lru_conv_b,
     lru_wa, lru_ba, lru_wx, lru_bx, lru_lambda, w_branch_attn, w_branch_gla, w_branch_lru, w_merge, b_merge,
     w_out, norm_ffn, w_ffn_gate, ffn_conv_w, ffn_conv_b, w_ffn_up, w_ffn_down) = params
    f32 = jnp.float32
    dt = x.dtype
    b, t, _ = x.shape
    p_len = k_past.shape[1]
    xn = rmsnorm(x, norm_mix)
    z = xn @ w_in
    aq, ak, av, gq, gk, gv, gr, ga, lx, lg = split_cols(z, PROJ_SPLITS)

    q_pos = p_len + jnp.arange(t, dtype=jnp.int32)
    k_pos = jnp.arange(p_len + t, dtype=jnp.int32)
    rp = q_pos.astype(f32)
    q = rope(aq.astype(f32).reshape(b, t, A_HEADS, 2, A_HEAD_DIM), rp)
    k = rope(ak.astype(f32).reshape(b, t, A_HEADS, 2, A_HEAD_DIM), rp)
    v = av.astype(f32).reshape(b, t, A_HEADS, A_VDIM)
    k_all = jnp.concatenate([k_past.astype(f32), k], axis=1)
    v_all = jnp.concatenate([v_past.astype(f32), v], axis=1)
    lam_init = 0.8 - 0.6 * math.exp(-0.3 * l)
    lq = lambda_qk.astype(f32)
    lam = jnp.exp(jnp.sum(lq[0] * lq[1])) - jnp.exp(jnp.sum(lq[2] * lq[3])) + lam_init
    o = diff_attention(q, k_all, v_all, q_pos, k_pos, lam)
    o = head_rmsnorm(o, attn_subln) * (1.0 - lam_init)
    y_attn = o.reshape(b, t, A_WIDTH).astype(dt) @ w_branch_attn

    gq_ = gq.astype(f32).reshape(b, t, G_HEADS, G_DK) * (G_DK ** -0.5)
    gk_ = gk.astype(f32).reshape(b, t, G_HEADS, G_DK)
    gv_ = gv.astype(f32).reshape(b, t, G_HEADS, G_DV)
    la = jax.nn.log_sigmoid((ga @ w_gla_gate2 + b_gla_gate).astype(f32)).reshape(b, t, G_HEADS, G_DK) / G_GATE_TAU
    s_new, go = gla(gq_, gk_, gv_, la, s_gla.astype(f32))
    go = head_rmsnorm(go, gla_norm) * jax.nn.silu(gr.astype(f32).reshape(b, t, G_HEADS, G_DV))
    y_gla = go.reshape(b, t, G_WIDTH).astype(dt) @ w_branch_gla

    xc, lru_buf_new = causal_dwconv(lx, lru_buf, lru_conv_w, lru_conv_b)
    h, h_last = rg_lru(xc, h_lru.astype(f32), lru_wa, lru_ba, lru_wx, lru_bx, lru_lambda)
    y_lru = (h * jax.nn.gelu(lg.astype(f32))).astype(dt) @ w_branch_lru

    g = jax.nn.sigmoid((xn @ w_merge + b_merge).astype(f32)).reshape(b, t, N_BRANCH, D_MODEL)
    merged = g[:, :, 0] * y_attn.astype(f32) + g[:, :, 1] * y_gla.astype(f32) + g[:, :, 2] * y_lru.astype(f32)
    x = x + merged.astype(dt) @ w_out

    hn = rmsnorm(x, norm_ffn)
    gu = hn @ w_ffn_gate
    gc, ffn_buf_new = causal_dwconv(gu, ffn_buf, ffn_conv_w, ffn_conv_b)
    f = jax.nn.gelu(gc.astype(f32)) * (hn @ w_ffn_up).astype(f32)
    x = x + f.astype(dt) @ w_ffn_down
    return x, (k.astype(dt), v.astype(dt), s_new.astype(dt), lru_buf_new, h_last.astype(dt), ffn_buf_new)


def trunk(x, cache_k, cache_v, st_gla, st_lru_conv, st_lru_h, st_ffn_conv, layer_params, norm_final):
    outs = [[], [], [], [], [], []]
    for l in range(DEPTH):
        p = tuple(w[l] for w in layer_params)
        x, new = layer(x, l, cache_k[l], cache_v[l], st_gla[l], st_lru_conv[l], st_lru_h[l], st_ffn_conv[l], p)
        for lst, a in zip(outs, new):
            lst.append(a)
    y = rmsnorm(x, norm_final)
    return y, [jnp.stack(lst) for lst in outs]


def setup_inputs(seed: int = 0) -> dict:
    key = jax.random.key(seed)
    ks = iter(list(jax.random.split(key, 64)))
    f32 = jnp.float32

    def nrm(shape, scale):
        return jax.random.normal(next(ks), shape, f32) * scale

    def gain(shape):
        return 1.0 + nrm(shape, 0.02)

    u = jax.random.uniform(next(ks), (DEPTH, LRU_WIDTH), f32, 0.9, 0.999)
    base = u ** (1.0 / LRU_C)
    lru_lambda = jnp.log(base) - jnp.log1p(-base)
    return {
        'x_prompt': nrm((BATCH, SEQ, D_MODEL), 1.0),
        'x_sample': nrm((DEC_BATCH, DEC_SEQ, D_MODEL), 1.0),
        'cache_attn_k': nrm((DEPTH, DEC_BATCH, PAST_LEN, A_HEADS, 2, A_HEAD_DIM), 1.0),
        'cache_attn_v': nrm((DEPTH, DEC_BATCH, PAST_LEN, A_HEADS, A_VDIM), 1.0),
        'state_gla': nrm((DEPTH, DEC_BATCH, G_HEADS, G_DK, G_DV), 1.0),
        'state_lru_conv': nrm((DEPTH, DEC_BATCH, LRU_CONV - 1, LRU_WIDTH), 1.0),
        'state_lru_h': nrm((DEPTH, DEC_BATCH, LRU_WIDTH), 0.5),
        'state_ffn_conv': nrm((DEPTH, DEC_BATCH, FFN_CONV - 1, D_FF), 1.0),
        'norm_mix': gain((DEPTH, D_MODEL)),
        'w_in': nrm((DEPTH, D_MODEL, PROJ_WIDTH), D_MODEL ** -0.5),
        'lambda_qk': nrm((DEPTH, 4, A_HEAD_DIM), 0.1),
        'attn_subln': gain((DEPTH, A_VDIM)),
        'w_gla_gate2': nrm((DEPTH, G_GATE_RANK, G_HEADS * G_DK), G_GATE_RANK ** -0.5),
        'b_gla_gate': nrm((DEPTH, G_HEADS * G_DK), 0.01),
        'gla_norm': gain((DEPTH, G_DV)),
        'lru_conv_w': nrm((DEPTH, LRU_CONV, LRU_WIDTH), 0.5),
        'lru_conv_b': nrm((DEPTH, LRU_WIDTH), 0.01),
        'lru_wa': nrm((DEPTH, LRU_BLOCKS, LRU_BLOCK, LRU_BLOCK), LRU_BLOCK ** -0.5),
        'lru_ba': nrm((DEPTH, LRU_WIDTH), 0.01),
        'lru_wx': nrm((DEPTH, LRU_BLOCKS, LRU_BLOCK, LRU_BLOCK), LRU_BLOCK ** -0.5),
        'lru_bx': nrm((DEPTH, LRU_WIDTH), 0.01),
        'lru_lambda': lru_lambda,
        'w_branch_attn': nrm((DEPTH, A_WIDTH, D_MODEL), A_WIDTH ** -0.5),
        'w_branch_gla': nrm((DEPTH, G_WIDTH, D_MODEL), G_WIDTH ** -0.5),
        'w_branch_lru': nrm((DEPTH, LRU_WIDTH, D_MODEL), LRU_WIDTH ** -0.5),
        'w_merge': nrm((DEPTH, D_MODEL, N_BRANCH * D_MODEL), D_MODEL ** -0.5),
        'b_merge': nrm((DEPTH, N_BRANCH * D_MODEL), 0.01),
        'w_out': nrm((DEPTH, D_MODEL, D_MODEL), D_MODEL ** -0.5),
        'norm_ffn': gain((DEPTH, D_MODEL)),
        'w_ffn_gate': nrm((DEPTH, D_MODEL, D_FF), D_MODEL ** -0.5),
        'ffn_conv_w': nrm((DEPTH, FFN_CONV, D_FF), FFN_CONV ** -0.5),
        'ffn_conv_b': nrm((DEPTH, D_FF), 0.01),
        'w_ffn_up': nrm((DEPTH, D_MODEL, D_FF), D_MODEL ** -0.5),
        'w_ffn_down': nrm((DEPTH, D_FF, D_MODEL), D_FF ** -0.5),
        'norm_final': gain((D_MODEL,)),
    }


def reference(x_prompt, x_sample, cache_attn_k, cache_attn_v, state_gla, state_lru_conv, state_lru_h, state_ffn_conv,
              norm_mix, w_in, lambda_qk, attn_subln, w_gla_gate2, b_gla_gate, gla_norm, lru_conv_w, lru_conv_b,
              lru_wa, lru_ba, lru_wx, lru_bx, lru_lambda, w_branch_attn, w_branch_gla, w_branch_lru, w_merge, b_merge,
              w_out, norm_ffn, w_ffn_gate, ffn_conv_w, ffn_conv_b, w_ffn_up, w_ffn_down, norm_final):
    layer_params = (norm_mix, w_in, lambda_qk, attn_subln, w_gla_gate2, b_gla_gate, gla_norm, lru_conv_w, lru_conv_b,
                    lru_wa, lru_ba, lru_wx, lru_bx, lru_lambda, w_branch_attn, w_branch_gla, w_branch_lru, w_merge,
                    b_merge, w_out, norm_ffn, w_ffn_gate, ffn_conv_w, ffn_conv_b, w_ffn_up, w_ffn_down)
    bp = x_prompt.shape[0]
    dt = x_prompt.dtype
    zero_k = jnp.zeros((DEPTH, bp, 0, A_HEADS, 2, A_HEAD_DIM), dt)
    zero_v = jnp.zeros((DEPTH, bp, 0, A_HEADS, A_VDIM), dt)
    zero_gla = jnp.zeros((DEPTH, bp, G_HEADS, G_DK, G_DV), jnp.float32)
    zero_lru_conv = jnp.zeros((DEPTH, bp, LRU_CONV - 1, LRU_WIDTH), dt)
    zero_lru_h = jnp.zeros((DEPTH, bp, LRU_WIDTH), jnp.float32)
    zero_ffn_conv = jnp.zeros((DEPTH, bp, FFN_CONV - 1, D_FF), dt)
    y_prompt, (k_p, v_p, gla_p, lconv_p, lh_p, fconv_p) = trunk(
        x_prompt, zero_k, zero_v, zero_gla, zero_lru_conv, zero_lru_h, zero_ffn_conv, layer_params, norm_final)
    y_sample, (k_s, v_s, gla_s, lconv_s, lh_s, fconv_s) = trunk(
        x_sample, cache_attn_k, cache_attn_v, state_gla, state_lru_conv, state_lru_h, state_ffn_conv, layer_params, norm_final)
    return (y_prompt, y_sample, k_p, v_p, gla_p, lconv_p, lh_p, fconv_p, k_s, v_s, gla_s, lconv_s, lh_s, fconv_s)
```

```python
import math
from contextlib import ExitStack

import numpy as np
import concourse.bass as bass
import concourse.mybir as mybir
from concourse.bass_utils import run_bass_kernel_spmd

F32 = mybir.dt.float32
BF16 = mybir.dt.bfloat16
AF = mybir.ActivationFunctionType
ALU = mybir.AluOpType

ENGS = ["tensor", "vector", "scalar", "gpsimd", "sync"]

CFG = {"n_ptiles": 16, "depth": 4, "samples": True}

D = 1024
DEPTH = 4
SEQ = 8192
TTP = 512
PAST = 2048
DSEQ = 32
EPS = 1e-6
AQ, AK, AV, GQ, GK, GV, GR, GA, LX, LG, PW = 0, 512, 1024, 1536, 1792, 2048, 2560, 3072, 3088, 3600, 4112
DFF = 2816
NFC = 22
WSLOT = 4224


class Tile:
    __slots__ = ("name", "last_w", "readers", "psum")

    def __init__(self, name):
        self.name = name
        self.last_w = None
        self.readers = []
        self.psum = False


class Op:
    __slots__ = ("eng", "fn", "deps", "dma", "signal", "sem", "val", "idx", "phase")


class Sched:
    RING = 12

    def __init__(self, nc):
        self.nc = nc
        self.ops = []
        self.stack = ExitStack()
        self.ntile = 0
        self.phase = 'init'

    def sbuf(self, name, shape, dtype):
        return self.stack.enter_context(self.nc.sbuf_tensor("sb_" + name, list(shape), dtype))

    def psum(self, name, shape, dtype):
        return self.stack.enter_context(self.nc.psum_tensor("pp_" + name, list(shape), dtype))

    def tile(self, name=None):
        self.ntile += 1
        return Tile(name or f"t{self.ntile}")

    def op(self, eng, fn, reads=(), writes=(), dma=False):
        o = Op()
        o.eng = eng
        o.fn = fn
        o.dma = dma
        o.signal = False
        o.sem = None
        o.val = 0
        o.idx = len(self.ops)
        o.phase = self.phase
        deps = set()
        ops = self.ops
        for t in reads:
            if t.last_w is not None:
                deps.add(t.last_w)
            if t.psum:
                for r in t.readers:
                    if ops[r].eng != eng:
                        deps.add(r)
        for t in writes:
            if t.last_w is not None:
                deps.add(t.last_w)
            for r in t.readers:
                ro = ops[r]
                if (not dma) and (not ro.dma) and ro.eng == eng:
                    continue
                deps.add(r)
        if eng == "tensor" and not dma:
            deps = {d for d in deps if ops[d].dma or ops[d].eng != "tensor"}
        o.deps = deps
        for t in reads:
            if not dma:
                t.readers = [r for r in t.readers if ops[r].dma or ops[r].eng != eng]
            t.readers.append(o.idx)
        for t in writes:
            t.last_w = o.idx
            t.readers = []
        ops.append(o)
        return o

    def emit(self):
        nc = self.nc
        ops = self.ops
        stack = self.stack
        dma_count = {e: 0 for e in ENGS}
        dma_hist = {e: [] for e in ENGS}
        for o in ops:
            if o.dma:
                j = dma_count[o.eng]
                if j >= self.RING:
                    o.deps.add(dma_hist[o.eng][j - self.RING])
                dma_hist[o.eng].append(o.idx)
                dma_count[o.eng] += 1
        for o in ops:
            for d in o.deps:
                ops[d].signal = True
        eng_sem = {e: stack.enter_context(nc.semaphore(f"s_{e}")) for e in ENGS}
        ring_sems = {e: [stack.enter_context(nc.semaphore(f"d_{e}_{i}")) for i in range(self.RING)]
                     for e in ENGS if dma_count[e] > 0}
        cnt = {e: 0 for e in ENGS}
        dj = {e: 0 for e in ENGS}
        for o in ops:
            if o.dma:
                j = dj[o.eng]
                o.sem = ring_sems[o.eng][j % self.RING]
                o.val = 16 * (j // self.RING + 1)
                dj[o.eng] += 1
            elif o.signal:
                cnt[o.eng] += 1
                o.sem = eng_sem[o.eng]
                o.val = cnt[o.eng]
        per_eng = {e: [] for e in ENGS}
        for o in ops:
            per_eng[o.eng].append(o)
        self.stats = {e: len(per_eng[e]) for e in ENGS}
        block = stack.enter_context(nc.Block())

        def make(ename):
            def body(eng):
                waited = {}
                for o in per_eng[ename]:
                    need = {}
                    for d in o.deps:
                        do = ops[d]
                        k = id(do.sem)
                        if k not in need or need[k][1] < do.val:
                            need[k] = (do.sem, do.val)
                    for k, (sem, val) in need.items():
                        if waited.get(k, 0) >= val:
                            continue
                        eng.wait_ge(sem, val)
                        waited[k] = val
                    ins = o.fn(eng)
                    if o.dma:
                        ins.then_inc(o.sem, 16)
                    elif o.signal:
                        ins.then_inc(o.sem, 1)
            return body

        for e in ENGS:
            if per_eng[e]:
                getattr(block, e)(make(e))
        stack.close()


class Buf:
    __slots__ = ("t", "k")

    def __init__(self, t, k):
        self.t = t
        self.k = k


def build_program(cfg):
    nc = bass.Bass("TRN2", target_bir_lowering=False)
    S = Sched(nc)
    NL = cfg["depth"]

    def din(name, shape, dt=F32):
        return nc.dram_tensor(name, list(shape), dt, kind="ExternalInput").ap()

    def dout(name, shape, dt=F32):
        return nc.dram_tensor(name, list(shape), dt, kind="ExternalOutput").ap()

    def dscr(name, shape, dt=BF16):
        return nc.dram_tensor(name, list(shape), dt).ap()

    xp = din("xp", [SEQ, D])
    xs = din("xs", [2, DSEQ, D])
    ck = din("ck", [DEPTH, 2, PAST, 512])
    cv = din("cv", [DEPTH, 2, PAST, 512])
    sg = din("sg", [DEPTH, 2, 4, 64, 128])
    slc = din("slc", [DEPTH, 2, 3, 512])
    slh = din("slh", [DEPTH, 2, 512])
    sfc = din("sfc", [DEPTH, 2, 2, DFF])
    W = {}
    for name, shape in [
        ("norm_mix", [NL, D]), ("w_in", [NL, D, PW]), ("lambda_qk", [NL, 4, 64]),
        ("attn_subln", [NL, 128]), ("w_gla_gate2", [NL, 16, 256]), ("b_gla_gate", [NL, 256]),
        ("gla_norm", [NL, 128]), ("lru_conv_w", [NL, 4, 512]), ("lru_conv_b", [NL, 512]),
        ("lru_wa", [NL, 8, 64, 64]), ("lru_ba", [NL, 512]), ("lru_wx", [NL, 8, 64, 64]),
        ("lru_bx", [NL, 512]), ("lru_lambda", [NL, 512]), ("w_branch_attn", [NL, 512, D]),
        ("w_branch_gla", [NL, 512, D]), ("w_branch_lru", [NL, 512, D]), ("w_merge", [NL, D, 3 * D]),
        ("b_merge", [NL, 3 * D]), ("w_out", [NL, D, D]), ("norm_ffn", [NL, D]),
        ("w_ffn_gate", [NL, D, DFF]), ("ffn_conv_w", [NL, 3, DFF]), ("ffn_conv_b", [NL, DFF]),
        ("w_ffn_up", [NL, D, DFF]), ("w_ffn_down", [NL, DFF, D]), ("norm_final", [D]),
    ]:
        W[name] = din(name, shape)
    ropep = din("ropep", [SEQ, 128])
    ropes = din("ropes", [DSEQ, 128])
    cmask_d = din("cmask", [1, TTP])
    tri_d = din("tri", [64, 64])
    ident_d = din("ident", [128, 128])

    yp = dout("yp", [SEQ, D])
    ys = dout("ys", [2, DSEQ, D])
    kp = dout("kp", [DEPTH, SEQ, 512])
    vp = dout("vp", [DEPTH, SEQ, 512])
    glap = dout("glap", [DEPTH, 4, 64, 128])
    lcp = dout("lcp", [DEPTH, 3, 512])
    lhp = dout("lhp", [DEPTH, 512])
    fcp = dout("fcp", [DEPTH, 2, DFF])
    ks_o = dout("ks", [DEPTH, 2, DSEQ, 512])
    vs_o = dout("vs", [DEPTH, 2, DSEQ, 512])
    glas = dout("glas", [DEPTH, 2, 4, 64, 128])
    lcs = dout("lcs", [DEPTH, 2, 3, 512])
    lhs_o = dout("lhs", [DEPTH, 2, 512])
    fcs = dout("fcs", [DEPTH, 2, 2, DFF])
    out_tiles = []

    def otile():
        t = S.tile()
        out_tiles.append(t)
        return t

    ktscr = dscr("ktscr", [DEPTH, 4, 128, SEQ])
    vscr = dscr("vscr", [DEPTH, 4, SEQ // 512, 128, 512])
    kt_t = [[[S.tile() for u in range(16)] for h in range(4)] for l in range(DEPTH)]
    v_t = [[[S.tile() for u in range(16)] for h in range(4)] for l in range(DEPTH)]

    units = {}

    def mk(name, shape, dt):
        return Buf(S.sbuf(name, shape, dt), S.tile(name))

    xT = mk("xT", [128, 8, TTP], F32)
    NP = 16
    pool_t = S.sbuf("pool", [128, NP, TTP], F32)
    pool_k = [S.tile(f"pool{i}") for i in range(NP)]
    pool_i = [0]

    class Tmp:
        __slots__ = ("f", "b", "k")

    def tmp():
        i = pool_i[0] % NP
        pool_i[0] += 1
        r = Tmp()
        r.f = pool_t[:, i, :]
        r.b = pool_t[:, i, :].bitcast(BF16)
        r.k = pool_k[i]
        return r

    def tmp8():
        if (pool_i[0] % NP) + 8 > NP:
            pool_i[0] += NP - (pool_i[0] % NP)
        base = pool_i[0] % NP
        ks_ = [tmp().k for _ in range(8)]
        return base, ks_

    xn = mk("xn", [128, 8, TTP], BF16)
    NWR = 5
    wring = [mk(f"wring{i}", [128, WSLOT], BF16) for i in range(NWR)]
    wr_i = [0]
    QT = mk("QT", [128, 4, TTP], BF16)
    KTc = mk("KTc", [128, 4, TTP], BF16)
    qtl = mk("qtl", [64, 4, TTP], BF16)
    ktl = mk("ktl", [64, 4, TTP], BF16)
    v_bf = mk("v_bf", [128, 4, 512], BF16)
    gv_tok = mk("gv_tok", [64, 8, 512], BF16)
    ktok = mk("ktok", [64, 8, 256], BF16)
    fT = [Buf(None, S.tile(f"fT{j}")) for j in range(NFC)]
    fT_t = S.sbuf("fT", [128, NFC, TTP], BF16)
    mergedb = Buf(fT_t[:, 0:8, :], S.tile("mergedb"))
    oattn = Buf(fT_t[:, 8:12, :], S.tile("oattn"))
    gon = Buf(fT_t[:, 12:16, :], S.tile("gon"))
    hl = Buf(fT_t[:, 16:20, :], S.tile("hl"))
    for j in range(NFC):
        if j < 8:
            fT[j].k = mergedb.k
        elif j < 12:
            fT[j].k = oattn.k
        elif j < 16:
            fT[j].k = gon.k
        elif j < 20:
            fT[j].k = hl.k
    pring = [mk(f"pring{i}", [128, TTP], BF16) for i in range(4)]
    pr_i = [0]
    kring = [mk(f"kring{i}", [128, 512], BF16) for i in range(3)]
    vring = [mk(f"vring{i}", [128, 4, 128], BF16) for i in range(3)]
    kv_i = [0]
    kst = [mk(f"kst{i}", [128, 4, 128], BF16) for i in range(1)]
    kst_i = [0]
    gS = [mk(f"gS{l}", [64, 4, 128], F32) for l in range(DEPTH)]
    gSb = [mk(f"gSb{l}", [64, 4, 128], BF16) for l in range(DEPTH)]
    gSk = [[S.tile(f"gSk{l}_{h}") for h in range(4)] for l in range(DEPTH)]
    gSbk = [[S.tile(f"gSbk{l}_{h}") for h in range(4)] for l in range(DEPTH)]
    hst = [mk(f"hst{l}", [128, 4], F32) for l in range(DEPTH)]
    lxh = [mk(f"lxh{l}", [128, 4, 3], F32) for l in range(DEPTH)]
    guh = [mk(f"guh{l}", [128, NFC, 2], F32) for l in range(DEPTH)]
    parA = [mk(f"parA{l}", [128, 128], F32) for l in range(DEPTH)]
    parB = [mk(f"parB{l}", [128, 64], F32) for l in range(DEPTH)]
    lamc = [mk(f"lamc{l}", [128, 8], F32) for l in range(DEPTH)]
    w2b = [mk(f"w2b{l}", [16, 256], BF16) for l in range(DEPTH)]
    identf = mk("identf", [128, 128], F32)
    identb = mk("identb", [128, 128], BF16)
    onesb = mk("onesb", [128, 128], BF16)
    tri = mk("tri", [64, 64], F32)
    cmask = mk("cmask_sb", [128, TTP], F32)
    rope_sb = mk("rope_sb", [128, 4, 128], F32)
    lxT = mk("lxT", [128, 4, 3 + TTP], F32)
    gcs = mk("gcs", [64, 4, TTP], F32)
    gab = mk("gab", [16, TTP], BF16)
    ebl = mk("ebl", [64, 4, 8], F32)

    PS = [Buf(S.psum(f"ps{i}", [128, 512], F32), S.tile(f"ps{i}")) for i in range(8)]
    for b_ in PS:
        b_.k.psum = True
    ps_i = [0]

    def psn():
        i = ps_i[0] % 8
        ps_i[0] += 1
        return PS[i]

    def dma(out, in_, reads, writes, eng="sync", **kw):
        S.op(eng, lambda e: e.dma_start(out=out, in_=in_, **kw), reads=reads, writes=writes, dma=True)

    def mm(out, lhsT, rhs, start, stop, reads, writes):
        S.op("tensor", lambda e: e.matmul(out, lhsT=lhsT, rhs=rhs, start=start, stop=stop), reads=reads, writes=writes)

    def tr(out, in_, ident, reads, writes):
        S.op("tensor", lambda e: e.transpose(out=out, in_=in_, identity=ident), reads=reads, writes=writes)

    def act(out, in_, func, reads, writes, bias=None, scale=None):
        kw = {}
        if bias is not None:
            kw["bias"] = bias
        if scale is not None:
            kw["scale"] = scale
        S.op("scalar", lambda e: e.activation(out=out, in_=in_, func=func, **kw), reads=reads, writes=writes)

    def tt(out, in0, in1, op, reads, writes, eng="vector"):
        S.op(eng, lambda e: e.tensor_tensor(out=out, in0=in0, in1=in1, op=op), reads=reads, writes=writes)

    def ts(out, in0, s1, s2, op0, op1, reads, writes, eng="vector"):
        if op1 is None:
            S.op(eng, lambda e: e.tensor_scalar(out=out, in0=in0, scalar1=s1, scalar2=None, op0=op0), reads=reads, writes=writes)
        else:
            S.op(eng, lambda e: e.tensor_scalar(out=out, in0=in0, scalar1=s1, scalar2=s2, op0=op0, op1=op1), reads=reads, writes=writes)

    def stt(out, in0, scalar, in1, op0, op1, reads, writes):
        S.op("vector", lambda e: e.scalar_tensor_tensor(out=out, in0=in0, scalar=scalar, in1=in1, op0=op0, op1=op1), reads=reads, writes=writes)

    def cp(out, in_, reads, writes, eng="vector"):
        S.op(eng, lambda e: e.tensor_copy(out=out, in_=in_), reads=reads, writes=writes)

    def mset(ap, val, reads, writes, eng="gpsimd"):
        S.op(eng, lambda e: e.memset(ap, val), reads=reads, writes=writes)

    def scan(out, d0, d1, init, reads, writes):
        S.op("vector", lambda e: e.tensor_tensor_scan(out=out, data0=d0, data1=d1, initial=init, op0=ALU.mult, op1=ALU.add), reads=reads, writes=writes)

    def recip(out, in_, reads, writes):
        S.op("vector", lambda e: e.reciprocal(out=out, in_=in_), reads=reads, writes=writes)

    ev_i = [0]

    def evac(out, in_, reads, writes):
        ev_i[0] += 1
        if ev_i[0] % 2:
            act(out, in_, AF.Copy, reads, writes)
        else:
            cp(out, in_, reads, writes)

    dma(identf.t[:], ident_d[:, :], [], [identf.k])
    cp(identb.t[:], identf.t[:], [identf.k], [identb.k])
    mset(onesb.t[:], 1.0, [], [onesb.k])
    dma(tri.t[:], tri_d[:, :], [], [tri.k])
    dma(cmask.t[:], cmask_d.rearrange("a b -> (a b)").partition_broadcast(128), [], [cmask.k])

    def wview(name, l):
        return W[name][l].rearrange("(kc p) n -> p kc n", p=128)

    cast_done = set()

    def cast_layer(l):
        if l in cast_done or l >= NL:
            return
        cast_done.add(l)
        zt = tmp()
        mset(zt.f, 0.0, [], [zt.k])
        ul = {}

        def unit(name, width):
            ap = dscr(f"wu_{l}_{name}", [128, width])
            k = S.tile(f"wu_{l}_{name}")
            ul[name] = (ap, k, width)
            return ap, k

        win = wview("w_in", l)
        for name, c0, c1 in [("U0", 0, 512), ("U1", 512, 1024), ("U2", 1024, 1536), ("U3", 1536, 2048),
                             ("U4", 2048, 2560), ("U5", 2560, 3072), ("U6", 3072, 3600), ("U7", 3600, 4112)]:
            w = c1 - c0
            ap, k = unit(name, 8 * w)
            dma(ap.rearrange("p (kc n) -> p kc n", kc=8), win[:, :, c0:c1], [], [k], eng="gpsimd")
        ap, k = unit("U8", 8 * 128)
        dma(ap[:, :], zt.b[:, 0:1024], [zt.k], [k], eng="gpsimd")
        a3 = ap.rearrange("p (c n) -> p c n", c=8)
        for gi, wn in enumerate(["lru_wa", "lru_wx"]):
            for c in range(4):
                for half in range(2):
                    dma(a3[half * 64:(half + 1) * 64, gi * 4 + c, half * 64:(half + 1) * 64],
                        W[wn][l, 2 * c + half], [], [k], eng="gpsimd")
        wm = W["w_merge"][l].rearrange("(kc p) (j m f) -> p kc j m f", p=128, j=3, m=8)
        wbr = [W[n][l].rearrange("(kc p) (m f) -> p kc m f", p=128, m=8) for n in ("w_branch_attn", "w_branch_gla", "w_branch_lru")]
        for m in range(8):
            ap, k = unit(f"M{m}", 3072)
            a_m = ap[:, 0:3072].rearrange("p (kc j f) -> p kc j f", kc=8, j=3)
            for j in range(3):
                dma(a_m[:, :, j, :], wm[:, :, j, m, :], [], [k], eng="gpsimd")
            ap, k = unit(f"B{m}", 1536)
            a_b = ap[:, 0:1536].rearrange("p (kc j f) -> p kc j f", kc=4, j=3)
            for j in range(3):
                dma(a_b[:, :, j, :], wbr[j][:, :, m, :], [], [k], eng="gpsimd")
        wo = wview("w_out", l)
        for i in range(2):
            ap, k = unit(f"O{i}", 4096)
            dma(ap.rearrange("p (kc n) -> p kc n", kc=8), wo[:, :, i * 512:(i + 1) * 512], [], [k], eng="gpsimd")
        wg = wview("w_ffn_gate", l)
        wu = wview("w_ffn_up", l)
        for i in range(11):
            ap, k = unit(f"F{i}", 4096)
            a4 = ap.rearrange("p (kc g n) -> p kc g n", kc=8, g=2)
            dma(a4[:, :, 0, :], wg[:, :, i * 256:(i + 1) * 256], [], [k], eng="gpsimd")
            dma(a4[:, :, 1, :], wu[:, :, i * 256:(i + 1) * 256], [], [k], eng="gpsimd")
        wd = wview("w_ffn_down", l)
        for m in range(8):
            ap, k = unit(f"D{m}", NFC * 128)
            dma(ap.rearrange("p (kc n) -> p kc n", kc=NFC), wd[:, :, m * 128:(m + 1) * 128], [], [k], eng="gpsimd")
        units[l] = ul


    cast_layer(0)

    for l in range(NL):
        st = tmp()
        st_t = st.f[:, 0:128]
        mset(st_t, 0.0, [], [st.k])
        rows = [
            (0, 66, W["ffn_conv_w"][l].rearrange("j (c f) -> (j c) f", f=128)),
            (66, 22, W["ffn_conv_b"][l].rearrange("(c f) -> c f", f=128)),
            (88, 24, W["b_merge"][l].rearrange("(c f) -> c f", f=128)),
            (112, 8, W["norm_mix"][l].rearrange("(c f) -> c f", f=128)),
            (120, 8, W["norm_ffn"][l].rearrange("(c f) -> c f", f=128)),
        ]
        for r0, n, src in rows:
            dma(st_t[r0:r0 + n, :], src, [], [st.k])
        p = psn()
        tr(p.t[:, 0:128], st_t, identf.t[:], [st.k, identf.k], [p.k])
        cp(parA[l].t[:], p.t[:, 0:128], [p.k], [parA[l].k])
        st = tmp()
        st_t = st.f[:, 0:128]
        mset(st_t, 0.0, [], [st.k])
        rows = [
            (0, 16, W["lru_conv_w"][l].rearrange("j (c f) -> (j c) f", f=128), 128),
            (16, 4, W["lru_conv_b"][l].rearrange("(c f) -> c f", f=128), 128),
            (20, 4, W["lru_ba"][l].rearrange("(c f) -> c f", f=128), 128),
            (24, 4, W["lru_bx"][l].rearrange("(c f) -> c f", f=128), 128),
            (28, 4, W["lru_lambda"][l].rearrange("(c f) -> c f", f=128), 128),
            (32, 1, W["attn_subln"][l].rearrange("(c f) -> c f", f=128), 128),
            (33, 1, W["gla_norm"][l].rearrange("(c f) -> c f", f=128), 128),
            (34, 4, W["b_gla_gate"][l].rearrange("(c f) -> c f", f=64), 64),
            (38, 8, W["norm_final"].rearrange("(c f) -> c f", f=128), 128),
        ]
        for r0, n, src, wd_ in rows:
            dma(st_t[r0:r0 + n, 0:wd_], src, [], [st.k])
        p = psn()
        tr(p.t[:, 0:128], st_t, identf.t[:], [st.k, identf.k], [p.k])
        cp(parB[l].t[:], p.t[:, 0:64], [p.k], [parB[l].k])
        lam_init = 0.8 - 0.6 * math.exp(-0.3 * l)
        lq = tmp()
        dma(lq.f[:, 0:256], W["lambda_qk"][l].rearrange("a b -> (a b)").partition_broadcast(128), [], [lq.k])
        t1 = tmp()
        tt(t1.f[:, 0:64], lq.f[:, 0:64], lq.f[:, 64:128], ALU.mult, [lq.k], [t1.k])
        tt(t1.f[:, 64:128], lq.f[:, 128:192], lq.f[:, 192:256], ALU.mult, [lq.k, t1.k], [t1.k])
        S.op("vector", lambda e, t1=t1: e.tensor_reduce(out=t1.f[:, 128:130], in_=t1.f[:, 0:128].rearrange("p (a b) -> p a b", a=2), axis=mybir.AxisListType.X, op=ALU.add), reads=[t1.k], writes=[t1.k])
        act(t1.f[:, 130:132], t1.f[:, 128:130], AF.Exp, [t1.k], [t1.k])
        tt(t1.f[:, 132:133], t1.f[:, 131:132], t1.f[:, 130:131], ALU.subtract, [t1.k], [t1.k])
        ts(lamc[l].t[:, 0:1], t1.f[:, 132:133], -lam_init, None, ALU.add, None, [t1.k], [lamc[l].k])
        ts(lamc[l].t[:, 1:2], parB[l].t[:, 32:33], 1.0 - lam_init, None, ALU.mult, None, [parB[l].k, lamc[l].k], [lamc[l].k])
        act(t1.f[:, 140:144], parB[l].t[:, 28:32], AF.Exp, [parB[l].k, t1.k], [t1.k], scale=-1.0)
        act(t1.f[:, 144:148], t1.f[:, 140:144], AF.Ln, [t1.k], [t1.k], bias=1.0)
        ts(lamc[l].t[:, 2:6], t1.f[:, 144:148], -8.0, None, ALU.mult, None, [t1.k, lamc[l].k], [lamc[l].k])
        ts(parB[l].t[:, 46:50], parB[l].t[:, 34:38], -1.0, None, ALU.mult, None, [parB[l].k], [parB[l].k])
        t2 = tmp()
        dma(t2.f[0:16, 0:256], W["w_gla_gate2"][l], [], [t2.k])
        cp(w2b[l].t[:], t2.f[0:16, 0:256], [t2.k], [w2b[l].k])

    def wload(l, name):
        ap, k, width = units[l][name]
        slot = wring[wr_i[0] % NWR]
        wr_i[0] += 1
        dma(slot.t[:, 0:width], ap[:, :], [k], [slot.k])
        return slot

    def run_tile(seq, TT, BP, NB, L, NCK, xsrc, ysrc, rope_src, t0, units_past, is_last, outs):
        S.phase = 'xload'
        base, xk = tmp8()

        def xtok(bi):
            return pool_t[0:BP, base + 2 * bi: base + 2 * bi + 2, :].rearrange("p a n -> p (a n)")

        for bi in range(NB):
            dma(xtok(bi), xsrc[bi * BP:(bi + 1) * BP, :], [], [xk[2 * bi], xk[2 * bi + 1]])
        dma(rope_sb.t[0:BP, 0:NB, :], rope_src.rearrange("(b p) f -> p b f", p=BP), [], [rope_sb.k])
        for c in range(8):
            p = psn()
            for bi in range(NB):
                tr(p.t[:, bi * BP:(bi + 1) * BP], xtok(bi)[:, c * 128:(c + 1) * 128], identf.t[0:BP, 0:BP],
                   [xk[2 * bi], xk[2 * bi + 1], identf.k], [p.k])
            evac(xT.t[:, c, 0:TT], p.t[:, 0:TT], [p.k], [xT.k])

        def rmsnorm_to(dst, gcol):
            p = psn()
            for c in range(8):
                sq = tmp()
                act(sq.b[:, 0:TT], xT.t[:, c, 0:TT], AF.Square, [xT.k], [sq.k])
                mm(p.t[:, 0:TT], onesb.t[:, :], sq.b[:, 0:TT], c == 0, c == 7, [onesb.k, sq.k], [p.k])
            rs = tmp()
            act(rs.f[:, 0:TT], p.t[:, 0:TT], AF.Ln, [p.k], [rs.k], bias=eps_ap, scale=1.0 / D)
            act(rs.f[:, 0:TT], rs.f[:, 0:TT], AF.Exp, [rs.k], [rs.k], scale=-0.5)
            for c in range(8):
                stt(dst.t[:, c, 0:TT], xT.t[:, c, 0:TT], gcol(c), rs.f[:, 0:TT], ALU.mult, ALU.mult, [xT.k, rs.k], [dst.k])

        for l in range(NL):
            cast_layer(l + 1)
            pA, pB = parA[l], parB[l]
            if t0 == 0:
                if seq == 'p':
                    mset(gS[l].t[:], 0.0, [], gSk[l])
                    mset(gSb[l].t[:], 0.0, [], gSbk[l])
                    mset(hst[l].t[:], 0.0, [], [hst[l].k])
                    mset(lxh[l].t[:], 0.0, [], [lxh[l].k])
                    mset(guh[l].t[:], 0.0, [], [guh[l].k])
                else:
                    dma(gS[l].t[:], sg[l, seq].rearrange("h k v -> k h v"), [], gSk[l])
                    cp(gSb[l].t[:], gS[l].t[:], gSk[l], gSbk[l], eng="gpsimd")
                    dma(hst[l].t[:], slh[l, seq].rearrange("(c p) -> p c", p=128), [], [hst[l].k], allow_slow_non_contiguous=True)
                    for j in range(3):
                        dma(lxh[l].t[:, :, j], slc[l, seq, j].rearrange("(c p) -> p c", p=128), [], [lxh[l].k], allow_slow_non_contiguous=True)
                    for j in range(2):
                        dma(guh[l].t[:, :, j], sfc[l, seq, j].rearrange("(c p) -> p c", p=128), [], [guh[l].k], allow_slow_non_contiguous=True)

            S.phase = 'norm1'
            rmsnorm_to(xn, lambda c: pA.t[:, 112 + c:113 + c])
            xn_r = [xn.k, pA.k]

            S.phase = 'qkv'
            for ui, uname in enumerate(["U0", "U1", "U2"]):
                ws = wload(l, uname)
                w3 = ws.t[:, 0:4096].rearrange("p (kc n) -> p kc n", kc=8)
                tbs = []
                for bi in range(NB):
                    p = psn()
                    for kc in range(8):
                        mm(p.t[0:BP, :], xn.t[:, kc, bi * BP:(bi + 1) * BP], w3[:, kc, :], kc == 0, kc == 7, [xn.k, ws.k], [p.k])
                    tk = tmp()
                    act(tk.f[0:BP, :], p.t[0:BP, :], AF.Copy, [p.k], [tk.k])
                    if ui < 2:
                        x4 = tk.f[0:BP, :].rearrange("p (a d) -> p a d", d=64)
                        cs4 = rope_sb.t[0:BP, bi, 0:64].rearrange("p (a d) -> p a d", d=8)
                        sn4 = rope_sb.t[0:BP, bi, 64:128].rearrange("p (a d) -> p a d", d=8)
                        tq = tmp()
                        q4 = tq.f[0:BP, 0:256].rearrange("p (j a d) -> p j a d", j=4, d=8)
                        rk = [tk.k, rope_sb.k, tq.k]
                        tt(q4[:, 0], x4[:, :, 0:8], cs4, ALU.mult, rk, [tq.k])
                        tt(q4[:, 1], x4[:, :, 8:16], sn4, ALU.mult, rk, [tq.k])
                        tt(q4[:, 2], x4[:, :, 8:16], cs4, ALU.mult, rk, [tq.k])
                        tt(q4[:, 3], x4[:, :, 0:8], sn4, ALU.mult, rk, [tq.k])
                        tt(x4[:, :, 0:8], q4[:, 0], q4[:, 1], ALU.subtract, [tq.k, tk.k], [tk.k])
                        tt(x4[:, :, 8:16], q4[:, 2], q4[:, 3], ALU.add, [tq.k, tk.k], [tk.k])
                        tb = tmp()
                        cp(tb.b[0:BP, 0:512], tk.f[0:BP, :], [tk.k], [tb.k])
                        tbs.append(tb)
                        if ui == 1:
                            ot = otile()
                            dma(outs["k"][l][bi * BP:(bi + 1) * BP, :], tk.f[0:BP, :], [tk.k], [ot], eng="gpsimd")
                    else:
                        ot = otile()
                        dma(outs["v"][l][bi * BP:(bi + 1) * BP, :], tk.f[0:BP, :], [tk.k], [ot], eng="gpsimd")
                        cp(v_bf.t[0:BP, bi, :], tk.f[0:BP, :], [tk.k], [v_bf.k], eng="gpsimd")
                for bi, tb in enumerate(tbs):
                    pt = psn()
                    ptb = pt.t[:].bitcast(BF16)
                    for hc in range(4):
                        tr(ptb[:, hc * BP:(hc + 1) * BP], tb.b[0:BP, hc * 128:(hc + 1) * 128], identb.t[0:BP, 0:BP], [tb.k, identb.k], [pt.k])
                    dst = QT if ui == 0 else KTc
                    evac(dst.t[:, :, bi * BP:(bi + 1) * BP], ptb[:, 0:4 * BP].rearrange("p (h t) -> p h t", h=4), [pt.k], [dst.k])
            if seq == 'p' and not is_last:
                u = t0 // 512
                dma(ktscr[l].rearrange("h p t -> p h t")[:, :, t0:t0 + TT], KTc.t[:, :, 0:TT], [KTc.k], [kt_t[l][h][u] for h in range(4)], eng="gpsimd")
                for h in range(4):
                    dma(vscr[l, h, u].rearrange("p (b e) -> p b e", b=4), v_bf.t[:, :, h * 128:(h + 1) * 128], [v_bf.k], [v_t[l][h][u]], eng="gpsimd")

            S.phase = 'gla_prep'
            ws = wload(l, "U6")
            w3 = ws.t[:, 0:8 * 528].rearrange("p (kc n) -> p kc n", kc=8)
            p = psn()
            for kc in range(8):
                mm(p.t[0:16, 0:TT], w3[:, kc, 0:16], xn.t[:, kc, 0:TT], kc == 0, kc == 7, [xn.k, ws.k], [p.k])
            evac(gab.t[0:16, 0:TT], p.t[0:16, 0:TT], [p.k], [gab.k])
            cp(lxT.t[:, :, 0:3], lxh[l].t[:, :, :], [lxh[l].k], [lxT.k], eng="gpsimd")
            for c in range(4):
                p = psn()
                for kc in range(8):
                    mm(p.t[:, 0:TT], w3[:, kc, 16 + c * 128:16 + (c + 1) * 128], xn.t[:, kc, 0:TT], kc == 0, kc == 7, [xn.k, ws.k], [p.k])
                evac(lxT.t[:, c, 3:3 + TT], p.t[:, 0:TT], [p.k], [lxT.k])
            cp(lxh[l].t[:, :, :], lxT.t[:, :, TT:TT + 3], [lxT.k], [lxh[l].k], eng="gpsimd")
            w8 = wload(l, "U8")
            w83 = w8.t[:, 0:1024].rearrange("p (c n) -> p c n", c=8)
            w7 = wload(l, "U7")
            w73 = w7.t[:, 0:4096].rearrange("p (kc n) -> p kc n", kc=8)


            S.phase = 'attn'
            acc = PS[0:4]
            sring = PS[4:8]
            sr_i = 0
            nkb_cur = (TT + 127) // 128
            pend_fin = None
            for h in range(4):
                S.phase = 'attn'

                def load_unit(u):
                    slot = kv_i[0] % 3
                    kv_i[0] += 1
                    kr, vr = kring[slot], vring[slot]
                    if seq == 'p':
                        dma(kr.t[:, :], ktscr[l, h, :, u * 512:(u + 1) * 512], [kt_t[l][h][u]], [kr.k])
                        dma(vr.t[:, :, :], vscr[l, h, u].rearrange("p (b e) -> p b e", b=4), [v_t[l][h][u]], [vr.k])
                    else:
                        ks_ = kst[0]
                        dma(ks_.t[:, :, :], ck[l, seq, u * 512:(u + 1) * 512, h * 128:(h + 1) * 128].rearrange("(b p) f -> p b f", p=128), [], [ks_.k], eng="gpsimd")
                        dma(vr.t[:, :, :], cv[l, seq, u * 512:(u + 1) * 512, h * 128:(h + 1) * 128].rearrange("(b p) f -> p b f", p=128), [], [vr.k], eng="gpsimd")
                        pt = sring[sr_box[0] % 4]
                        sr_box[0] += 1
                        ptb = pt.t[:].bitcast(BF16)
                        for b_ in range(4):
                            tr(ptb[:, b_ * 128:(b_ + 1) * 128], ks_.t[:, b_, :], identb.t[:, :], [ks_.k, identb.k], [pt.k])
                        evac(kr.t[:, :], ptb[:, 0:512], [pt.k], [kr.k])
                    return kr, vr

                sr_box = [sr_i]
                loaded = {}

                def ensure(u):
                    if u < units_past and u not in loaded:
                        loaded[u] = load_unit(u)

                nblk = units_past * 4 + nkb_cur

                def gen_blocks():
                    for u in range(units_past):
                        ensure(u)
                        ensure(u + 1)
                        kr, vr = loaded.pop(u)
                        for kb in range(4):
                            yield (kr.t[:, kb * 128:(kb + 1) * 128], vr.t[:, kb, :], 128, None, [kr.k, vr.k])
                    for kb in range(nkb_cur):
                        nk = min(128, TT - kb * 128)
                        yield (KTc.t[:, h, kb * 128:kb * 128 + nk], v_bf.t[0:nk, kb, h * 128:(h + 1) * 128], nk,
                               kb if seq == 'p' else None, [KTc.k, v_bf.k])

                def stage1(blk):
                    ksrc, vsrc, nk, diag, rds = blk
                    prs = []
                    for m in range(2):
                        sp_ = sring[sr_box[0] % 4]
                        sr_box[0] += 1
                        mm(sp_.t[0:nk, 0:TT], ksrc[m * 64:(m + 1) * 64, :], QT.t[m * 64:(m + 1) * 64, h, 0:TT], True, True, rds + [QT.k], [sp_.k])
                        pr = pring[pr_i[0] % 4]
                        pr_i[0] += 1
                        act(pr.t[0:nk, 0:TT], sp_.t[0:nk, 0:TT], AF.Exp, [sp_.k], [pr.k], scale=0.125)
                        if diag is not None:
                            if diag > 0:
                                mset(pr.t[0:nk, 0:diag * 128], 0.0, [pr.k], [pr.k])
                            if nk > 64:
                                mset(pr.t[64:nk, diag * 128:diag * 128 + 64], 0.0, [pr.k], [pr.k])
                        prs.append(pr)
                    return prs

                def stage2(blk, prs, bidx):
                    ksrc, vsrc, nk, diag, rds = blk
                    first = bidx == 0
                    last = bidx == nblk - 1
                    for m in range(2):
                        pr = prs[m]
                        mm(acc[m].t[:, 0:TT], vsrc, pr.t[0:nk, 0:TT], first, last, rds + [pr.k], [acc[m].k])
                        mm(acc[2 + m].t[:, 0:TT], onesb.t[0:nk, :], pr.t[0:nk, 0:TT], first, last, [onesb.k, pr.k], [acc[2 + m].k])

                def sbank():
                    bk = sring[sr_box[0] % 4]
                    sr_box[0] += 1
                    return bk

                prev = None
                bidx = 0
                nstage1 = 0
                for blk in gen_blocks():
                    prs = stage1(blk)
                    nstage1 += 1
                    if prev is not None:
                        stage2(prev[0], prev[1], bidx)
                        bidx += 1
                    prev = (blk, prs)
                    if nstage1 == 2 and pend_fin is not None:
                        pend_fin(sbank)
                        pend_fin = None
                        S.phase = 'attn'
                stage2(prev[0], prev[1], bidx)
                sr_i = sr_box[0]
                if pend_fin is not None:
                    pend_fin(sbank)
                    pend_fin = None
                    sr_i = sr_box[0]
                S.phase = 'attn_fin'
                r0, r1, t0_, t1_ = tmp(), tmp(), tmp(), tmp()
                act(r0.f[:, 0:TT], acc[2].t[:, 0:TT], AF.Ln, [acc[2].k], [r0.k])
                act(r1.f[:, 0:TT], acc[3].t[:, 0:TT], AF.Ln, [acc[3].k], [r1.k])
                act(r0.f[:, 0:TT], r0.f[:, 0:TT], AF.Exp, [r0.k], [r0.k], scale=-1.0)
                act(r1.f[:, 0:TT], r1.f[:, 0:TT], AF.Exp, [r1.k], [r1.k], scale=-1.0)
                tt(t0_.f[:, 0:TT], acc[0].t[:, 0:TT], r0.f[:, 0:TT], ALU.mult, [acc[0].k, r0.k], [t0_.k])
                tt(t1_.f[:, 0:TT], acc[1].t[:, 0:TT], r1.f[:, 0:TT], ALU.mult, [acc[1].k, r1.k], [t1_.k])
                stt(t0_.f[:, 0:TT], t1_.f[:, 0:TT], lamc[l].t[:, 0:1], t0_.f[:, 0:TT], ALU.mult, ALU.add, [t1_.k, t0_.k, lamc[l].k], [t0_.k])
                sq = tmp()
                act(sq.b[:, 0:TT], t0_.f[:, 0:TT], AF.Square, [t0_.k], [sq.k])

                def fin_b(bank, h=h, t0_=t0_, sq=sq):
                    S.phase = 'attn_fin'
                    pb_ = bank()
                    mm(pb_.t[:, 0:TT], onesb.t[:, :], sq.b[:, 0:TT], True, True, [onesb.k, sq.k], [pb_.k])
                    rs = tmp()
                    act(rs.f[:, 0:TT], pb_.t[:, 0:TT], AF.Ln, [pb_.k], [rs.k], bias=eps_ap, scale=1.0 / 128)
                    act(rs.f[:, 0:TT], rs.f[:, 0:TT], AF.Exp, [rs.k], [rs.k], scale=-0.5)
                    stt(oattn.t[:, h, 0:TT], t0_.f[:, 0:TT], lamc[l].t[:, 1:2], rs.f[:, 0:TT], ALU.mult, ALU.mult, [t0_.k, rs.k, lamc[l].k], [oattn.k])

                pend_fin = fin_b
            sr_box = [sr_i]

            def sbank2():
                bk = sring[sr_box[0] % 4]
                sr_box[0] += 1
                return bk
            pend_fin(sbank2)
            pend_fin = None

            S.phase = 'gla_prep'
            ws = wload(l, "U4")
            w3 = ws.t[:, 0:4096].rearrange("p (kc n) -> p kc n", kc=8)
            for ck_ in range(NCK):
                p = psn()
                for kc in range(8):
                    mm(p.t[0:L, :], xn.t[:, kc, ck_ * L:(ck_ + 1) * L], w3[:, kc, :], kc == 0, kc == 7, [xn.k, ws.k], [p.k])
                evac(gv_tok.t[0:L, ck_, :], p.t[0:L, :], [p.k], [gv_tok.k])
            for h in range(4):
                p = psn()
                mm(p.t[0:64, 0:TT], w2b[l].t[0:16, h * 64:(h + 1) * 64], gab.t[0:16, 0:TT], True, True, [w2b[l].k, gab.k], [p.k])
                e1 = tmp()
                act(e1.f[0:64, 0:TT], p.t[0:64, 0:TT], AF.Exp, [p.k, pB.k], [e1.k], bias=pB.t[0:64, 46 + h:47 + h], scale=-1.0)
                act(e1.f[0:64, 0:TT], e1.f[0:64, 0:TT], AF.Ln, [e1.k], [e1.k], bias=1.0)
                scan(gcs.t[:, h, 0:TT], cmask.t[0:64, 0:TT], e1.f[0:64, 0:TT], 0.0, [cmask.k, e1.k], [gcs.k])
            ws = wload(l, "U3")
            w3 = ws.t[:, 0:4096].rearrange("p (kc n) -> p kc n", kc=8)
            for h in range(4):
                p = psn()
                for kc in range(8):
                    mm(p.t[0:64, 0:TT], w3[:, kc, h * 64:(h + 1) * 64], xn.t[:, kc, 0:TT], kc == 0, kc == 7, [xn.k, ws.k], [p.k])
                eq = tmp()
                act(eq.f[0:64, 0:TT], gcs.t[:, h, 0:TT], AF.Exp, [gcs.k], [eq.k], scale=-1.0 / 16)
                stt(qtl.t[:, h, 0:TT], p.t[0:64, 0:TT], 0.125, eq.f[0:64, 0:TT], ALU.mult, ALU.mult, [p.k, eq.k], [qtl.k])
                cp(ebl.t[:, h, 0:NCK], eq.f[0:64, 0:TT].rearrange("p (c t) -> p c t", t=L)[:, :, L - 1], [eq.k], [ebl.k])
                p = psn()
                for kc in range(8):
                    mm(p.t[0:64, 0:TT], w3[:, kc, 256 + h * 64:256 + (h + 1) * 64], xn.t[:, kc, 0:TT], kc == 0, kc == 7, [xn.k, ws.k], [p.k])
                ek = tmp()
                act(ek.f[0:64, 0:TT], gcs.t[:, h, 0:TT], AF.Exp, [gcs.k], [ek.k], scale=1.0 / 16)
                tt(ktl.t[:, h, 0:TT], p.t[0:64, 0:TT], ek.f[0:64, 0:TT], ALU.mult, [p.k, ek.k], [ktl.k])
            for ck_ in range(NCK):
                pt = psn()
                ptb = pt.t[:].bitcast(BF16)
                for h in range(4):
                    tr(ptb[0:L, h * 64:(h + 1) * 64], ktl.t[:, h, ck_ * L:(ck_ + 1) * L], identb.t[0:64, 0:64], [ktl.k, identb.k], [pt.k])
                evac(ktok.t[0:L, ck_, :], ptb[0:L, 0:256], [pt.k], [ktok.k])
            S.phase = 'gla_chunks'
            gacc = PS[0:4]
            gring = PS[4:8]
            gr_i = 0
            for ck_ in range(NCK):
                cs_ = slice(ck_ * L, (ck_ + 1) * L)
                pas, ams = [], []
                for h in range(4):
                    pa = gring[h]
                    mm(pa.t[0:L, 0:L], ktl.t[:, h, cs_], qtl.t[:, h, cs_], True, True, [ktl.k, qtl.k], [pa.k])
                    pas.append(pa)
                for h in range(4):
                    am = tmp()
                    tt(am.b[0:L, 0:L], pas[h].t[0:L, 0:L], tri.t[0:L, 0:L], ALU.mult, [pas[h].k, tri.k], [am.k])
                    ams.append(am)
                for h in range(4):
                    mm(gacc[h].t[:, cs_], gv_tok.t[0:L, ck_, h * 128:(h + 1) * 128], ams[h].b[0:L, 0:L], True, False, [gv_tok.k, ams[h].k], [gacc[h].k])
                    mm(gacc[h].t[:, cs_], gSb[l].t[:, h, :], qtl.t[:, h, cs_], False, True, [gSbk[l][h], qtl.k], [gacc[h].k])
                for h in range(4):
                    pu = gring[h]
                    mm(pu.t[0:64, 0:128], ktok.t[0:L, ck_, h * 64:(h + 1) * 64], gv_tok.t[0:L, ck_, h * 128:(h + 1) * 128], True, True, [ktok.k, gv_tok.k], [pu.k])
                for h in range(4):
                    pu = gring[h]
                    tt(gS[l].t[:, h, :], gS[l].t[:, h, :], pu.t[0:64, 0:128], ALU.add, [gSk[l][h], pu.k], [gSk[l][h]])
                    ts(gS[l].t[:, h, :], gS[l].t[:, h, :], ebl.t[:, h, ck_:ck_ + 1], None, ALU.mult, None, [gSk[l][h], ebl.k], [gSk[l][h]])
                    evac(gSb[l].t[:, h, :], gS[l].t[:, h, :], [gSk[l][h]], [gSbk[l][h]])
            gr_i = 0
            S.phase = 'gla_out'
            ws = wload(l, "U5")
            w3 = ws.t[:, 0:4096].rearrange("p (kc n) -> p kc n", kc=8)
            for h in range(4):
                gf = tmp()
                act(gf.f[:, 0:TT], gacc[h].t[:, 0:TT], AF.Copy, [gacc[h].k], [gf.k])
                sq = tmp()
                act(sq.b[:, 0:TT], gf.f[:, 0:TT], AF.Square, [gf.k], [sq.k])
                p = gring[gr_i % 4]
                gr_i += 1
                mm(p.t[:, 0:TT], onesb.t[:, :], sq.b[:, 0:TT], True, True, [onesb.k, sq.k], [p.k])
                rs = tmp()
                act(rs.f[:, 0:TT], p.t[:, 0:TT], AF.Ln, [p.k], [rs.k], bias=eps_ap, scale=1.0 / 128)
                act(rs.f[:, 0:TT], rs.f[:, 0:TT], AF.Exp, [rs.k], [rs.k], scale=-0.5)
                stt(gf.f[:, 0:TT], gf.f[:, 0:TT], pB.t[:, 33:34], rs.f[:, 0:TT], ALU.mult, ALU.mult, [gf.k, rs.k, pB.k], [gf.k])
                p = gring[gr_i % 4]
                gr_i += 1
                for kc in range(8):
                    mm(p.t[:, 0:TT], w3[:, kc, h * 128:(h + 1) * 128], xn.t[:, kc, 0:TT], kc == 0, kc == 7, [xn.k, ws.k], [p.k])
                sl = tmp()
                act(sl.f[:, 0:TT], p.t[:, 0:TT], AF.Silu, [p.k], [sl.k])
                tt(gon.t[:, h, 0:TT], gf.f[:, 0:TT], sl.f[:, 0:TT], ALU.mult, [gf.k, sl.k], [gon.k])

            S.phase = 'lru'
            for half in range(2):
                cs2 = [2 * half, 2 * half + 1]
                T1 = {c: tmp() for c in cs2}
                T2 = {c: tmp() for c in cs2}
                T3 = {c: tmp() for c in cs2}
                T4 = {c: tmp() for c in cs2}
                rk = [lxT.k, pB.k]
                for c in cs2:
                    xc = T1[c]
                    ts(xc.f[:, 0:TT], lxT.t[:, c, 0:TT], pB.t[:, c:c + 1], pB.t[:, 16 + c:17 + c], ALU.mult, ALU.add, rk, [xc.k])
                for j in range(1, 4):
                    for c in cs2:
                        xc = T1[c]
                        stt(xc.f[:, 0:TT], lxT.t[:, c, j:j + TT], pB.t[:, 4 * j + c:4 * j + c + 1], xc.f[:, 0:TT], ALU.mult, ALU.add, rk + [xc.k], [xc.k])
                for c in cs2:
                    cp(T2[c].b[:, 0:TT], T1[c].f[:, 0:TT], [T1[c].k], [T2[c].k], eng="gpsimd")
                prs_, pis_ = {}, {}
                for c in cs2:
                    prs_[c] = psn()
                    mm(prs_[c].t[:, 0:TT], w83[:, c, :], T2[c].b[:, 0:TT], True, True, [w8.k, T2[c].k], [prs_[c].k])
                    pis_[c] = psn()
                    mm(pis_[c].t[:, 0:TT], w83[:, 4 + c, :], T2[c].b[:, 0:TT], True, True, [w8.k, T2[c].k], [pis_[c].k])
                for c in cs2:
                    act(T3[c].f[:, 0:TT], prs_[c].t[:, 0:TT], AF.Sigmoid, [prs_[c].k, pB.k], [T3[c].k], bias=pB.t[:, 20 + c:21 + c])
                    act(T4[c].f[:, 0:TT], pis_[c].t[:, 0:TT], AF.Sigmoid, [pis_[c].k, pB.k], [T4[c].k], bias=pB.t[:, 24 + c:25 + c])
                for c in cs2:
                    act(T3[c].f[:, 0:TT], T3[c].f[:, 0:TT], AF.Exp, [T3[c].k, lamc[l].k], [T3[c].k], scale=lamc[l].t[:, 2 + c:3 + c])
                for c in cs2:
                    tt(T2[c].f[:, 0:TT], T3[c].f[:, 0:TT], T3[c].f[:, 0:TT], ALU.mult, [T3[c].k, T2[c].k], [T2[c].k])
                    ts(T2[c].f[:, 0:TT], T2[c].f[:, 0:TT], -1.0, 1.0, ALU.mult, ALU.add, [T2[c].k], [T2[c].k])
                    ts(T2[c].f[:, 0:TT], T2[c].f[:, 0:TT], 1e-18, None, ALU.max, None, [T2[c].k], [T2[c].k])
                    tt(T4[c].f[:, 0:TT], T4[c].f[:, 0:TT], T1[c].f[:, 0:TT], ALU.mult, [T4[c].k, T1[c].k], [T4[c].k])
                for c in cs2:
                    act(T2[c].f[:, 0:TT], T2[c].f[:, 0:TT], AF.Ln, [T2[c].k], [T2[c].k])
                for c in cs2:
                    act(T2[c].f[:, 0:TT], T2[c].f[:, 0:TT], AF.Exp, [T2[c].k], [T2[c].k], scale=0.5)
                for c in cs2:
                    tt(T4[c].f[:, 0:TT], T4[c].f[:, 0:TT], T2[c].f[:, 0:TT], ALU.mult, [T4[c].k, T2[c].k], [T4[c].k])
                for c in cs2:
                    scan(T1[c].f[:, 0:TT], T3[c].f[:, 0:TT], T4[c].f[:, 0:TT], hst[l].t[:, c:c + 1], [T3[c].k, T4[c].k, hst[l].k, T1[c].k], [T1[c].k])
                    cp(hst[l].t[:, c:c + 1], T1[c].f[:, TT - 1:TT], [T1[c].k], [hst[l].k])
                pgs = {}
                for c in cs2:
                    pgs[c] = psn()
                    for kc in range(8):
                        mm(pgs[c].t[:, 0:TT], w73[:, kc, c * 128:(c + 1) * 128], xn.t[:, kc, 0:TT], kc == 0, kc == 7, [xn.k, w7.k], [pgs[c].k])
                for c in cs2:
                    act(T2[c].f[:, 0:TT], pgs[c].t[:, 0:TT], AF.Gelu_apprx_tanh, [pgs[c].k, T2[c].k], [T2[c].k])
                for c in cs2:
                    tt(hl.t[:, c, 0:TT], T1[c].f[:, 0:TT], T2[c].f[:, 0:TT], ALU.mult, [T1[c].k, T2[c].k], [hl.k])

            S.phase = 'merge'
            for m in range(8):
                ws = wload(l, f"M{m}")
                wsb = wload(l, f"B{m}")
                wm4 = ws.t[:, 0:3072].rearrange("p (kc j f) -> p kc j f", kc=8, j=3)
                wb4 = wsb.t[:, 0:1536].rearrange("p (kc j f) -> p kc j f", kc=4, j=3)
                srcs = [oattn, gon, hl]
                acc_t = None
                for j in range(3):
                    pg = psn()
                    for kc in range(8):
                        mm(pg.t[:, 0:TT], wm4[:, kc, j, :], xn.t[:, kc, 0:TT], kc == 0, kc == 7, [xn.k, ws.k], [pg.k])
                    g = tmp()
                    act(g.f[:, 0:TT], pg.t[:, 0:TT], AF.Sigmoid, [pg.k, pA.k], [g.k], bias=pA.t[:, 88 + j * 8 + m:89 + j * 8 + m])
                    py = psn()
                    for kc in range(4):
                        mm(py.t[:, 0:TT], wb4[:, kc, j, :], srcs[j].t[:, kc, 0:TT], kc == 0, kc == 3, [srcs[j].k, wsb.k], [py.k])
                    if j == 0:
                        acc_t = tmp()
                        tt(acc_t.f[:, 0:TT], g.f[:, 0:TT], py.t[:, 0:TT], ALU.mult, [g.k, py.k], [acc_t.k])
                    else:
                        tt(g.f[:, 0:TT], g.f[:, 0:TT], py.t[:, 0:TT], ALU.mult, [g.k, py.k], [g.k])
                        if j == 1:
                            tt(acc_t.f[:, 0:TT], acc_t.f[:, 0:TT], g.f[:, 0:TT], ALU.add, [g.k, acc_t.k], [acc_t.k])
                        else:
                            tt(mergedb.t[:, m, 0:TT], acc_t.f[:, 0:TT], g.f[:, 0:TT], ALU.add, [g.k, acc_t.k], [mergedb.k])
            S.phase = 'wout'
            for i in range(2):
                ws = wload(l, f"O{i}")
                w3 = ws.t[:, 0:4096].rearrange("p (kc n) -> p kc n", kc=8)
                for mi in range(4):
                    m = i * 4 + mi
                    p = psn()
                    for kc in range(8):
                        mm(p.t[:, 0:TT], w3[:, kc, mi * 128:(mi + 1) * 128], mergedb.t[:, kc, 0:TT], kc == 0, kc == 7, [mergedb.k, ws.k], [p.k])
                    tt(xT.t[:, m, 0:TT], xT.t[:, m, 0:TT], p.t[:, 0:TT], ALU.add, [xT.k, p.k], [xT.k])

            S.phase = 'ffn_gu'
            rmsnorm_to(xn, lambda c: pA.t[:, 120 + c:121 + c])
            for i in range(11):
                ws = wload(l, f"F{i}")
                w4 = ws.t[:, 0:4096].rearrange("p (kc g n) -> p kc g n", kc=8, g=2)
                for s_ in range(2):
                    j = 2 * i + s_
                    pg = psn()
                    for kc in range(8):
                        mm(pg.t[:, 0:TT], w4[:, kc, 0, s_ * 128:(s_ + 1) * 128], xn.t[:, kc, 0:TT], kc == 0, kc == 7, [xn.k, ws.k], [pg.k])
                    pu = psn()
                    for kc in range(8):
                        mm(pu.t[:, 0:TT], w4[:, kc, 1, s_ * 128:(s_ + 1) * 128], xn.t[:, kc, 0:TT], kc == 0, kc == 7, [xn.k, ws.k], [pu.k])
                    gu = tmp()
                    gu2 = tmp()
                    act(gu.f[:, 0:TT], pg.t[:, 0:TT], AF.Copy, [pg.k], [gu.k])
                    gc = gu2
                    act(gc.f[:, 0:TT], pg.t[:, 0:TT], AF.Identity, [pg.k, pA.k], [gc.k], bias=pA.t[:, 66 + j:67 + j], scale=pA.t[:, 44 + j:45 + j])
                    if TT > 1:
                        stt(gc.f[:, 1:TT], gu.f[:, 0:TT - 1], pA.t[:, 22 + j:23 + j], gc.f[:, 1:TT], ALU.mult, ALU.add, [gu.k, gc.k, pA.k], [gc.k])
                    stt(gc.f[:, 0:1], guh[l].t[:, j, 1:2], pA.t[:, 22 + j:23 + j], gc.f[:, 0:1], ALU.mult, ALU.add, [guh[l].k, gc.k, pA.k], [gc.k])
                    stt(gc.f[:, 2:TT], gu.f[:, 0:TT - 2], pA.t[:, j:j + 1], gc.f[:, 2:TT], ALU.mult, ALU.add, [gu.k, gc.k, pA.k], [gc.k])
                    stt(gc.f[:, 0:2], guh[l].t[:, j, 0:2], pA.t[:, j:j + 1], gc.f[:, 0:2], ALU.mult, ALU.add, [guh[l].k, gc.k, pA.k], [gc.k])
                    cp(guh[l].t[:, j, :], gu.f[:, TT - 2:TT], [gu.k, gc.k], [guh[l].k], eng="gpsimd")
                    act(gc.f[:, 0:TT], gc.f[:, 0:TT], AF.Gelu_apprx_tanh, [gc.k], [gc.k])
                    tt(fT_t[:, j, 0:TT], gc.f[:, 0:TT], pu.t[:, 0:TT], ALU.mult, [gc.k, pu.k], [fT[j].k])
            S.phase = 'ffn_down'
            for m in range(8):
                ws = wload(l, f"D{m}")
                w3 = ws.t[:, 0:NFC * 128].rearrange("p (kc n) -> p kc n", kc=NFC)
                p = psn()
                for kc in range(NFC):
                    mm(p.t[:, 0:TT], w3[:, kc, :], fT_t[:, kc, 0:TT], kc == 0, kc == NFC - 1, [fT[kc].k, ws.k], [p.k])
                tt(xT.t[:, m, 0:TT], xT.t[:, m, 0:TT], p.t[:, 0:TT], ALU.add, [xT.k, p.k], [xT.k])

            S.phase = 'stateout'
            if is_last:
                ot = otile()
                dma(outs["gla"][l].rearrange("h k v -> k h v"), gS[l].t[:, :, :], gSk[l], [ot], eng="gpsimd")
                for j in range(3):
                    ot = otile()
                    dma(outs["lc"][l][j].rearrange("(c p) -> p c", p=128), lxh[l].t[:, :, j], [lxh[l].k], [ot], allow_slow_non_contiguous=True)
                ot = otile()
                dma(outs["lh"][l].rearrange("(c p) -> p c", p=128), hst[l].t[:, :], [hst[l].k], [ot], allow_slow_non_contiguous=True)
                for j in range(2):
                    ot = otile()
                    dma(outs["fc"][l][j].rearrange("(c p) -> p c", p=128), guh[l].t[:, :, j], [guh[l].k], [ot], allow_slow_non_contiguous=True)

        S.phase = 'yout'
        p = psn()
        for c in range(8):
            sq = tmp()
            act(sq.b[:, 0:TT], xT.t[:, c, 0:TT], AF.Square, [xT.k], [sq.k])
            mm(p.t[:, 0:TT], onesb.t[:, :], sq.b[:, 0:TT], c == 0, c == 7, [onesb.k, sq.k], [p.k])
        rsf = lxT.t[:, 0, 0:TT]
        act(rsf, p.t[:, 0:TT], AF.Ln, [p.k], [lxT.k], bias=eps_ap, scale=1.0 / D)
        act(rsf, rsf, AF.Exp, [lxT.k], [lxT.k], scale=-0.5)
        base2, yk = tmp8()

        def ytok(bi):
            return pool_t[0:BP, base2 + 2 * bi: base2 + 2 * bi + 2, :].rearrange("p a n -> p (a n)")

        for c in range(8):
            yc = tmp()
            stt(yc.f[:, 0:TT], xT.t[:, c, 0:TT], parB[0].t[:, 38 + c:39 + c], rsf, ALU.mult, ALU.mult, [xT.k, lxT.k, parB[0].k], [yc.k])
            pt = psn()
            for bi in range(NB):
                tr(pt.t[0:BP, bi * 128:(bi + 1) * 128], yc.f[:, bi * BP:(bi + 1) * BP], identf.t[:, :], [yc.k, identf.k], [pt.k])
            for bi in range(NB):
                evac(ytok(bi)[:, c * 128:(c + 1) * 128], pt.t[0:BP, bi * 128:(bi + 1) * 128], [pt.k], [yk[2 * bi], yk[2 * bi + 1]])
        for bi in range(NB):
            ot = otile()
            dma(ysrc[bi * BP:(bi + 1) * BP, :], ytok(bi), [yk[2 * bi], yk[2 * bi + 1]], [ot], eng="gpsimd")

    epsb = mk("epsb", [128, 1], F32)
    mset(epsb.t[:], EPS, [], [epsb.k])
    eps_ap = epsb.t[:, 0:1]
    _orig_act = act

    def act(out, in_, func, reads, writes, bias=None, scale=None):
        if bias is eps_ap:
            reads = list(reads) + [epsb.k]
        _orig_act(out, in_, func, reads, writes, bias=bias, scale=scale)

    if cfg["samples"]:
        for s in range(2):
            outs = {
                "k": [ks_o[l, s] for l in range(DEPTH)], "v": [vs_o[l, s] for l in range(DEPTH)],
                "gla": [glas[l, s] for l in range(DEPTH)], "lc": [lcs[l, s] for l in range(DEPTH)],
                "lh": [lhs_o[l, s] for l in range(DEPTH)], "fc": [fcs[l, s] for l in range(DEPTH)],
            }
            run_tile(s, DSEQ, DSEQ, 1, 32, 1, xs[s], ys[s], ropes, 0, PAST // 512, True, outs)
    npt = cfg["n_ptiles"]
    for ti in range(npt):
        t0 = ti * TTP
        outs = {
            "k": [kp[l, t0:t0 + TTP] for l in range(DEPTH)], "v": [vp[l, t0:t0 + TTP] for l in range(DEPTH)],
            "gla": [glap[l] for l in range(DEPTH)], "lc": [lcp[l] for l in range(DEPTH)],
            "lh": [lhp[l] for l in range(DEPTH)], "fc": [fcp[l] for l in range(DEPTH)],
        }
        run_tile('p', TTP, 128, 4, 64, 8, xp[t0:t0 + TTP], yp[t0:t0 + TTP], ropep[t0:t0 + TTP], t0, ti, ti == npt - 1, outs)

    S.op("sync", lambda e: e.nop(), reads=out_tiles)
    S.emit()
    return nc, S


def _rope_table(pos):
    half = 8
    inv = (np.float32(500000.0) ** (-np.arange(half, dtype=np.float32) / np.float32(half))).astype(np.float32)
    ang = (pos.astype(np.float32)[:, None] * inv[None, :]).astype(np.float32)
    cos = np.cos(ang).astype(np.float32)
    sin = np.sin(ang).astype(np.float32)
    return np.concatenate([np.tile(cos, (1, 8)), np.tile(sin, (1, 8))], axis=1).astype(np.float32)


_WNAMES = ["norm_mix", "w_in", "lambda_qk", "attn_subln", "w_gla_gate2", "b_gla_gate", "gla_norm", "lru_conv_w",
           "lru_conv_b", "lru_wa", "lru_ba", "lru_wx", "lru_bx", "lru_lambda", "w_branch_attn", "w_branch_gla",
           "w_branch_lru", "w_merge", "b_merge", "w_out", "norm_ffn", "w_ffn_gate", "ffn_conv_w", "ffn_conv_b",
           "w_ffn_up", "w_ffn_down", "norm_final"]


def kernel(**inp):
    cfg = dict(CFG)
    nc, S = build_program(cfg)
    f = lambda a: np.ascontiguousarray(np.asarray(a, dtype=np.float32))
    NLc = cfg['depth']
    wts = {n: (f(inp[n]) if n == 'norm_final' else f(np.asarray(inp[n])[:NLc])) for n in _WNAMES}
    consts = {
        "ropep": _rope_table(np.arange(SEQ)),
        "ropes": _rope_table(PAST + np.arange(DSEQ)),
        "cmask": (np.arange(TTP) % 64 != 0).astype(np.float32)[None, :],
        "tri": np.triu(np.ones((64, 64), np.float32)),
        "ident": np.eye(128, dtype=np.float32),
    }
    x_prompt = f(inp["x_prompt"])
    x_sample = f(inp["x_sample"])
    ckf = np.asarray(inp["cache_attn_k"], dtype=np.float32).reshape(DEPTH, 16, PAST, 512)
    cvf = np.asarray(inp["cache_attn_v"], dtype=np.float32).reshape(DEPTH, 16, PAST, 512)
    sgf = f(inp["state_gla"])
    slcf = f(inp["state_lru_conv"])
    slhf = f(inp["state_lru_h"])
    sfcf = f(inp["state_ffn_conv"])
    in_maps = []
    for c in range(8):
        s0 = 2 * c
        m = {
            "xp": x_prompt[c % 4],
            "xs": x_sample[s0:s0 + 2],
            "ck": np.ascontiguousarray(ckf[:, s0:s0 + 2]),
            "cv": np.ascontiguousarray(cvf[:, s0:s0 + 2]),
            "sg": np.ascontiguousarray(sgf[:, s0:s0 + 2]),
            "slc": np.ascontiguousarray(slcf[:, s0:s0 + 2]),
            "slh": np.ascontiguousarray(slhf[:, s0:s0 + 2]),
            "sfc": np.ascontiguousarray(sfcf[:, s0:s0 + 2]),
        }
        m.update(wts)
        m.update(consts)
        in_maps.append(m)
    res = run_bass_kernel_spmd(nc, in_maps, core_ids=list(range(8))).results
    B = 4
    y_prompt = np.stack([res[b]["yp"] for b in range(B)])
    k_p = np.stack([res[b]["kp"] for b in range(B)], axis=1).reshape(DEPTH, B, SEQ, 4, 2, 64)
    v_p = np.stack([res[b]["vp"] for b in range(B)], axis=1).reshape(DEPTH, B, SEQ, 4, 128)
    gla_p = np.stack([res[b]["glap"] for b in range(B)], axis=1)
    lc_p = np.stack([res[b]["lcp"] for b in range(B)], axis=1)
    lh_p = np.stack([res[b]["lhp"] for b in range(B)], axis=1)
    fc_p = np.stack([res[b]["fcp"] for b in range(B)], axis=1)
    y_sample = np.concatenate([res[c]["ys"] for c in range(8)], axis=0)
    k_s = np.concatenate([res[c]["ks"] for c in range(8)], axis=1).reshape(DEPTH, 16, DSEQ, 4, 2, 64)
    v_s = np.concatenate([res[c]["vs"] for c in range(8)], axis=1).reshape(DEPTH, 16, DSEQ, 4, 128)
    gla_s = np.concatenate([res[c]["glas"] for c in range(8)], axis=1)
    lc_s = np.concatenate([res[c]["lcs"] for c in range(8)], axis=1)
    lh_s = np.concatenate([res[c]["lhs"] for c in range(8)], axis=1)
    fc_s = np.concatenate([res[c]["fcs"] for c in range(8)], axis=1)
    outs = (y_prompt, y_sample, k_p, v_p, gla_p, lc_p, lh_p, fc_p, k_s, v_s, gla_s, lc_s, lh_s, fc_s)
    return tuple(np.ascontiguousarray(o, dtype=np.float32) for o in outs)
```

```python
import math
from contextlib import ExitStack

import numpy as np
import concourse.bass as bass
import concourse.mybir as mybir
from concourse.bass_utils import run_bass_kernel_spmd

F32 = mybir.dt.float32
BF16 = mybir.dt.bfloat16
AF = mybir.ActivationFunctionType
ALU = mybir.AluOpType

ENGS = ["tensor", "vector", "scalar", "gpsimd", "sync"]

CFG = {"n_ptiles": 16, "depth": 4, "samples": True}

D = 1024
DEPTH = 4
SEQ = 8192
TTP = 512
PAST = 2048
DSEQ = 32
EPS = 1e-6
AQ, AK, AV, GQ, GK, GV, GR, GA, LX, LG, PW = 0, 512, 1024, 1536, 1792, 2048, 2560, 3072, 3088, 3600, 4112
DFF = 2816
NFC = 22
WSLOT = 4224


class Tile:
    __slots__ = ("name", "last_w", "readers", "psum")

    def __init__(self, name):
        self.name = name
        self.last_w = None
        self.readers = []
        self.psum = False


class Op:
    __slots__ = ("eng", "fn", "deps", "dma", "signal", "sem", "val", "idx", "phase")


class Sched:
    RING = 12

    def __init__(self, nc):
        self.nc = nc
        self.ops = []
        self.stack = ExitStack()
        self.ntile = 0
        self.phase = 'init'

    def sbuf(self, name, shape, dtype):
        return self.stack.enter_context(self.nc.sbuf_tensor("sb_" + name, list(shape), dtype))

    def psum(self, name, shape, dtype):
        return self.stack.enter_context(self.nc.psum_tensor("pp_" + name, list(shape), dtype))

    def tile(self, name=None):
        self.ntile += 1
        return Tile(name or f"t{self.ntile}")

    def op(self, eng, fn, reads=(), writes=(), dma=False):
        o = Op()
        o.eng = eng
        o.fn = fn
        o.dma = dma
        o.signal = False
        o.sem = None
        o.val = 0
        o.idx = len(self.ops)
        o.phase = self.phase
        deps = set()
        ops = self.ops
        for t in reads:
            if t.last_w is not None:
                deps.add(t.last_w)
            if t.psum:
                for r in t.readers:
                    if ops[r].eng != eng:
                        deps.add(r)
        for t in writes:
            if t.last_w is not None:
                deps.add(t.last_w)
            for r in t.readers:
                ro = ops[r]
                if (not dma) and (not ro.dma) and ro.eng == eng:
                    continue
                deps.add(r)
        if eng == "tensor" and not dma:
            deps = {d for d in deps if ops[d].dma or ops[d].eng != "tensor"}
        o.deps = deps
        for t in reads:
            if not dma:
                t.readers = [r for r in t.readers if ops[r].dma or ops[r].eng != eng]
            t.readers.append(o.idx)
        for t in writes:
            t.last_w = o.idx
            t.readers = []
        ops.append(o)
        return o

    def emit(self):
        nc = self.nc
        ops = self.ops
        stack = self.stack
        dma_count = {e: 0 for e in ENGS}
        dma_hist = {e: [] for e in ENGS}
        for o in ops:
            if o.dma:
                j = dma_count[o.eng]
                if j >= self.RING:
                    o.deps.add(dma_hist[o.eng][j - self.RING])
                dma_hist[o.eng].append(o.idx)
                dma_count[o.eng] += 1
        for o in ops:
            for d in o.deps:
                ops[d].signal = True
        eng_sem = {e: stack.enter_context(nc.semaphore(f"s_{e}")) for e in ENGS}
        ring_sems = {e: [stack.enter_context(nc.semaphore(f"d_{e}_{i}")) for i in range(self.RING)]
                     for e in ENGS if dma_count[e] > 0}
        cnt = {e: 0 for e in ENGS}
        dj = {e: 0 for e in ENGS}
        for o in ops:
            if o.dma:
                j = dj[o.eng]
                o.sem = ring_sems[o.eng][j % self.RING]
                o.val = 16 * (j // self.RING + 1)
                dj[o.eng] += 1
            elif o.signal:
                cnt[o.eng] += 1
                o.sem = eng_sem[o.eng]
                o.val = cnt[o.eng]
        per_eng = {e: [] for e in ENGS}
        for o in ops:
            per_eng[o.eng].append(o)
        self.stats = {e: len(per_eng[e]) for e in ENGS}
        block = stack.enter_context(nc.Block())

        def make(ename):
            def body(eng):
                waited = {}
                for o in per_eng[ename]:
                    need = {}
                    for d in o.deps:
                        do = ops[d]
                        k = id(do.sem)
                        if k not in need or need[k][1] < do.val:
                            need[k] = (do.sem, do.val)
                    for k, (sem, val) in need.items():
                        if waited.get(k, 0) >= val:
                            continue
                        eng.wait_ge(sem, val)
                        waited[k] = val
                    ins = o.fn(eng)
                    if o.dma:
                        ins.then_inc(o.sem, 16)
                    elif o.signal:
                        ins.then_inc(o.sem, 1)
            return body

        for e in ENGS:
            if per_eng[e]:
                getattr(block, e)(make(e))
        stack.close()


class Buf:
    __slots__ = ("t", "k")

    def __init__(self, t, k):
        self.t = t
        self.k = k


def build_program(cfg):
    nc = bass.Bass("TRN2", target_bir_lowering=False)
    S = Sched(nc)
    NL = cfg["depth"]

    def din(name, shape, dt=F32):
        return nc.dram_tensor(name, list(shape), dt, kind="ExternalInput").ap()

    def dout(name, shape, dt=F32):
        return nc.dram_tensor(name, list(shape), dt, kind="ExternalOutput").ap()

    def dscr(name, shape, dt=BF16):
        return nc.dram_tensor(name, list(shape), dt).ap()

    xp = din("xp", [SEQ, D])
    xs = din("xs", [2, DSEQ, D])
    ck = din("ck", [DEPTH, 2, PAST, 512])
    cv = din("cv", [DEPTH, 2, PAST, 512])
    sg = din("sg", [DEPTH, 2, 4, 64, 128])
    slc = din("slc", [DEPTH, 2, 3, 512])
    slh = din("slh", [DEPTH, 2, 512])
    sfc = din("sfc", [DEPTH, 2, 2, DFF])
    W = {}
    for name, shape in [
        ("norm_mix", [NL, D]), ("w_in", [NL, D, PW]), ("lambda_qk", [NL, 4, 64]),
        ("attn_subln", [NL, 128]), ("w_gla_gate2", [NL, 16, 256]), ("b_gla_gate", [NL, 256]),
        ("gla_norm", [NL, 128]), ("lru_conv_w", [NL, 4, 512]), ("lru_conv_b", [NL, 512]),
        ("lru_wa", [NL, 8, 64, 64]), ("lru_ba", [NL, 512]), ("lru_wx", [NL, 8, 64, 64]),
        ("lru_bx", [NL, 512]), ("lru_lambda", [NL, 512]), ("w_branch_attn", [NL, 512, D]),
        ("w_branch_gla", [NL, 512, D]), ("w_branch_lru", [NL, 512, D]), ("w_merge", [NL, D, 3 * D]),
        ("b_merge", [NL, 3 * D]), ("w_out", [NL, D, D]), ("norm_ffn", [NL, D]),
        ("w_ffn_gate", [NL, D, DFF]), ("ffn_conv_w", [NL, 3, DFF]), ("ffn_conv_b", [NL, DFF]),
        ("w_ffn_up", [NL, D, DFF]), ("w_ffn_down", [NL, DFF, D]), ("norm_final", [D]),
    ]:
        W[name] = din(name, shape)
    ropep = din("ropep", [SEQ, 128])
    ropes = din("ropes", [DSEQ, 128])
    cmask_d = din("cmask", [1, TTP])
    tri_d = din("tri", [64, 64])
    ident_d = din("ident", [128, 128])

    yp = dout("yp", [SEQ, D])
    ys = dout("ys", [2, DSEQ, D])
    kp = dout("kp", [DEPTH, SEQ, 512])
    vp = dout("vp", [DEPTH, SEQ, 512])
    glap = dout("glap", [DEPTH, 4, 64, 128])
    lcp = dout("lcp", [DEPTH, 3, 512])
    lhp = dout("lhp", [DEPTH, 512])
    fcp = dout("fcp", [DEPTH, 2, DFF])
    ks_o = dout("ks", [DEPTH, 2, DSEQ, 512])
    vs_o = dout("vs", [DEPTH, 2, DSEQ, 512])
    glas = dout("glas", [DEPTH, 2, 4, 64, 128])
    lcs = dout("lcs", [DEPTH, 2, 3, 512])
    lhs_o = dout("lhs", [DEPTH, 2, 512])
    fcs = dout("fcs", [DEPTH, 2, 2, DFF])
    out_tiles = []

    def otile():
        t = S.tile()
        out_tiles.append(t)
        return t

    ktscr = dscr("ktscr", [DEPTH, 4, 128, SEQ])
    vscr = dscr("vscr", [DEPTH, 4, SEQ // 512, 128, 512])
    kt_t = [[[S.tile() for u in range(16)] for h in range(4)] for l in range(DEPTH)]
    v_t = [[[S.tile() for u in range(16)] for h in range(4)] for l in range(DEPTH)]

    units = {}

    def mk(name, shape, dt):
        return Buf(S.sbuf(name, shape, dt), S.tile(name))

    xT = mk("xT", [128, 8, TTP], F32)
    NP = 16
    pool_t = S.sbuf("pool", [128, NP, TTP], F32)
    pool_k = [S.tile(f"pool{i}") for i in range(NP)]
    pool_i = [0]

    class Tmp:
        __slots__ = ("f", "b", "k")

    def tmp():
        i = pool_i[0] % NP
        pool_i[0] += 1
        r = Tmp()
        r.f = pool_t[:, i, :]
        r.b = pool_t[:, i, :].bitcast(BF16)
        r.k = pool_k[i]
        return r

    def tmp8():
        if (pool_i[0] % NP) + 8 > NP:
            pool_i[0] += NP - (pool_i[0] % NP)
        base = pool_i[0] % NP
        ks_ = [tmp().k for _ in range(8)]
        return base, ks_

    xn = mk("xn", [128, 8, TTP], BF16)
    NWR = 5
    wring = [mk(f"wring{i}", [128, WSLOT], BF16) for i in range(NWR)]
    wr_i = [0]
    QT = mk("QT", [128, 4, TTP], BF16)
    KTc = mk("KTc", [128, 4, TTP], BF16)
    qtl = mk("qtl", [64, 4, TTP], BF16)
    ktl = mk("ktl", [64, 4, TTP], BF16)
    v_bf = mk("v_bf", [128, 4, 512], BF16)
    gv_tok = mk("gv_tok", [64, 8, 512], BF16)
    ktok = mk("ktok", [64, 8, 256], BF16)
    fT = [Buf(None, S.tile(f"fT{j}")) for j in range(NFC)]
    fT_t = S.sbuf("fT", [128, NFC, TTP], BF16)
    mergedb = Buf(fT_t[:, 0:8, :], S.tile("mergedb"))
    oattn = Buf(fT_t[:, 8:12, :], S.tile("oattn"))
    gon = Buf(fT_t[:, 12:16, :], S.tile("gon"))
    hl = Buf(fT_t[:, 16:20, :], S.tile("hl"))
    for j in range(NFC):
        if j < 8:
            fT[j].k = mergedb.k
        elif j < 12:
            fT[j].k = oattn.k
        elif j < 16:
            fT[j].k = gon.k
        elif j < 20:
            fT[j].k = hl.k
    pring = [mk(f"pring{i}", [128, TTP], BF16) for i in range(4)]
    pr_i = [0]
    kring = [mk(f"kring{i}", [128, 512], BF16) for i in range(3)]
    vring = [mk(f"vring{i}", [128, 4, 128], BF16) for i in range(3)]
    kv_i = [0]
    kst = [mk(f"kst{i}", [128, 4, 128], BF16) for i in range(1)]
    kst_i = [0]
    gS = [mk(f"gS{l}", [64, 4, 128], F32) for l in range(DEPTH)]
    gSb = [mk(f"gSb{l}", [64, 4, 128], BF16) for l in range(DEPTH)]
    gSk = [[S.tile(f"gSk{l}_{h}") for h in range(4)] for l in range(DEPTH)]
    gSbk = [[S.tile(f"gSbk{l}_{h}") for h in range(4)] for l in range(DEPTH)]
    hst = [mk(f"hst{l}", [128, 4], F32) for l in range(DEPTH)]
    lxh = [mk(f"lxh{l}", [128, 4, 3], F32) for l in range(DEPTH)]
    guh = [mk(f"guh{l}", [128, NFC, 2], F32) for l in range(DEPTH)]
    parA = [mk(f"parA{l}", [128, 128], F32) for l in range(DEPTH)]
    parB = [mk(f"parB{l}", [128, 64], F32) for l in range(DEPTH)]
    lamc = [mk(f"lamc{l}", [128, 8], F32) for l in range(DEPTH)]
    w2b = [mk(f"w2b{l}", [16, 256], BF16) for l in range(DEPTH)]
    identf = mk("identf", [128, 128], F32)
    identb = mk("identb", [128, 128], BF16)
    onesb = mk("onesb", [128, 128], BF16)
    tri = mk("tri", [64, 64], F32)
    cmask = mk("cmask_sb", [128, TTP], F32)
    rope_sb = mk("rope_sb", [128, 4, 128], F32)
    lxT = mk("lxT", [128, 4, 3 + TTP], F32)
    gcs = mk("gcs", [64, 4, TTP], F32)
    gab = mk("gab", [16, TTP], BF16)
    ebl = mk("ebl", [64, 4, 8], F32)

    PS = [Buf(S.psum(f"ps{i}", [128, 512], F32), S.tile(f"ps{i}")) for i in range(8)]
    for b_ in PS:
        b_.k.psum = True
    ps_i = [0]

    def psn():
        i = ps_i[0] % 8
        ps_i[0] += 1
        return PS[i]

    def dma(out, in_, reads, writes, eng="sync", **kw):
        S.op(eng, lambda e: e.dma_start(out=out, in_=in_, **kw), reads=reads, writes=writes, dma=True)

    def mm(out, lhsT, rhs, start, stop, reads, writes):
        S.op("tensor", lambda e: e.matmul(out, lhsT=lhsT, rhs=rhs, start=start, stop=stop), reads=reads, writes=writes)

    def tr(out, in_, ident, reads, writes):
        S.op("tensor", lambda e: e.transpose(out=out, in_=in_, identity=ident), reads=reads, writes=writes)

    def act(out, in_, func, reads, writes, bias=None, scale=None):
        kw = {}
        if bias is not None:
            kw["bias"] = bias
        if scale is not None:
            kw["scale"] = scale
        S.op("scalar", lambda e: e.activation(out=out, in_=in_, func=func, **kw), reads=reads, writes=writes)

    def tt(out, in0, in1, op, reads, writes, eng="vector"):
        S.op(eng, lambda e: e.tensor_tensor(out=out, in0=in0, in1=in1, op=op), reads=reads, writes=writes)

    def ts(out, in0, s1, s2, op0, op1, reads, writes, eng="vector"):
        if op1 is None:
            S.op(eng, lambda e: e.tensor_scalar(out=out, in0=in0, scalar1=s1, scalar2=None, op0=op0), reads=reads, writes=writes)
        else:
            S.op(eng, lambda e: e.tensor_scalar(out=out, in0=in0, scalar1=s1, scalar2=s2, op0=op0, op1=op1), reads=reads, writes=writes)

    def stt(out, in0, scalar, in1, op0, op1, reads, writes):
        S.op("vector", lambda e: e.scalar_tensor_tensor(out=out, in0=in0, scalar=scalar, in1=in1, op0=op0, op1=op1), reads=reads, writes=writes)

    def cp(out, in_, reads, writes, eng="vector"):
        S.op(eng, lambda e: e.tensor_copy(out=out, in_=in_), reads=reads, writes=writes)

    def mset(ap, val, reads, writes, eng="gpsimd"):
        S.op(eng, lambda e: e.memset(ap, val), reads=reads, writes=writes)

    def scan(out, d0, d1, init, reads, writes):
        S.op("vector", lambda e: e.tensor_tensor_scan(out=out, data0=d0, data1=d1, initial=init, op0=ALU.mult, op1=ALU.add), reads=reads, writes=writes)

    def recip(out, in_, reads, writes):
        S.op("vector", lambda e: e.reciprocal(out=out, in_=in_), reads=reads, writes=writes)

    ev_i = [0]

    def evac(out, in_, reads, writes):
        ev_i[0] += 1
        if ev_i[0] % 2:
            act(out, in_, AF.Copy, reads, writes)
        else:
            cp(out, in_, reads, writes)

    dma(identf.t[:], ident_d[:, :], [], [identf.k])
    cp(identb.t[:], identf.t[:], [identf.k], [identb.k])
    mset(onesb.t[:], 1.0, [], [onesb.k])
    dma(tri.t[:], tri_d[:, :], [], [tri.k])
    dma(cmask.t[:], cmask_d.rearrange("a b -> (a b)").partition_broadcast(128), [], [cmask.k])

    def wview(name, l):
        return W[name][l].rearrange("(kc p) n -> p kc n", p=128)

    cast_done = set()

    def cast_layer(l):
        if l in cast_done or l >= NL:
            return
        cast_done.add(l)
        zt = tmp()
        mset(zt.f, 0.0, [], [zt.k])
        ul = {}

        def unit(name, width):
            ap = dscr(f"wu_{l}_{name}", [128, width])
            k = S.tile(f"wu_{l}_{name}")
            ul[name] = (ap, k, width)
            return ap, k

        win = wview("w_in", l)
        for name, c0, c1 in [("U0", 0, 512), ("U1", 512, 1024), ("U2", 1024, 1536), ("U3", 1536, 2048),
                             ("U4", 2048, 2560), ("U5", 2560, 3072), ("U6", 3072, 3600), ("U7", 3600, 4112)]:
            w = c1 - c0
            ap, k = unit(name, 8 * w)
            dma(ap.rearrange("p (kc n) -> p kc n", kc=8), win[:, :, c0:c1], [], [k], eng="gpsimd")
        ap, k = unit("U8", 8 * 128)
        dma(ap[:, :], zt.b[:, 0:1024], [zt.k], [k], eng="gpsimd")
        a3 = ap.rearrange("p (c n) -> p c n", c=8)
        for gi, wn in enumerate(["lru_wa", "lru_wx"]):
            for c in range(4):
                for half in range(2):
                    dma(a3[half * 64:(half + 1) * 64, gi * 4 + c, half * 64:(half + 1) * 64],
                        W[wn][l, 2 * c + half], [], [k], eng="gpsimd")
        wm = W["w_merge"][l].rearrange("(kc p) (j m f) -> p kc j m f", p=128, j=3, m=8)
        wbr = [W[n][l].rearrange("(kc p) (m f) -> p kc m f", p=128, m=8) for n in ("w_branch_attn", "w_branch_gla", "w_branch_lru")]
        for m in range(8):
            ap, k = unit(f"M{m}", 3072)
            a_m = ap[:, 0:3072].rearrange("p (kc j f) -> p kc j f", kc=8, j=3)
            for j in range(3):
                dma(a_m[:, :, j, :], wm[:, :, j, m, :], [], [k], eng="gpsimd")
            ap, k = unit(f"B{m}", 1536)
            a_b = ap[:, 0:1536].rearrange("p (kc j f) -> p kc j f", kc=4, j=3)
            for j in range(3):
                dma(a_b[:, :, j, :], wbr[j][:, :, m, :], [], [k], eng="gpsimd")
        wo = wview("w_out", l)
        for i in range(2):
            ap, k = unit(f"O{i}", 4096)
            dma(ap.rearrange("p (kc n) -> p kc n", kc=8), wo[:, :, i * 512:(i + 1) * 512], [], [k], eng="gpsimd")
        wg = wview("w_ffn_gate", l)
        wu = wview("w_ffn_up", l)
        for i in range(11):
            ap, k = unit(f"F{i}", 4096)
            a4 = ap.rearrange("p (kc g n) -> p kc g n", kc=8, g=2)
            dma(a4[:, :, 0, :], wg[:, :, i * 256:(i + 1) * 256], [], [k], eng="gpsimd")
            dma(a4[:, :, 1, :], wu[:, :, i * 256:(i + 1) * 256], [], [k], eng="gpsimd")
        wd = wview("w_ffn_down", l)
        for m in range(8):
            ap, k = unit(f"D{m}", NFC * 128)
            dma(ap.rearrange("p (kc n) -> p kc n", kc=NFC), wd[:, :, m * 128:(m + 1) * 128], [], [k], eng="gpsimd")
        units[l] = ul


    cast_layer(0)

    for l in range(NL):
        st = tmp()
        st_t = st.f[:, 0:128]
        mset(st_t, 0.0, [], [st.k])
        rows = [
            (0, 66, W["ffn_conv_w"][l].rearrange("j (c f) -> (j c) f", f=128)),
            (66, 22, W["ffn_conv_b"][l].rearrange("(c f) -> c f", f=128)),
            (88, 24, W["b_merge"][l].rearrange("(c f) -> c f", f=128)),
            (112, 8, W["norm_mix"][l].rearrange("(c f) -> c f", f=128)),
            (120, 8, W["norm_ffn"][l].rearrange("(c f) -> c f", f=128)),
        ]
        for r0, n, src in rows:
            dma(st_t[r0:r0 + n, :], src, [], [st.k])
        p = psn()
        tr(p.t[:, 0:128], st_t, identf.t[:], [st.k, identf.k], [p.k])
        cp(parA[l].t[:], p.t[:, 0:128], [p.k], [parA[l].k])
        st = tmp()
        st_t = st.f[:, 0:128]
        mset(st_t, 0.0, [], [st.k])
        rows = [
            (0, 16, W["lru_conv_w"][l].rearrange("j (c f) -> (j c) f", f=128), 128),
            (16, 4, W["lru_conv_b"][l].rearrange("(c f) -> c f", f=128), 128),
            (20, 4, W["lru_ba"][l].rearrange("(c f) -> c f", f=128), 128),
            (24, 4, W["lru_bx"][l].rearrange("(c f) -> c f", f=128), 128),
            (28, 4, W["lru_lambda"][l].rearrange("(c f) -> c f", f=128), 128),
            (32, 1, W["attn_subln"][l].rearrange("(c f) -> c f", f=128), 128),
            (33, 1, W["gla_norm"][l].rearrange("(c f) -> c f", f=128), 128),
            (34, 4, W["b_gla_gate"][l].rearrange("(c f) -> c f", f=64), 64),
            (38, 8, W["norm_final"].rearrange("(c f) -> c f", f=128), 128),
        ]
        for r0, n, src, wd_ in rows:
            dma(st_t[r0:r0 + n, 0:wd_], src, [], [st.k])
        p = psn()
        tr(p.t[:, 0:128], st_t, identf.t[:], [st.k, identf.k], [p.k])
        cp(parB[l].t[:], p.t[:, 0:64], [p.k], [parB[l].k])
        lam_init = 0.8 - 0.6 * math.exp(-0.3 * l)
        lq = tmp()
        dma(lq.f[:, 0:256], W["lambda_qk"][l].rearrange("a b -> (a b)").partition_broadcast(128), [], [lq.k])
        t1 = tmp()
        tt(t1.f[:, 0:64], lq.f[:, 0:64], lq.f[:, 64:128], ALU.mult, [lq.k], [t1.k])
        tt(t1.f[:, 64:128], lq.f[:, 128:192], lq.f[:, 192:256], ALU.mult, [lq.k, t1.k], [t1.k])
        S.op("vector", lambda e, t1=t1: e.tensor_reduce(out=t1.f[:, 128:130], in_=t1.f[:, 0:128].rearrange("p (a b) -> p a b", a=2), axis=mybir.AxisListType.X, op=ALU.add), reads=[t1.k], writes=[t1.k])
        act(t1.f[:, 130:132], t1.f[:, 128:130], AF.Exp, [t1.k], [t1.k])
        tt(t1.f[:, 132:133], t1.f[:, 131:132], t1.f[:, 130:131], ALU.subtract, [t1.k], [t1.k])
        ts(lamc[l].t[:, 0:1], t1.f[:, 132:133], -lam_init, None, ALU.add, None, [t1.k], [lamc[l].k])
        ts(lamc[l].t[:, 1:2], parB[l].t[:, 32:33], 1.0 - lam_init, None, ALU.mult, None, [parB[l].k, lamc[l].k], [lamc[l].k])
        act(t1.f[:, 140:144], parB[l].t[:, 28:32], AF.Exp, [parB[l].k, t1.k], [t1.k], scale=-1.0)
        act(t1.f[:, 144:148], t1.f[:, 140:144], AF.Ln, [t1.k], [t1.k], bias=1.0)
        ts(lamc[l].t[:, 2:6], t1.f[:, 144:148], -8.0, None, ALU.mult, None, [t1.k, lamc[l].k], [lamc[l].k])
        ts(parB[l].t[:, 46:50], parB[l].t[:, 34:38], -1.0, None, ALU.mult, None, [parB[l].k], [parB[l].k])
        t2 = tmp()
        dma(t2.f[0:16, 0:256], W["w_gla_gate2"][l], [], [t2.k])
        cp(w2b[l].t[:], t2.f[0:16, 0:256], [t2.k], [w2b[l].k])

    def wload(l, name):
        ap, k, width = units[l][name]
        slot = wring[wr_i[0] % NWR]
        wr_i[0] += 1
        dma(slot.t[:, 0:width], ap[:, :], [k], [slot.k])
        return slot

    def run_tile(seq, TT, BP, NB, L, NCK, xsrc, ysrc, rope_src, t0, units_past, is_last, outs):
        S.phase = 'xload'
        base, xk = tmp8()

        def xtok(bi):
            return pool_t[0:BP, base + 2 * bi: base + 2 * bi + 2, :].rearrange("p a n -> p (a n)")

        for bi in range(NB):
            dma(xtok(bi), xsrc[bi * BP:(bi + 1) * BP, :], [], [xk[2 * bi], xk[2 * bi + 1]])
        dma(rope_sb.t[0:BP, 0:NB, :], rope_src.rearrange("(b p) f -> p b f", p=BP), [], [rope_sb.k])
        for c in range(8):
            p = psn()
            for bi in range(NB):
                tr(p.t[:, bi * BP:(bi + 1) * BP], xtok(bi)[:, c * 128:(c + 1) * 128], identf.t[0:BP, 0:BP],
                   [xk[2 * bi], xk[2 * bi + 1], identf.k], [p.k])
            evac(xT.t[:, c, 0:TT], p.t[:, 0:TT], [p.k], [xT.k])

        def rmsnorm_to(dst, gcol):
            p = psn()
            for c in range(8):
                sq = tmp()
                act(sq.b[:, 0:TT], xT.t[:, c, 0:TT], AF.Square, [xT.k], [sq.k])
                mm(p.t[:, 0:TT], onesb.t[:, :], sq.b[:, 0:TT], c == 0, c == 7, [onesb.k, sq.k], [p.k])
            rs = tmp()
            act(rs.f[:, 0:TT], p.t[:, 0:TT], AF.Ln, [p.k], [rs.k], bias=eps_ap, scale=1.0 / D)
            act(rs.f[:, 0:TT], rs.f[:, 0:TT], AF.Exp, [rs.k], [rs.k], scale=-0.5)
            for c in range(8):
                stt(dst.t[:, c, 0:TT], xT.t[:, c, 0:TT], gcol(c), rs.f[:, 0:TT], ALU.mult, ALU.mult, [xT.k, rs.k], [dst.k])

        for l in range(NL):
            cast_layer(l + 1)
            pA, pB = parA[l], parB[l]
            if t0 == 0:
                if seq == 'p':
                    mset(gS[l].t[:], 0.0, [], gSk[l])
                    mset(gSb[l].t[:], 0.0, [], gSbk[l])
                    mset(hst[l].t[:], 0.0, [], [hst[l].k])
                    mset(lxh[l].t[:], 0.0, [], [lxh[l].k])
                    mset(guh[l].t[:], 0.0, [], [guh[l].k])
                else:
                    dma(gS[l].t[:], sg[l, seq].rearrange("h k v -> k h v"), [], gSk[l])
                    cp(gSb[l].t[:], gS[l].t[:], gSk[l], gSbk[l], eng="gpsimd")
                    dma(hst[l].t[:], slh[l, seq].rearrange("(c p) -> p c", p=128), [], [hst[l].k], allow_slow_non_contiguous=True)
                    for j in range(3):
                        dma(lxh[l].t[:, :, j], slc[l, seq, j].rearrange("(c p) -> p c", p=128), [], [lxh[l].k], allow_slow_non_contiguous=True)
                    for j in range(2):
                        dma(guh[l].t[:, :, j], sfc[l, seq, j].rearrange("(c p) -> p c", p=128), [], [guh[l].k], allow_slow_non_contiguous=True)

            S.phase = 'norm1'
            rmsnorm_to(xn, lambda c: pA.t[:, 112 + c:113 + c])
            xn_r = [xn.k, pA.k]

            S.phase = 'qkv'
            for ui, uname in enumerate(["U0", "U1", "U2"]):
                ws = wload(l, uname)
                w3 = ws.t[:, 0:4096].rearrange("p (kc n) -> p kc n", kc=8)
                tbs = []
                for bi in range(NB):
                    p = psn()
                    for kc in range(8):
                        mm(p.t[0:BP, :], xn.t[:, kc, bi * BP:(bi + 1) * BP], w3[:, kc, :], kc == 0, kc == 7, [xn.k, ws.k], [p.k])
                    tk = tmp()
                    act(tk.f[0:BP, :], p.t[0:BP, :], AF.Copy, [p.k], [tk.k])
                    if ui < 2:
                        x4 = tk.f[0:BP, :].rearrange("p (a d) -> p a d", d=64)
                        cs4 = rope_sb.t[0:BP, bi, 0:64].rearrange("p (a d) -> p a d", d=8)
                        sn4 = rope_sb.t[0:BP, bi, 64:128].rearrange("p (a d) -> p a d", d=8)
                        tq = tmp()
                        q4 = tq.f[0:BP, 0:256].rearrange("p (j a d) -> p j a d", j=4, d=8)
                        rk = [tk.k, rope_sb.k, tq.k]
                        tt(q4[:, 0], x4[:, :, 0:8], cs4, ALU.mult, rk, [tq.k])
                        tt(q4[:, 1], x4[:, :, 8:16], sn4, ALU.mult, rk, [tq.k])
                        tt(q4[:, 2], x4[:, :, 8:16], cs4, ALU.mult, rk, [tq.k])
                        tt(q4[:, 3], x4[:, :, 0:8], sn4, ALU.mult, rk, [tq.k])
                        tt(x4[:, :, 0:8], q4[:, 0], q4[:, 1], ALU.subtract, [tq.k, tk.k], [tk.k])
                        tt(x4[:, :, 8:16], q4[:, 2], q4[:, 3], ALU.add, [tq.k, tk.k], [tk.k])
                        tb = tmp()
                        cp(tb.b[0:BP, 0:512], tk.f[0:BP, :], [tk.k], [tb.k])
                        tbs.append(tb)
                        if ui == 1:
                            ot = otile()
                            dma(outs["k"][l][bi * BP:(bi + 1) * BP, :], tk.f[0:BP, :], [tk.k], [ot], eng="gpsimd")
                    else:
                        ot = otile()
                        dma(outs["v"][l][bi * BP:(bi + 1) * BP, :], tk.f[0:BP, :], [tk.k], [ot], eng="gpsimd")
                        cp(v_bf.t[0:BP, bi, :], tk.f[0:BP, :], [tk.k], [v_bf.k], eng="gpsimd")
                for bi, tb in enumerate(tbs):
                    pt = psn()
                    ptb = pt.t[:].bitcast(BF16)
                    for hc in range(4):
                        tr(ptb[:, hc * BP:(hc + 1) * BP], tb.b[0:BP, hc * 128:(hc + 1) * 128], identb.t[0:BP, 0:BP], [tb.k, identb.k], [pt.k])
                    dst = QT if ui == 0 else KTc
                    evac(dst.t[:, :, bi * BP:(bi + 1) * BP], ptb[:, 0:4 * BP].rearrange("p (h t) -> p h t", h=4), [pt.k], [dst.k])
            if seq == 'p' and not is_last:
                u = t0 // 512
                dma(ktscr[l].rearrange("h p t -> p h t")[:, :, t0:t0 + TT], KTc.t[:, :, 0:TT], [KTc.k], [kt_t[l][h][u] for h in range(4)], eng="gpsimd")
                for h in range(4):
                    dma(vscr[l, h, u].rearrange("p (b e) -> p b e", b=4), v_bf.t[:, :, h * 128:(h + 1) * 128], [v_bf.k], [v_t[l][h][u]], eng="gpsimd")

            S.phase = 'gla_prep'
            ws = wload(l, "U6")
            w3 = ws.t[:, 0:8 * 528].rearrange("p (kc n) -> p kc n", kc=8)
            p = psn()
            for kc in range(8):
                mm(p.t[0:16, 0:TT], w3[:, kc, 0:16], xn.t[:, kc, 0:TT], kc == 0, kc == 7, [xn.k, ws.k], [p.k])
            evac(gab.t[0:16, 0:TT], p.t[0:16, 0:TT], [p.k], [gab.k])
            cp(lxT.t[:, :, 0:3], lxh[l].t[:, :, :], [lxh[l].k], [lxT.k], eng="gpsimd")
            for c in range(4):
                p = psn()
                for kc in range(8):
                    mm(p.t[:, 0:TT], w3[:, kc, 16 + c * 128:16 + (c + 1) * 128], xn.t[:, kc, 0:TT], kc == 0, kc == 7, [xn.k, ws.k], [p.k])
                evac(lxT.t[:, c, 3:3 + TT], p.t[:, 0:TT], [p.k], [lxT.k])
            cp(lxh[l].t[:, :, :], lxT.t[:, :, TT:TT + 3], [lxT.k], [lxh[l].k], eng="gpsimd")
            w8 = wload(l, "U8")
            w83 = w8.t[:, 0:1024].rearrange("p (c n) -> p c n", c=8)
            w7 = wload(l, "U7")
            w73 = w7.t[:, 0:4096].rearrange("p (kc n) -> p kc n", kc=8)


            S.phase = 'attn'
            acc = PS[0:4]
            sring = PS[4:8]
            sr_i = 0
            nkb_cur = (TT + 127) // 128
            pend_fin = None
            for h in range(4):
                S.phase = 'attn'

                def load_unit(u):
                    slot = kv_i[0] % 3
                    kv_i[0] += 1
                    kr, vr = kring[slot], vring[slot]
                    if seq == 'p':
                        dma(kr.t[:, :], ktscr[l, h, :, u * 512:(u + 1) * 512], [kt_t[l][h][u]], [kr.k])
                        dma(vr.t[:, :, :], vscr[l, h, u].rearrange("p (b e) -> p b e", b=4), [v_t[l][h][u]], [vr.k])
                    else:
                        ks_ = kst[0]
                        dma(ks_.t[:, :, :], ck[l, seq, u * 512:(u + 1) * 512, h * 128:(h + 1) * 128].rearrange("(b p) f -> p b f", p=128), [], [ks_.k], eng="gpsimd")
                        dma(vr.t[:, :, :], cv[l, seq, u * 512:(u + 1) * 512, h * 128:(h + 1) * 128].rearrange("(b p) f -> p b f", p=128), [], [vr.k], eng="gpsimd")
                        pt = sring[sr_box[0] % 4]
                        sr_box[0] += 1
                        ptb = pt.t[:].bitcast(BF16)
                        for b_ in range(4):
                            tr(ptb[:, b_ * 128:(b_ + 1) * 128], ks_.t[:, b_, :], identb.t[:, :], [ks_.k, identb.k], [pt.k])
                        evac(kr.t[:, :], ptb[:, 0:512], [pt.k], [kr.k])
                    return kr, vr

                sr_box = [sr_i]
                loaded = {}

                def ensure(u):
                    if u < units_past and u not in loaded:
                        loaded[u] = load_unit(u)

                nblk = units_past * 4 + nkb_cur

                def gen_blocks():
                    for u in range(units_past):
                        ensure(u)
                        ensure(u + 1)
                        kr, vr = loaded.pop(u)
                        for kb in range(4):
                            yield (kr.t[:, kb * 128:(kb + 1) * 128], vr.t[:, kb, :], 128, None, [kr.k, vr.k])
                    for kb in range(nkb_cur):
                        nk = min(128, TT - kb * 128)
                        yield (KTc.t[:, h, kb * 128:kb * 128 + nk], v_bf.t[0:nk, kb, h * 128:(h + 1) * 128], nk,
                               kb if seq == 'p' else None, [KTc.k, v_bf.k])

                def stage1(blk):
                    ksrc, vsrc, nk, diag, rds = blk
                    prs = []
                    for m in range(2):
                        sp_ = sring[sr_box[0] % 4]
                        sr_box[0] += 1
                        mm(sp_.t[0:nk, 0:TT], ksrc[m * 64:(m + 1) * 64, :], QT.t[m * 64:(m + 1) * 64, h, 0:TT], True, True, rds + [QT.k], [sp_.k])
                        pr = pring[pr_i[0] % 4]
                        pr_i[0] += 1
                        act(pr.t[0:nk, 0:TT], sp_.t[0:nk, 0:TT], AF.Exp, [sp_.k], [pr.k], scale=0.125)
                        if diag is not None:
                            if diag > 0:
                                mset(pr.t[0:nk, 0:diag * 128], 0.0, [pr.k], [pr.k])
                            if nk > 64:
                                mset(pr.t[64:nk, diag * 128:diag * 128 + 64], 0.0, [pr.k], [pr.k])
                        prs.append(pr)
                    return prs

                def stage2(blk, prs, bidx):
                    ksrc, vsrc, nk, diag, rds = blk
                    first = bidx == 0
                    last = bidx == nblk - 1
                    for m in range(2):
                        pr = prs[m]
                        mm(acc[m].t[:, 0:TT], vsrc, pr.t[0:nk, 0:TT], first, last, rds + [pr.k], [acc[m].k])
                        mm(acc[2 + m].t[:, 0:TT], onesb.t[0:nk, :], pr.t[0:nk, 0:TT], first, last, [onesb.k, pr.k], [acc[2 + m].k])

                def sbank():
                    bk = sring[sr_box[0] % 4]
                    sr_box[0] += 1
                    return bk

                prev = None
                bidx = 0
                nstage1 = 0
                for blk in gen_blocks():
                    prs = stage1(blk)
                    nstage1 += 1
                    if prev is not None:
                        stage2(prev[0], prev[1], bidx)
                        bidx += 1
                    prev = (blk, prs)
                    if nstage1 == 2 and pend_fin is not None:
                        pend_fin(sbank)
                        pend_fin = None
                        S.phase = 'attn'
                stage2(prev[0], prev[1], bidx)
                sr_i = sr_box[0]
                if pend_fin is not None:
                    pend_fin(sbank)
                    pend_fin = None
                    sr_i = sr_box[0]
                S.phase = 'attn_fin'
                r0, r1, t0_, t1_ = tmp(), tmp(), tmp(), tmp()
                act(r0.f[:, 0:TT], acc[2].t[:, 0:TT], AF.Ln, [acc[2].k], [r0.k])
                act(r1.f[:, 0:TT], acc[3].t[:, 0:TT], AF.Ln, [acc[3].k], [r1.k])
                act(r0.f[:, 0:TT], r0.f[:, 0:TT], AF.Exp, [r0.k], [r0.k], scale=-1.0)
                act(r1.f[:, 0:TT], r1.f[:, 0:TT], AF.Exp, [r1.k], [r1.k], scale=-1.0)
                tt(t0_.f[:, 0:TT], acc[0].t[:, 0:TT], r0.f[:, 0:TT], ALU.mult, [acc[0].k, r0.k], [t0_.k])
                tt(t1_.f[:, 0:TT], acc[1].t[:, 0:TT], r1.f[:, 0:TT], ALU.mult, [acc[1].k, r1.k], [t1_.k])
                stt(t0_.f[:, 0:TT], t1_.f[:, 0:TT], lamc[l].t[:, 0:1], t0_.f[:, 0:TT], ALU.mult, ALU.add, [t1_.k, t0_.k, lamc[l].k], [t0_.k])
                sq = tmp()
                act(sq.b[:, 0:TT], t0_.f[:, 0:TT], AF.Square, [t0_.k], [sq.k])

                def fin_b(bank, h=h, t0_=t0_, sq=sq):
                    S.phase = 'attn_fin'
                    pb_ = bank()
                    mm(pb_.t[:, 0:TT], onesb.t[:, :], sq.b[:, 0:TT], True, True, [onesb.k, sq.k], [pb_.k])
                    rs = tmp()
                    act(rs.f[:, 0:TT], pb_.t[:, 0:TT], AF.Ln, [pb_.k], [rs.k], bias=eps_ap, scale=1.0 / 128)
                    act(rs.f[:, 0:TT], rs.f[:, 0:TT], AF.Exp, [rs.k], [rs.k], scale=-0.5)
                    stt(oattn.t[:, h, 0:TT], t0_.f[:, 0:TT], lamc[l].t[:, 1:2], rs.f[:, 0:TT], ALU.mult, ALU.mult, [t0_.k, rs.k, lamc[l].k], [oattn.k])

                pend_fin = fin_b
            sr_box = [sr_i]

            def sbank2():
                bk = sring[sr_box[0] % 4]
                sr_box[0] += 1
                return bk
            pend_fin(sbank2)
            pend_fin = None

            S.phase = 'gla_prep'
            ws = wload(l, "U4")
            w3 = ws.t[:, 0:4096].rearrange("p (kc n) -> p kc n", kc=8)
            for ck_ in range(NCK):
                p = psn()
                for kc in range(8):
                    mm(p.t[0:L, :], xn.t[:, kc, ck_ * L:(ck_ + 1) * L], w3[:, kc, :], kc == 0, kc == 7, [xn.k, ws.k], [p.k])
                evac(gv_tok.t[0:L, ck_, :], p.t[0:L, :], [p.k], [gv_tok.k])
            for h in range(4):
                p = psn()
                mm(p.t[0:64, 0:TT], w2b[l].t[0:16, h * 64:(h + 1) * 64], gab.t[0:16, 0:TT], True, True, [w2b[l].k, gab.k], [p.k])
                e1 = tmp()
                act(e1.f[0:64, 0:TT], p.t[0:64, 0:TT], AF.Exp, [p.k, pB.k], [e1.k], bias=pB.t[0:64, 46 + h:47 + h], scale=-1.0)
                act(e1.f[0:64, 0:TT], e1.f[0:64, 0:TT], AF.Ln, [e1.k], [e1.k], bias=1.0)
                scan(gcs.t[:, h, 0:TT], cmask.t[0:64, 0:TT], e1.f[0:64, 0:TT], 0.0, [cmask.k, e1.k], [gcs.k])
            ws = wload(l, "U3")
            w3 = ws.t[:, 0:4096].rearrange("p (kc n) -> p kc n", kc=8)
            for h in range(4):
                p = psn()
                for kc in range(8):
                    mm(p.t[0:64, 0:TT], w3[:, kc, h * 64:(h + 1) * 64], xn.t[:, kc, 0:TT], kc == 0, kc == 7, [xn.k, ws.k], [p.k])
                eq = tmp()
                act(eq.f[0:64, 0:TT], gcs.t[:, h, 0:TT], AF.Exp, [gcs.k], [eq.k], scale=-1.0 / 16)
                stt(qtl.t[:, h, 0:TT], p.t[0:64, 0:TT], 0.125, eq.f[0:64, 0:TT], ALU.mult, ALU.mult, [p.k, eq.k], [qtl.k])
                cp(ebl.t[:, h, 0:NCK], eq.f[0:64, 0:TT].rearrange("p (c t) -> p c t", t=L)[:, :, L - 1], [eq.k], [ebl.k])
                p = psn()
                for kc in range(8):
                    mm(p.t[0:64, 0:TT], w3[:, kc, 256 + h * 64:256 + (h + 1) * 64], xn.t[:, kc, 0:TT], kc == 0, kc == 7, [xn.k, ws.k], [p.k])
                ek = tmp()
                act(ek.f[0:64, 0:TT], gcs.t[:, h, 0:TT], AF.Exp, [gcs.k], [ek.k], scale=1.0 / 16)
                tt(ktl.t[:, h, 0:TT], p.t[0:64, 0:TT], ek.f[0:64, 0:TT], ALU.mult, [p.k, ek.k], [ktl.k])
            for ck_ in range(NCK):
                pt = psn()
                ptb = pt.t[:].bitcast(BF16)
                for h in range(4):
                    tr(ptb[0:L, h * 64:(h + 1) * 64], ktl.t[:, h, ck_ * L:(ck_ + 1) * L], identb.t[0:64, 0:64], [ktl.k, identb.k], [pt.k])
                evac(ktok.t[0:L, ck_, :], ptb[0:L, 0:256], [pt.k], [ktok.k])
            S.phase = 'gla_chunks'
            gacc = PS[0:4]
            gring = PS[4:8]
            gr_i = 0
            for ck_ in range(NCK):
                cs_ = slice(ck_ * L, (ck_ + 1) * L)
                pas, ams = [], []
                for h in range(4):
                    pa = gring[h]
                    mm(pa.t[0:L, 0:L], ktl.t[:, h, cs_], qtl.t[:, h, cs_], True, True, [ktl.k, qtl.k], [pa.k])
                    pas.append(pa)
                for h in range(4):
                    am = tmp()
                    tt(am.b[0:L, 0:L], pas[h].t[0:L, 0:L], tri.t[0:L, 0:L], ALU.mult, [pas[h].k, tri.k], [am.k])
                    ams.append(am)
                for h in range(4):
                    mm(gacc[h].t[:, cs_], gv_tok.t[0:L, ck_, h * 128:(h + 1) * 128], ams[h].b[0:L, 0:L], True, False, [gv_tok.k, ams[h].k], [gacc[h].k])
                    mm(gacc[h].t[:, cs_], gSb[l].t[:, h, :], qtl.t[:, h, cs_], False, True, [gSbk[l][h], qtl.k], [gacc[h].k])
                for h in range(4):
                    pu = gring[h]
                    mm(pu.t[0:64, 0:128], ktok.t[0:L, ck_, h * 64:(h + 1) * 64], gv_tok.t[0:L, ck_, h * 128:(h + 1) * 128], True, True, [ktok.k, gv_tok.k], [pu.k])
                for h in range(4):
                    pu = gring[h]
                    tt(gS[l].t[:, h, :], gS[l].t[:, h, :], pu.t[0:64, 0:128], ALU.add, [gSk[l][h], pu.k], [gSk[l][h]])
                    ts(gS[l].t[:, h, :], gS[l].t[:, h, :], ebl.t[:, h, ck_:ck_ + 1], None, ALU.mult, None, [gSk[l][h], ebl.k], [gSk[l][h]])
                    evac(gSb[l].t[:, h, :], gS[l].t[:, h, :], [gSk[l][h]], [gSbk[l][h]])
            gr_i = 0
            S.phase = 'gla_out'
            ws = wload(l, "U5")
            w3 = ws.t[:, 0:4096].rearrange("p (kc n) -> p kc n", kc=8)
            for h in range(4):
                gf = tmp()
                act(gf.f[:, 0:TT], gacc[h].t[:, 0:TT], AF.Copy, [gacc[h].k], [gf.k])
                sq = tmp()
                act(sq.b[:, 0:TT], gf.f[:, 0:TT], AF.Square, [gf.k], [sq.k])
                p = gring[gr_i % 4]
                gr_i += 1
                mm(p.t[:, 0:TT], onesb.t[:, :], sq.b[:, 0:TT], True, True, [onesb.k, sq.k], [p.k])
                rs = tmp()
                act(rs.f[:, 0:TT], p.t[:, 0:TT], AF.Ln, [p.k], [rs.k], bias=eps_ap, scale=1.0 / 128)
                act(rs.f[:, 0:TT], rs.f[:, 0:TT], AF.Exp, [rs.k], [rs.k], scale=-0.5)
                stt(gf.f[:, 0:TT], gf.f[:, 0:TT], pB.t[:, 33:34], rs.f[:, 0:TT], ALU.mult, ALU.mult, [gf.k, rs.k, pB.k], [gf.k])
                p = gring[gr_i % 4]
                gr_i += 1
                for kc in range(8):
                    mm(p.t[:, 0:TT], w3[:, kc, h * 128:(h + 1) * 128], xn.t[:, kc, 0:TT], kc == 0, kc == 7, [xn.k, ws.k], [p.k])
                sl = tmp()
                act(sl.f[:, 0:TT], p.t[:, 0:TT], AF.Silu, [p.k], [sl.k])
                tt(gon.t[:, h, 0:TT], gf.f[:, 0:TT], sl.f[:, 0:TT], ALU.mult, [gf.k, sl.k], [gon.k])

            S.phase = 'lru'
            for half in range(2):
                cs2 = [2 * half, 2 * half + 1]
                T1 = {c: tmp() for c in cs2}
                T2 = {c: tmp() for c in cs2}
                T3 = {c: tmp() for c in cs2}
                T4 = {c: tmp() for c in cs2}
                rk = [lxT.k, pB.k]
                for c in cs2:
                    xc = T1[c]
                    ts(xc.f[:, 0:TT], lxT.t[:, c, 0:TT], pB.t[:, c:c + 1], pB.t[:, 16 + c:17 + c], ALU.mult, ALU.add, rk, [xc.k])
                for j in range(1, 4):
                    for c in cs2:
                        xc = T1[c]
                        stt(xc.f[:, 0:TT], lxT.t[:, c, j:j + TT], pB.t[:, 4 * j + c:4 * j + c + 1], xc.f[:, 0:TT], ALU.mult, ALU.add, rk + [xc.k], [xc.k])
                for c in cs2:
                    cp(T2[c].b[:, 0:TT], T1[c].f[:, 0:TT], [T1[c].k], [T2[c].k], eng="gpsimd")
                prs_, pis_ = {}, {}
                for c in cs2:
                    prs_[c] = psn()
                    mm(prs_[c].t[:, 0:TT], w83[:, c, :], T2[c].b[:, 0:TT], True, True, [w8.k, T2[c].k], [prs_[c].k])
                    pis_[c] = psn()
                    mm(pis_[c].t[:, 0:TT], w83[:, 4 + c, :], T2[c].b[:, 0:TT], True, True, [w8.k, T2[c].k], [pis_[c].k])
                for c in cs2:
                    act(T3[c].f[:, 0:TT], prs_[c].t[:, 0:TT], AF.Sigmoid, [prs_[c].k, pB.k], [T3[c].k], bias=pB.t[:, 20 + c:21 + c])
                    act(T4[c].f[:, 0:TT], pis_[c].t[:, 0:TT], AF.Sigmoid, [pis_[c].k, pB.k], [T4[c].k], bias=pB.t[:, 24 + c:25 + c])
                for c in cs2:
                    act(T3[c].f[:, 0:TT], T3[c].f[:, 0:TT], AF.Exp, [T3[c].k, lamc[l].k], [T3[c].k], scale=lamc[l].t[:, 2 + c:3 + c])
                for c in cs2:
                    tt(T2[c].f[:, 0:TT], T3[c].f[:, 0:TT], T3[c].f[:, 0:TT], ALU.mult, [T3[c].k, T2[c].k], [T2[c].k])
                    ts(T2[c].f[:, 0:TT], T2[c].f[:, 0:TT], -1.0, 1.0, ALU.mult, ALU.add, [T2[c].k], [T2[c].k])
                    ts(T2[c].f[:, 0:TT], T2[c].f[:, 0:TT], 1e-18, None, ALU.max, None, [T2[c].k], [T2[c].k])
                    tt(T4[c].f[:, 0:TT], T4[c].f[:, 0:TT], T1[c].f[:, 0:TT], ALU.mult, [T4[c].k, T1[c].k], [T4[c].k])
                for c in cs2:
                    act(T2[c].f[:, 0:TT], T2[c].f[:, 0:TT], AF.Ln, [T2[c].k], [T2[c].k])
                for c in cs2:
                    act(T2[c].f[:, 0:TT], T2[c].f[:, 0:TT], AF.Exp, [T2[c].k], [T2[c].k], scale=0.5)
                for c in cs2:
                    tt(T4[c].f[:, 0:TT], T4[c].f[:, 0:TT], T2[c].f[:, 0:TT], ALU.mult, [T4[c].k, T2[c].k], [T4[c].k])
                for c in cs2:
                    scan(T1[c].f[:, 0:TT], T3[c].f[:, 0:TT], T4[c].f[:, 0:TT], hst[l].t[:, c:c + 1], [T3[c].k, T4[c].k, hst[l].k, T1[c].k], [T1[c].k])
                    cp(hst[l].t[:, c:c + 1], T1[c].f[:, TT - 1:TT], [T1[c].k], [hst[l].k])
                pgs = {}
                for c in cs2:
                    pgs[c] = psn()
                    for kc in range(8):
                        mm(pgs[c].t[:, 0:TT], w73[:, kc, c * 128:(c + 1) * 128], xn.t[:, kc, 0:TT], kc == 0, kc == 7, [xn.k, w7.k], [pgs[c].k])
                for c in cs2:
                    act(T2[c].f[:, 0:TT], pgs[c].t[:, 0:TT], AF.Gelu_apprx_tanh, [pgs[c].k, T2[c].k], [T2[c].k])
                for c in cs2:
                    tt(hl.t[:, c, 0:TT], T1[c].f[:, 0:TT], T2[c].f[:, 0:TT], ALU.mult, [T1[c].k, T2[c].k], [hl.k])

            S.phase = 'merge'
            for m in range(8):
                ws = wload(l, f"M{m}")
                wsb = wload(l, f"B{m}")
                wm4 = ws.t[:, 0:3072].rearrange("p (kc j f) -> p kc j f", kc=8, j=3)
                wb4 = wsb.t[:, 0:1536].rearrange("p (kc j f) -> p kc j f", kc=4, j=3)
                srcs = [oattn, gon, hl]
                acc_t = None
                for j in range(3):
                    pg = psn()
                    for kc in range(8):
                        mm(pg.t[:, 0:TT], wm4[:, kc, j, :], xn.t[:, kc, 0:TT], kc == 0, kc == 7, [xn.k, ws.k], [pg.k])
                    g = tmp()
                    act(g.f[:, 0:TT], pg.t[:, 0:TT], AF.Sigmoid, [pg.k, pA.k], [g.k], bias=pA.t[:, 88 + j * 8 + m:89 + j * 8 + m])
                    py = psn()
                    for kc in range(4):
                        mm(py.t[:, 0:TT], wb4[:, kc, j, :], srcs[j].t[:, kc, 0:TT], kc == 0, kc == 3, [srcs[j].k, wsb.k], [py.k])
                    if j == 0:
                        acc_t = tmp()
                        tt(acc_t.f[:, 0:TT], g.f[:, 0:TT], py.t[:, 0:TT], ALU.mult, [g.k, py.k], [acc_t.k])
                    else:
                        tt(g.f[:, 0:TT], g.f[:, 0:TT], py.t[:, 0:TT], ALU.mult, [g.k, py.k], [g.k])
                        if j == 1:
                            tt(acc_t.f[:, 0:TT], acc_t.f[:, 0:TT], g.f[:, 0:TT], ALU.add, [g.k, acc_t.k], [acc_t.k])
                        else:
                            tt(mergedb.t[:, m, 0:TT], acc_t.f[:, 0:TT], g.f[:, 0:TT], ALU.add, [g.k, acc_t.k], [mergedb.k])
            S.phase = 'wout'
            for i in range(2):
                ws = wload(l, f"O{i}")
                w3 = ws.t[:, 0:4096].rearrange("p (kc n) -> p kc n", kc=8)
                for mi in range(4):
                    m = i * 4 + mi
                    p = psn()
                    for kc in range(8):
                        mm(p.t[:, 0:TT], w3[:, kc, mi * 128:(mi + 1) * 128], mergedb.t[:, kc, 0:TT], kc == 0, kc == 7, [mergedb.k, ws.k], [p.k])
                    tt(xT.t[:, m, 0:TT], xT.t[:, m, 0:TT], p.t[:, 0:TT], ALU.add, [xT.k, p.k], [xT.k])

            S.phase = 'ffn_gu'
            rmsnorm_to(xn, lambda c: pA.t[:, 120 + c:121 + c])
            for i in range(11):
                ws = wload(l, f"F{i}")
                w4 = ws.t[:, 0:4096].rearrange("p (kc g n) -> p kc g n", kc=8, g=2)
                for s_ in range(2):
                    j = 2 * i + s_
                    pg = psn()
                    for kc in range(8):
                        mm(pg.t[:, 0:TT], w4[:, kc, 0, s_ * 128:(s_ + 1) * 128], xn.t[:, kc, 0:TT], kc == 0, kc == 7, [xn.k, ws.k], [pg.k])
                    pu = psn()
                    for kc in range(8):
                        mm(pu.t[:, 0:TT], w4[:, kc, 1, s_ * 128:(s_ + 1) * 128], xn.t[:, kc, 0:TT], kc == 0, kc == 7, [xn.k, ws.k], [pu.k])
                    gu = tmp()
                    gu2 = tmp()
                    act(gu.f[:, 0:TT], pg.t[:, 0:TT], AF.Copy, [pg.k], [gu.k])
                    gc = gu2
                    act(gc.f[:, 0:TT], pg.t[:, 0:TT], AF.Identity, [pg.k, pA.k], [gc.k], bias=pA.t[:, 66 + j:67 + j], scale=pA.t[:, 44 + j:45 + j])
                    if TT > 1:
                        stt(gc.f[:, 1:TT], gu.f[:, 0:TT - 1], pA.t[:, 22 + j:23 + j], gc.f[:, 1:TT], ALU.mult, ALU.add, [gu.k, gc.k, pA.k], [gc.k])
                    stt(gc.f[:, 0:1], guh[l].t[:, j, 1:2], pA.t[:, 22 + j:23 + j], gc.f[:, 0:1], ALU.mult, ALU.add, [guh[l].k, gc.k, pA.k], [gc.k])
                    stt(gc.f[:, 2:TT], gu.f[:, 0:TT - 2], pA.t[:, j:j + 1], gc.f[:, 2:TT], ALU.mult, ALU.add, [gu.k, gc.k, pA.k], [gc.k])
                    stt(gc.f[:, 0:2], guh[l].t[:, j, 0:2], pA.t[:, j:j + 1], gc.f[:, 0:2], ALU.mult, ALU.add, [guh[l].k, gc.k, pA.k], [gc.k])
                    cp(guh[l].t[:, j, :], gu.f[:, TT - 2:TT], [gu.k, gc.k], [guh[l].k], eng="gpsimd")
                    act(gc.f[:, 0:TT], gc.f[:, 0:TT], AF.Gelu_apprx_tanh, [gc.k], [gc.k])
                    tt(fT_t[:, j, 0:TT], gc.f[:, 0:TT], pu.t[:, 0:TT], ALU.mult, [gc.k, pu.k], [fT[j].k])
            S.phase = 'ffn_down'
            for m in range(8):
                ws = wload(l, f"D{m}")
                w3 = ws.t[:, 0:NFC * 128].rearrange("p (kc n) -> p kc n", kc=NFC)
                p = psn()
                for kc in range(NFC):
                    mm(p.t[:, 0:TT], w3[:, kc, :], fT_t[:, kc, 0:TT], kc == 0, kc == NFC - 1, [fT[kc].k, ws.k], [p.k])
                tt(xT.t[:, m, 0:TT], xT.t[:, m, 0:TT], p.t[:, 0:TT], ALU.add, [xT.k, p.k], [xT.k])

            S.phase = 'stateout'
            if is_last:
                ot = otile()
                dma(outs["gla"][l].rearrange("h k v -> k h v"), gS[l].t[:, :, :], gSk[l], [ot], eng="gpsimd")
                for j in range(3):
                    ot = otile()
                    dma(outs["lc"][l][j].rearrange("(c p) -> p c", p=128), lxh[l].t[:, :, j], [lxh[l].k], [ot], allow_slow_non_contiguous=True)
                ot = otile()
                dma(outs["lh"][l].rearrange("(c p) -> p c", p=128), hst[l].t[:, :], [hst[l].k], [ot], allow_slow_non_contiguous=True)
                for j in range(2):
                    ot = otile()
                    dma(outs["fc"][l][j].rearrange("(c p) -> p c", p=128), guh[l].t[:, :, j], [guh[l].k], [ot], allow_slow_non_contiguous=True)

        S.phase = 'yout'
        p = psn()
        for c in range(8):
            sq = tmp()
            act(sq.b[:, 0:TT], xT.t[:, c, 0:TT], AF.Square, [xT.k], [sq.k])
            mm(p.t[:, 0:TT], onesb.t[:, :], sq.b[:, 0:TT], c == 0, c == 7, [onesb.k, sq.k], [p.k])
        rsf = lxT.t[:, 0, 0:TT]
        act(rsf, p.t[:, 0:TT], AF.Ln, [p.k], [lxT.k], bias=eps_ap, scale=1.0 / D)
        act(rsf, rsf, AF.Exp, [lxT.k], [lxT.k], scale=-0.5)
        base2, yk = tmp8()

        def ytok(bi):
            return pool_t[0:BP, base2 + 2 * bi: base2 + 2 * bi + 2, :].rearrange("p a n -> p (a n)")

        for c in range(8):
            yc = tmp()
            stt(yc.f[:, 0:TT], xT.t[:, c, 0:TT], parB[0].t[:, 38 + c:39 + c], rsf, ALU.mult, ALU.mult, [xT.k, lxT.k, parB[0].k], [yc.k])
            pt = psn()
            for bi in range(NB):
                tr(pt.t[0:BP, bi * 128:(bi + 1) * 128], yc.f[:, bi * BP:(bi + 1) * BP], identf.t[:, :], [yc.k, identf.k], [pt.k])
            for bi in range(NB):
                evac(ytok(bi)[:, c * 128:(c + 1) * 128], pt.t[0:BP, bi * 128:(bi + 1) * 128], [pt.k], [yk[2 * bi], yk[2 * bi + 1]])
        for bi in range(NB):
            ot = otile()
            dma(ysrc[bi * BP:(bi + 1) * BP, :], ytok(bi), [yk[2 * bi], yk[2 * bi + 1]], [ot], eng="gpsimd")

    epsb = mk("epsb", [128, 1], F32)
    mset(epsb.t[:], EPS, [], [epsb.k])
    eps_ap = epsb.t[:, 0:1]
    _orig_act = act

    def act(out, in_, func, reads, writes, bias=None, scale=None):
        if bias is eps_ap:
            reads = list(reads) + [epsb.k]
        _orig_act(out, in_, func, reads, writes, bias=bias, scale=scale)

    if cfg["samples"]:
        for s in range(2):
            outs = {
                "k": [ks_o[l, s] for l in range(DEPTH)], "v": [vs_o[l, s] for l in range(DEPTH)],
                "gla": [glas[l, s] for l in range(DEPTH)], "lc": [lcs[l, s] for l in range(DEPTH)],
                "lh": [lhs_o[l, s] for l in range(DEPTH)], "fc": [fcs[l, s] for l in range(DEPTH)],
            }
            run_tile(s, DSEQ, DSEQ, 1, 32, 1, xs[s], ys[s], ropes, 0, PAST // 512, True, outs)
    npt = cfg["n_ptiles"]
    for ti in range(npt):
        t0 = ti * TTP
        outs = {
            "k": [kp[l, t0:t0 + TTP] for l in range(DEPTH)], "v": [vp[l, t0:t0 + TTP] for l in range(DEPTH)],
            "gla": [glap[l] for l in range(DEPTH)], "lc": [lcp[l] for l in range(DEPTH)],
            "lh": [lhp[l] for l in range(DEPTH)], "fc": [fcp[l] for l in range(DEPTH)],
        }
        run_tile('p', TTP, 128, 4, 64, 8, xp[t0:t0 + TTP], yp[t0:t0 + TTP], ropep[t0:t0 + TTP], t0, ti, ti == npt - 1, outs)

    S.op("sync", lambda e: e.nop(), reads=out_tiles)
    S.emit()
    return nc, S


def _rope_table(pos):
    half = 8
    inv = (np.float32(500000.0) ** (-np.arange(half, dtype=np.float32) / np.float32(half))).astype(np.float32)
    ang = (pos.astype(np.float32)[:, None] * inv[None, :]).astype(np.float32)
    cos = np.cos(ang).astype(np.float32)
    sin = np.sin(ang).astype(np.float32)
    return np.concatenate([np.tile(cos, (1, 8)), np.tile(sin, (1, 8))], axis=1).astype(np.float32)


_WNAMES = ["norm_mix", "w_in", "lambda_qk", "attn_subln", "w_gla_gate2", "b_gla_gate", "gla_norm", "lru_conv_w",
           "lru_conv_b", "lru_wa", "lru_ba", "lru_wx", "lru_bx", "lru_lambda", "w_branch_attn", "w_branch_gla",
           "w_branch_lru", "w_merge", "b_merge", "w_out", "norm_ffn", "w_ffn_gate", "ffn_conv_w", "ffn_conv_b",
           "w_ffn_up", "w_ffn_down", "norm_final"]


def kernel(**inp):
    cfg = dict(CFG)
    nc, S = build_program(cfg)
    f = lambda a: np.ascontiguousarray(np.asarray(a, dtype=np.float32))
    NLc = cfg['depth']
    wts = {n: (f(inp[n]) if n == 'norm_final' else f(np.asarray(inp[n])[:NLc])) for n in _WNAMES}
    consts = {
        "ropep": _rope_table(np.arange(SEQ)),
        "ropes": _rope_table(PAST + np.arange(DSEQ)),
        "cmask": (np.arange(TTP) % 64 != 0).astype(np.float32)[None, :],
        "tri": np.triu(np.ones((64, 64), np.float32)),
        "ident": np.eye(128, dtype=np.float32),
    }
    x_prompt = f(inp["x_prompt"])
    x_sample = f(inp["x_sample"])
    ckf = np.asarray(inp["cache_attn_k"], dtype=np.float32).reshape(DEPTH, 16, PAST, 512)
    cvf = np.asarray(inp["cache_attn_v"], dtype=np.float32).reshape(DEPTH, 16, PAST, 512)
    sgf = f(inp["state_gla"])
    slcf = f(inp["state_lru_conv"])
    slhf = f(inp["state_lru_h"])
    sfcf = f(inp["state_ffn_conv"])
    PCORES = [0, 1, 4, 5]
    zero_x = np.zeros((SEQ, D), np.float32)
    in_maps = []
    for c in range(8):
        s0 = 2 * c
        m = {
            "xp": x_prompt[PCORES.index(c)] if c in PCORES else zero_x,
            "xs": x_sample[s0:s0 + 2],
            "ck": np.ascontiguousarray(ckf[:, s0:s0 + 2]),
            "cv": np.ascontiguousarray(cvf[:, s0:s0 + 2]),
            "sg": np.ascontiguousarray(sgf[:, s0:s0 + 2]),
            "slc": np.ascontiguousarray(slcf[:, s0:s0 + 2]),
            "slh": np.ascontiguousarray(slhf[:, s0:s0 + 2]),
            "sfc": np.ascontiguousarray(sfcf[:, s0:s0 + 2]),
        }
        m.update(wts)
        m.update(consts)
        in_maps.append(m)
    res = run_bass_kernel_spmd(nc, in_maps, core_ids=list(range(8))).results
    B = 4
    y_prompt = np.stack([res[b]["yp"] for b in PCORES])
    k_p = np.stack([res[b]["kp"] for b in PCORES], axis=1).reshape(DEPTH, B, SEQ, 4, 2, 64)
    v_p = np.stack([res[b]["vp"] for b in PCORES], axis=1).reshape(DEPTH, B, SEQ, 4, 128)
    gla_p = np.stack([res[b]["glap"] for b in PCORES], axis=1)
    lc_p = np.stack([res[b]["lcp"] for b in PCORES], axis=1)
    lh_p = np.stack([res[b]["lhp"] for b in PCORES], axis=1)
    fc_p = np.stack([res[b]["fcp"] for b in PCORES], axis=1)
    y_sample = np.concatenate([res[c]["ys"] for c in range(8)], axis=0)
    k_s = np.concatenate([res[c]["ks"] for c in range(8)], axis=1).reshape(DEPTH, 16, DSEQ, 4, 2, 64)
    v_s = np.concatenate([res[c]["vs"] for c in range(8)], axis=1).reshape(DEPTH, 16, DSEQ, 4, 128)
    gla_s = np.concatenate([res[c]["glas"] for c in range(8)], axis=1)
    lc_s = np.concatenate([res[c]["lcs"] for c in range(8)], axis=1)
    lh_s = np.concatenate([res[c]["lhs"] for c in range(8)], axis=1)
    fc_s = np.concatenate([res[c]["fcs"] for c in range(8)], axis=1)
    outs = (y_prompt, y_sample, k_p, v_p, gla_p, lc_p, lh_p, fc_p, k_s, v_s, gla_s, lc_s, lh_s, fc_s)
    return tuple(np.ascontiguousarray(o, dtype=np.float32) for o in outs)
```

```python
import math
from contextlib import ExitStack

import numpy as np
import concourse.bass as bass
import concourse.mybir as mybir
from concourse.bass_utils import run_bass_kernel_spmd

F32 = mybir.dt.float32
BF16 = mybir.dt.bfloat16
AF = mybir.ActivationFunctionType
ALU = mybir.AluOpType

ENGS = ["tensor", "vector", "scalar", "gpsimd", "sync"]

CFG = {"n_ptiles": 16, "depth": 4, "samples": True}

D = 1024
DEPTH = 4
SEQ = 8192
TTP = 512
PAST = 2048
DSEQ = 32
EPS = 1e-6
AQ, AK, AV, GQ, GK, GV, GR, GA, LX, LG, PW = 0, 512, 1024, 1536, 1792, 2048, 2560, 3072, 3088, 3600, 4112
DFF = 2816
NFC = 22
WSLOT = 4224


class Tile:
    __slots__ = ("name", "last_w", "readers", "psum")

    def __init__(self, name):
        self.name = name
        self.last_w = None
        self.readers = []
        self.psum = False


class Op:
    __slots__ = ("eng", "fn", "deps", "dma", "signal", "sem", "val", "idx", "phase")


class Sched:
    RING = 12

    def __init__(self, nc):
        self.nc = nc
        self.ops = []
        self.stack = ExitStack()
        self.ntile = 0
        self.phase = 'init'

    def sbuf(self, name, shape, dtype):
        return self.stack.enter_context(self.nc.sbuf_tensor("sb_" + name, list(shape), dtype))

    def psum(self, name, shape, dtype):
        return self.stack.enter_context(self.nc.psum_tensor("pp_" + name, list(shape), dtype))

    def tile(self, name=None):
        self.ntile += 1
        return Tile(name or f"t{self.ntile}")

    def op(self, eng, fn, reads=(), writes=(), dma=False):
        o = Op()
        o.eng = eng
        o.fn = fn
        o.dma = dma
        o.signal = False
        o.sem = None
        o.val = 0
        o.idx = len(self.ops)
        o.phase = self.phase
        deps = set()
        ops = self.ops
        for t in reads:
            if t.last_w is not None:
                deps.add(t.last_w)
            if t.psum:
                for r in t.readers:
                    if ops[r].eng != eng:
                        deps.add(r)
        for t in writes:
            if t.last_w is not None:
                deps.add(t.last_w)
            for r in t.readers:
                ro = ops[r]
                if (not dma) and (not ro.dma) and ro.eng == eng:
                    continue
                deps.add(r)
        if eng == "tensor" and not dma:
            deps = {d for d in deps if ops[d].dma or ops[d].eng != "tensor"}
        o.deps = deps
        for t in reads:
            if not dma:
                t.readers = [r for r in t.readers if ops[r].dma or ops[r].eng != eng]
            t.readers.append(o.idx)
        for t in writes:
            t.last_w = o.idx
            t.readers = []
        ops.append(o)
        return o

    def emit(self):
        nc = self.nc
        ops = self.ops
        stack = self.stack
        dma_count = {e: 0 for e in ENGS}
        dma_hist = {e: [] for e in ENGS}
        for o in ops:
            if o.dma:
                j = dma_count[o.eng]
                if j >= self.RING:
                    o.deps.add(dma_hist[o.eng][j - self.RING])
                dma_hist[o.eng].append(o.idx)
                dma_count[o.eng] += 1
        for o in ops:
            for d in o.deps:
                ops[d].signal = True
        eng_sem = {e: stack.enter_context(nc.semaphore(f"s_{e}")) for e in ENGS}
        ring_sems = {e: [stack.enter_context(nc.semaphore(f"d_{e}_{i}")) for i in range(self.RING)]
                     for e in ENGS if dma_count[e] > 0}
        cnt = {e: 0 for e in ENGS}
        dj = {e: 0 for e in ENGS}
        for o in ops:
            if o.dma:
                j = dj[o.eng]
                o.sem = ring_sems[o.eng][j % self.RING]
                o.val = 16 * (j // self.RING + 1)
                dj[o.eng] += 1
            elif o.signal:
                cnt[o.eng] += 1
                o.sem = eng_sem[o.eng]
                o.val = cnt[o.eng]
        per_eng = {e: [] for e in ENGS}
        for o in ops:
            per_eng[o.eng].append(o)
        self.stats = {e: len(per_eng[e]) for e in ENGS}
        block = stack.enter_context(nc.Block())

        def make(ename):
            def body(eng):
                waited = {}
                for o in per_eng[ename]:
                    need = {}
                    for d in o.deps:
                        do = ops[d]
                        k = id(do.sem)
                        if k not in need or need[k][1] < do.val:
                            need[k] = (do.sem, do.val)
                    for k, (sem, val) in need.items():
                        if waited.get(k, 0) >= val:
                            continue
                        eng.wait_ge(sem, val)
                        waited[k] = val
                    ins = o.fn(eng)
                    if o.dma:
                        ins.then_inc(o.sem, 16)
                    elif o.signal:
                        ins.then_inc(o.sem, 1)
            return body

        for e in ENGS:
            if per_eng[e]:
                getattr(block, e)(make(e))
        stack.close()


class Buf:
    __slots__ = ("t", "k")

    def __init__(self, t, k):
        self.t = t
        self.k = k


def build_program(cfg):
    nc = bass.Bass("TRN2", target_bir_lowering=False)
    S = Sched(nc)
    NL = cfg["depth"]

    def din(name, shape, dt=F32):
        return nc.dram_tensor(name, list(shape), dt, kind="ExternalInput").ap()

    def dout(name, shape, dt=F32):
        return nc.dram_tensor(name, list(shape), dt, kind="ExternalOutput").ap()

    def dscr(name, shape, dt=BF16):
        return nc.dram_tensor(name, list(shape), dt).ap()

    xp = din("xp", [SEQ, D])
    xs = din("xs", [2, DSEQ, D])
    ck = din("ck", [DEPTH, 2, PAST, 512])
    cv = din("cv", [DEPTH, 2, PAST, 512])
    sg = din("sg", [DEPTH, 2, 4, 64, 128])
    slc = din("slc", [DEPTH, 2, 3, 512])
    slh = din("slh", [DEPTH, 2, 512])
    sfc = din("sfc", [DEPTH, 2, 2, DFF])
    W = {}
    for name, shape in [
        ("norm_mix", [NL, D]), ("w_in", [NL, D, PW]), ("lambda_qk", [NL, 4, 64]),
        ("attn_subln", [NL, 128]), ("w_gla_gate2", [NL, 16, 256]), ("b_gla_gate", [NL, 256]),
        ("gla_norm", [NL, 128]), ("lru_conv_w", [NL, 4, 512]), ("lru_conv_b", [NL, 512]),
        ("lru_wa", [NL, 8, 64, 64]), ("lru_ba", [NL, 512]), ("lru_wx", [NL, 8, 64, 64]),
        ("lru_bx", [NL, 512]), ("lru_lambda", [NL, 512]), ("w_branch_attn", [NL, 512, D]),
        ("w_branch_gla", [NL, 512, D]), ("w_branch_lru", [NL, 512, D]), ("w_merge", [NL, D, 3 * D]),
        ("b_merge", [NL, 3 * D]), ("w_out", [NL, D, D]), ("norm_ffn", [NL, D]),
        ("w_ffn_gate", [NL, D, DFF]), ("ffn_conv_w", [NL, 3, DFF]), ("ffn_conv_b", [NL, DFF]),
        ("w_ffn_up", [NL, D, DFF]), ("w_ffn_down", [NL, DFF, D]), ("norm_final", [D]),
    ]:
        W[name] = din(name, shape)
    ropep = din("ropep", [SEQ, 128])
    ropes = din("ropes", [DSEQ, 128])
    cmask_d = din("cmask", [1, TTP])
    tri_d = din("tri", [64, 64])
    ident_d = din("ident", [128, 128])

    yp = dout("yp", [SEQ, D])
    ys = dout("ys", [2, DSEQ, D])
    kp = dout("kp", [DEPTH, SEQ, 512])
    vp = dout("vp", [DEPTH, SEQ, 512])
    glap = dout("glap", [DEPTH, 4, 64, 128])
    lcp = dout("lcp", [DEPTH, 3, 512])
    lhp = dout("lhp", [DEPTH, 512])
    fcp = dout("fcp", [DEPTH, 2, DFF])
    ks_o = dout("ks", [DEPTH, 2, DSEQ, 512])
    vs_o = dout("vs", [DEPTH, 2, DSEQ, 512])
    glas = dout("glas", [DEPTH, 2, 4, 64, 128])
    lcs = dout("lcs", [DEPTH, 2, 3, 512])
    lhs_o = dout("lhs", [DEPTH, 2, 512])
    fcs = dout("fcs", [DEPTH, 2, 2, DFF])
    out_tiles = []

    def otile():
        t = S.tile()
        out_tiles.append(t)
        return t

    ktscr = dscr("ktscr", [DEPTH, 4, 128, SEQ])
    vscr = dscr("vscr", [DEPTH, 4, SEQ // 512, 128, 512])
    kt_t = [[[S.tile() for u in range(16)] for h in range(4)] for l in range(DEPTH)]
    v_t = [[[S.tile() for u in range(16)] for h in range(4)] for l in range(DEPTH)]

    units = {}

    def mk(name, shape, dt):
        return Buf(S.sbuf(name, shape, dt), S.tile(name))

    xT = mk("xT", [128, 8, TTP], F32)
    NP = 16
    pool_t = S.sbuf("pool", [128, NP, TTP], F32)
    pool_k = [S.tile(f"pool{i}") for i in range(NP)]
    pool_i = [0]

    class Tmp:
        __slots__ = ("f", "b", "k")

    def tmp():
        i = pool_i[0] % NP
        pool_i[0] += 1
        r = Tmp()
        r.f = pool_t[:, i, :]
        r.b = pool_t[:, i, :].bitcast(BF16)
        r.k = pool_k[i]
        return r

    def tmp8():
        if (pool_i[0] % NP) + 8 > NP:
            pool_i[0] += NP - (pool_i[0] % NP)
        base = pool_i[0] % NP
        ks_ = [tmp().k for _ in range(8)]
        return base, ks_

    xn = mk("xn", [128, 8, TTP], BF16)
    NWR = 5
    wring = [mk(f"wring{i}", [128, WSLOT], BF16) for i in range(NWR)]
    wr_i = [0]
    QT = mk("QT", [128, 4, TTP], BF16)
    KTc = mk("KTc", [128, 4, TTP], BF16)
    qtl = mk("qtl", [64, 4, TTP], BF16)
    ktl = mk("ktl", [64, 4, TTP], BF16)
    v_bf = mk("v_bf", [128, 4, 512], BF16)
    gv_tok = mk("gv_tok", [64, 8, 512], BF16)
    ktok = mk("ktok", [64, 8, 256], BF16)
    fT = [Buf(None, S.tile(f"fT{j}")) for j in range(NFC)]
    fT_t = S.sbuf("fT", [128, NFC, TTP], BF16)
    mergedb = Buf(fT_t[:, 0:8, :], S.tile("mergedb"))
    oattn = Buf(fT_t[:, 8:12, :], S.tile("oattn"))
    gon = Buf(fT_t[:, 12:16, :], S.tile("gon"))
    hl = Buf(fT_t[:, 16:20, :], S.tile("hl"))
    for j in range(NFC):
        if j < 8:
            fT[j].k = mergedb.k
        elif j < 12:
            fT[j].k = oattn.k
        elif j < 16:
            fT[j].k = gon.k
        elif j < 20:
            fT[j].k = hl.k
    pring = [mk(f"pring{i}", [128, TTP], BF16) for i in range(4)]
    pr_i = [0]
    kring = [mk(f"kring{i}", [128, 512], BF16) for i in range(3)]
    vring = [mk(f"vring{i}", [128, 4, 128], BF16) for i in range(3)]
    kv_i = [0]
    kst = [mk(f"kst{i}", [128, 4, 128], BF16) for i in range(1)]
    kst_i = [0]
    gS = [mk(f"gS{l}", [64, 4, 128], F32) for l in range(DEPTH)]
    gSb = [mk(f"gSb{l}", [64, 4, 128], BF16) for l in range(DEPTH)]
    gSk = [[S.tile(f"gSk{l}_{h}") for h in range(4)] for l in range(DEPTH)]
    gSbk = [[S.tile(f"gSbk{l}_{h}") for h in range(4)] for l in range(DEPTH)]
    hst = [mk(f"hst{l}", [128, 4], F32) for l in range(DEPTH)]
    lxh = [mk(f"lxh{l}", [128, 4, 3], F32) for l in range(DEPTH)]
    guh = [mk(f"guh{l}", [128, NFC, 2], F32) for l in range(DEPTH)]
    parA = [mk(f"parA{l}", [128, 128], F32) for l in range(DEPTH)]
    parB = [mk(f"parB{l}", [128, 64], F32) for l in range(DEPTH)]
    lamc = [mk(f"lamc{l}", [128, 8], F32) for l in range(DEPTH)]
    w2b = [mk(f"w2b{l}", [16, 256], BF16) for l in range(DEPTH)]
    identf = mk("identf", [128, 128], F32)
    identb = mk("identb", [128, 128], BF16)
    onesb = mk("onesb", [128, 128], BF16)
    tri = mk("tri", [64, 64], F32)
    cmask = mk("cmask_sb", [128, TTP], F32)
    rope_sb = mk("rope_sb", [128, 4, 128], F32)
    lxT = mk("lxT", [128, 4, 3 + TTP], F32)
    gcs = mk("gcs", [64, 4, TTP], F32)
    gab = mk("gab", [16, TTP], BF16)
    ebl = mk("ebl", [64, 4, 8], F32)

    PS = [Buf(S.psum(f"ps{i}", [128, 512], F32), S.tile(f"ps{i}")) for i in range(8)]
    for b_ in PS:
        b_.k.psum = True
    ps_i = [0]

    def psn():
        i = ps_i[0] % 8
        ps_i[0] += 1
        return PS[i]

    def dma(out, in_, reads, writes, eng="sync", **kw):
        S.op(eng, lambda e: e.dma_start(out=out, in_=in_, **kw), reads=reads, writes=writes, dma=True)

    def mm(out, lhsT, rhs, start, stop, reads, writes):
        S.op("tensor", lambda e: e.matmul(out, lhsT=lhsT, rhs=rhs, start=start, stop=stop), reads=reads, writes=writes)

    def tr(out, in_, ident, reads, writes):
        S.op("tensor", lambda e: e.transpose(out=out, in_=in_, identity=ident), reads=reads, writes=writes)

    def act(out, in_, func, reads, writes, bias=None, scale=None):
        kw = {}
        if bias is not None:
            kw["bias"] = bias
        if scale is not None:
            kw["scale"] = scale
        S.op("scalar", lambda e: e.activation(out=out, in_=in_, func=func, **kw), reads=reads, writes=writes)

    def tt(out, in0, in1, op, reads, writes, eng="vector"):
        S.op(eng, lambda e: e.tensor_tensor(out=out, in0=in0, in1=in1, op=op), reads=reads, writes=writes)

    def ts(out, in0, s1, s2, op0, op1, reads, writes, eng="vector"):
        if op1 is None:
            S.op(eng, lambda e: e.tensor_scalar(out=out, in0=in0, scalar1=s1, scalar2=None, op0=op0), reads=reads, writes=writes)
        else:
            S.op(eng, lambda e: e.tensor_scalar(out=out, in0=in0, scalar1=s1, scalar2=s2, op0=op0, op1=op1), reads=reads, writes=writes)

    def stt(out, in0, scalar, in1, op0, op1, reads, writes):
        S.op("vector", lambda e: e.scalar_tensor_tensor(out=out, in0=in0, scalar=scalar, in1=in1, op0=op0, op1=op1), reads=reads, writes=writes)

    def cp(out, in_, reads, writes, eng="vector"):
        S.op(eng, lambda e: e.tensor_copy(out=out, in_=in_), reads=reads, writes=writes)

    def mset(ap, val, reads, writes, eng="gpsimd"):
        S.op(eng, lambda e: e.memset(ap, val), reads=reads, writes=writes)

    def scan(out, d0, d1, init, reads, writes):
        S.op("vector", lambda e: e.tensor_tensor_scan(out=out, data0=d0, data1=d1, initial=init, op0=ALU.mult, op1=ALU.add), reads=reads, writes=writes)

    def recip(out, in_, reads, writes):
        S.op("vector", lambda e: e.reciprocal(out=out, in_=in_), reads=reads, writes=writes)

    ev_i = [0]

    def evac(out, in_, reads, writes):
        ev_i[0] += 1
        if ev_i[0] % 2:
            act(out, in_, AF.Copy, reads, writes)
        else:
            cp(out, in_, reads, writes)

    dma(identf.t[:], ident_d[:, :], [], [identf.k])
    cp(identb.t[:], identf.t[:], [identf.k], [identb.k])
    mset(onesb.t[:], 1.0, [], [onesb.k])
    dma(tri.t[:], tri_d[:, :], [], [tri.k])
    dma(cmask.t[:], cmask_d.rearrange("a b -> (a b)").partition_broadcast(128), [], [cmask.k])

    def wview(name, l):
        return W[name][l].rearrange("(kc p) n -> p kc n", p=128)

    cast_done = set()

    def cast_layer(l):
        if l in cast_done or l >= NL:
            return
        cast_done.add(l)
        zt = tmp()
        mset(zt.f, 0.0, [], [zt.k])
        ul = {}

        def unit(name, width):
            ap = dscr(f"wu_{l}_{name}", [128, width])
            k = S.tile(f"wu_{l}_{name}")
            ul[name] = (ap, k, width)
            return ap, k

        win = wview("w_in", l)
        for name, c0, c1 in [("U0", 0, 512), ("U1", 512, 1024), ("U2", 1024, 1536), ("U3", 1536, 2048),
                             ("U4", 2048, 2560), ("U5", 2560, 3072), ("U6", 3072, 3600), ("U7", 3600, 4112)]:
            w = c1 - c0
            ap, k = unit(name, 8 * w)
            dma(ap.rearrange("p (kc n) -> p kc n", kc=8), win[:, :, c0:c1], [], [k], eng="gpsimd")
        ap, k = unit("U8", 8 * 128)
        dma(ap[:, :], zt.b[:, 0:1024], [zt.k], [k], eng="gpsimd")
        a3 = ap.rearrange("p (c n) -> p c n", c=8)
        for gi, wn in enumerate(["lru_wa", "lru_wx"]):
            for c in range(4):
                for half in range(2):
                    dma(a3[half * 64:(half + 1) * 64, gi * 4 + c, half * 64:(half + 1) * 64],
                        W[wn][l, 2 * c + half], [], [k], eng="gpsimd")
        wm = W["w_merge"][l].rearrange("(kc p) (j m f) -> p kc j m f", p=128, j=3, m=8)
        wbr = [W[n][l].rearrange("(kc p) (m f) -> p kc m f", p=128, m=8) for n in ("w_branch_attn", "w_branch_gla", "w_branch_lru")]
        for m in range(8):
            ap, k = unit(f"M{m}", 3072)
            a_m = ap[:, 0:3072].rearrange("p (kc j f) -> p kc j f", kc=8, j=3)
            for j in range(3):
                dma(a_m[:, :, j, :], wm[:, :, j, m, :], [], [k], eng="gpsimd")
            ap, k = unit(f"B{m}", 1536)
            a_b = ap[:, 0:1536].rearrange("p (kc j f) -> p kc j f", kc=4, j=3)
            for j in range(3):
                dma(a_b[:, :, j, :], wbr[j][:, :, m, :], [], [k], eng="gpsimd")
        wo = wview("w_out", l)
        for i in range(2):
            ap, k = unit(f"O{i}", 4096)
            dma(ap.rearrange("p (kc n) -> p kc n", kc=8), wo[:, :, i * 512:(i + 1) * 512], [], [k], eng="gpsimd")
        wg = wview("w_ffn_gate", l)
        wu = wview("w_ffn_up", l)
        for i in range(11):
            ap, k = unit(f"F{i}", 4096)
            a4 = ap.rearrange("p (kc g n) -> p kc g n", kc=8, g=2)
            dma(a4[:, :, 0, :], wg[:, :, i * 256:(i + 1) * 256], [], [k], eng="gpsimd")
            dma(a4[:, :, 1, :], wu[:, :, i * 256:(i + 1) * 256], [], [k], eng="gpsimd")
        wd = wview("w_ffn_down", l)
        for m in range(8):
            ap, k = unit(f"D{m}", NFC * 128)
            dma(ap.rearrange("p (kc n) -> p kc n", kc=NFC), wd[:, :, m * 128:(m + 1) * 128], [], [k], eng="gpsimd")
        units[l] = ul


    cast_layer(0)

    for l in range(NL):
        st = tmp()
        st_t = st.f[:, 0:128]
        mset(st_t, 0.0, [], [st.k])
        rows = [
            (0, 66, W["ffn_conv_w"][l].rearrange("j (c f) -> (j c) f", f=128)),
            (66, 22, W["ffn_conv_b"][l].rearrange("(c f) -> c f", f=128)),
            (88, 24, W["b_merge"][l].rearrange("(c f) -> c f", f=128)),
            (112, 8, W["norm_mix"][l].rearrange("(c f) -> c f", f=128)),
            (120, 8, W["norm_ffn"][l].rearrange("(c f) -> c f", f=128)),
        ]
        for r0, n, src in rows:
            dma(st_t[r0:r0 + n, :], src, [], [st.k])
        p = psn()
        tr(p.t[:, 0:128], st_t, identf.t[:], [st.k, identf.k], [p.k])
        cp(parA[l].t[:], p.t[:, 0:128], [p.k], [parA[l].k])
        st = tmp()
        st_t = st.f[:, 0:128]
        mset(st_t, 0.0, [], [st.k])
        rows = [
            (0, 16, W["lru_conv_w"][l].rearrange("j (c f) -> (j c) f", f=128), 128),
            (16, 4, W["lru_conv_b"][l].rearrange("(c f) -> c f", f=128), 128),
            (20, 4, W["lru_ba"][l].rearrange("(c f) -> c f", f=128), 128),
            (24, 4, W["lru_bx"][l].rearrange("(c f) -> c f", f=128), 128),
            (28, 4, W["lru_lambda"][l].rearrange("(c f) -> c f", f=128), 128),
            (32, 1, W["attn_subln"][l].rearrange("(c f) -> c f", f=128), 128),
            (33, 1, W["gla_norm"][l].rearrange("(c f) -> c f", f=128), 128),
            (34, 4, W["b_gla_gate"][l].rearrange("(c f) -> c f", f=64), 64),
            (38, 8, W["norm_final"].rearrange("(c f) -> c f", f=128), 128),
        ]
        for r0, n, src, wd_ in rows:
            dma(st_t[r0:r0 + n, 0:wd_], src, [], [st.k])
        p = psn()
        tr(p.t[:, 0:128], st_t, identf.t[:], [st.k, identf.k], [p.k])
        cp(parB[l].t[:], p.t[:, 0:64], [p.k], [parB[l].k])
        lam_init = 0.8 - 0.6 * math.exp(-0.3 * l)
        lq = tmp()
        dma(lq.f[:, 0:256], W["lambda_qk"][l].rearrange("a b -> (a b)").partition_broadcast(128), [], [lq.k])
        t1 = tmp()
        tt(t1.f[:, 0:64], lq.f[:, 0:64], lq.f[:, 64:128], ALU.mult, [lq.k], [t1.k])
        tt(t1.f[:, 64:128], lq.f[:, 128:192], lq.f[:, 192:256], ALU.mult, [lq.k, t1.k], [t1.k])
        S.op("vector", lambda e, t1=t1: e.tensor_reduce(out=t1.f[:, 128:130], in_=t1.f[:, 0:128].rearrange("p (a b) -> p a b", a=2), axis=mybir.AxisListType.X, op=ALU.add), reads=[t1.k], writes=[t1.k])
        act(t1.f[:, 130:132], t1.f[:, 128:130], AF.Exp, [t1.k], [t1.k])
        tt(t1.f[:, 132:133], t1.f[:, 131:132], t1.f[:, 130:131], ALU.subtract, [t1.k], [t1.k])
        ts(lamc[l].t[:, 0:1], t1.f[:, 132:133], -lam_init, None, ALU.add, None, [t1.k], [lamc[l].k])
        ts(lamc[l].t[:, 1:2], parB[l].t[:, 32:33], 1.0 - lam_init, None, ALU.mult, None, [parB[l].k, lamc[l].k], [lamc[l].k])
        act(t1.f[:, 140:144], parB[l].t[:, 28:32], AF.Exp, [parB[l].k, t1.k], [t1.k], scale=-1.0)
        act(t1.f[:, 144:148], t1.f[:, 140:144], AF.Ln, [t1.k], [t1.k], bias=1.0)
        ts(lamc[l].t[:, 2:6], t1.f[:, 144:148], -8.0, None, ALU.mult, None, [t1.k, lamc[l].k], [lamc[l].k])
        ts(parB[l].t[:, 46:50], parB[l].t[:, 34:38], -1.0, None, ALU.mult, None, [parB[l].k], [parB[l].k])
        t2 = tmp()
        dma(t2.f[0:16, 0:256], W["w_gla_gate2"][l], [], [t2.k])
        cp(w2b[l].t[:], t2.f[0:16, 0:256], [t2.k], [w2b[l].k])

    def wload(l, name):
        ap, k, width = units[l][name]
        slot = wring[wr_i[0] % NWR]
        wr_i[0] += 1
        dma(slot.t[:, 0:width], ap[:, :], [k], [slot.k])
        return slot

    def run_tile(seq, TT, BP, NB, L, NCK, xsrc, ysrc, rope_src, t0, units_past, is_last, outs):
        S.phase = 'xload'
        base, xk = tmp8()

        def xtok(bi):
            return pool_t[0:BP, base + 2 * bi: base + 2 * bi + 2, :].rearrange("p a n -> p (a n)")

        for bi in range(NB):
            dma(xtok(bi), xsrc[bi * BP:(bi + 1) * BP, :], [], [xk[2 * bi], xk[2 * bi + 1]])
        dma(rope_sb.t[0:BP, 0:NB, :], rope_src.rearrange("(b p) f -> p b f", p=BP), [], [rope_sb.k])
        for c in range(8):
            p = psn()
            for bi in range(NB):
                tr(p.t[:, bi * BP:(bi + 1) * BP], xtok(bi)[:, c * 128:(c + 1) * 128], identf.t[0:BP, 0:BP],
                   [xk[2 * bi], xk[2 * bi + 1], identf.k], [p.k])
            evac(xT.t[:, c, 0:TT], p.t[:, 0:TT], [p.k], [xT.k])

        def rmsnorm_to(dst, gcol):
            p = psn()
            for c in range(8):
                sq = tmp()
                act(sq.b[:, 0:TT], xT.t[:, c, 0:TT], AF.Square, [xT.k], [sq.k])
                mm(p.t[:, 0:TT], onesb.t[:, :], sq.b[:, 0:TT], c == 0, c == 7, [onesb.k, sq.k], [p.k])
            rs = tmp()
            act(rs.f[:, 0:TT], p.t[:, 0:TT], AF.Ln, [p.k], [rs.k], bias=eps_ap, scale=1.0 / D)
            act(rs.f[:, 0:TT], rs.f[:, 0:TT], AF.Exp, [rs.k], [rs.k], scale=-0.5)
            for c in range(8):
                stt(dst.t[:, c, 0:TT], xT.t[:, c, 0:TT], gcol(c), rs.f[:, 0:TT], ALU.mult, ALU.mult, [xT.k, rs.k], [dst.k])

        for l in range(NL):
            cast_layer(l + 1)
            pA, pB = parA[l], parB[l]
            if t0 == 0:
                if seq == 'p':
                    mset(gS[l].t[:], 0.0, [], gSk[l])
                    mset(gSb[l].t[:], 0.0, [], gSbk[l])
                    mset(hst[l].t[:], 0.0, [], [hst[l].k])
                    mset(lxh[l].t[:], 0.0, [], [lxh[l].k])
                    mset(guh[l].t[:], 0.0, [], [guh[l].k])
                else:
                    dma(gS[l].t[:], sg[l, seq].rearrange("h k v -> k h v"), [], gSk[l])
                    cp(gSb[l].t[:], gS[l].t[:], gSk[l], gSbk[l], eng="gpsimd")
                    dma(hst[l].t[:], slh[l, seq].rearrange("(c p) -> p c", p=128), [], [hst[l].k], allow_slow_non_contiguous=True)
                    for j in range(3):
                        dma(lxh[l].t[:, :, j], slc[l, seq, j].rearrange("(c p) -> p c", p=128), [], [lxh[l].k], allow_slow_non_contiguous=True)
                    for j in range(2):
                        dma(guh[l].t[:, :, j], sfc[l, seq, j].rearrange("(c p) -> p c", p=128), [], [guh[l].k], allow_slow_non_contiguous=True)

            S.phase = 'norm1'
            rmsnorm_to(xn, lambda c: pA.t[:, 112 + c:113 + c])
            xn_r = [xn.k, pA.k]

            S.phase = 'qkv'
            for ui, uname in enumerate(["U0", "U1", "U2"]):
                ws = wload(l, uname)
                w3 = ws.t[:, 0:4096].rearrange("p (kc n) -> p kc n", kc=8)
                tbs = []
                for bi in range(NB):
                    p = psn()
                    for kc in range(8):
                        mm(p.t[0:BP, :], xn.t[:, kc, bi * BP:(bi + 1) * BP], w3[:, kc, :], kc == 0, kc == 7, [xn.k, ws.k], [p.k])
                    tk = tmp()
                    act(tk.f[0:BP, :], p.t[0:BP, :], AF.Copy, [p.k], [tk.k])
                    if ui < 2:
                        x4 = tk.f[0:BP, :].rearrange("p (a d) -> p a d", d=64)
                        cs4 = rope_sb.t[0:BP, bi, 0:64].rearrange("p (a d) -> p a d", d=8)
                        sn4 = rope_sb.t[0:BP, bi, 64:128].rearrange("p (a d) -> p a d", d=8)
                        tq = tmp()
                        q4 = tq.f[0:BP, 0:256].rearrange("p (j a d) -> p j a d", j=4, d=8)
                        rk = [tk.k, rope_sb.k, tq.k]
                        tt(q4[:, 0], x4[:, :, 0:8], cs4, ALU.mult, rk, [tq.k])
                        tt(q4[:, 1], x4[:, :, 8:16], sn4, ALU.mult, rk, [tq.k])
                        tt(q4[:, 2], x4[:, :, 8:16], cs4, ALU.mult, rk, [tq.k])
                        tt(q4[:, 3], x4[:, :, 0:8], sn4, ALU.mult, rk, [tq.k])
                        tt(x4[:, :, 0:8], q4[:, 0], q4[:, 1], ALU.subtract, [tq.k, tk.k], [tk.k])
                        tt(x4[:, :, 8:16], q4[:, 2], q4[:, 3], ALU.add, [tq.k, tk.k], [tk.k])
                        tb = tmp()
                        cp(tb.b[0:BP, 0:512], tk.f[0:BP, :], [tk.k], [tb.k])
                        tbs.append(tb)
                        if ui == 1:
                            ot = otile()
                            dma(outs["k"][l][bi * BP:(bi + 1) * BP, :], tk.f[0:BP, :], [tk.k], [ot], eng="gpsimd")
                    else:
                        ot = otile()
                        dma(outs["v"][l][bi * BP:(bi + 1) * BP, :], tk.f[0:BP, :], [tk.k], [ot], eng="gpsimd")
                        cp(v_bf.t[0:BP, bi, :], tk.f[0:BP, :], [tk.k], [v_bf.k], eng="gpsimd")
                for bi, tb in enumerate(tbs):
                    pt = psn()
                    ptb = pt.t[:].bitcast(BF16)
                    for hc in range(4):
                        tr(ptb[:, hc * BP:(hc + 1) * BP], tb.b[0:BP, hc * 128:(hc + 1) * 128], identb.t[0:BP, 0:BP], [tb.k, identb.k], [pt.k])
                    dst = QT if ui == 0 else KTc
                    evac(dst.t[:, :, bi * BP:(bi + 1) * BP], ptb[:, 0:4 * BP].rearrange("p (h t) -> p h t", h=4), [pt.k], [dst.k])
            if seq == 'p' and not is_last:
                u = t0 // 512
                dma(ktscr[l].rearrange("h p t -> p h t")[:, :, t0:t0 + TT], KTc.t[:, :, 0:TT], [KTc.k], [kt_t[l][h][u] for h in range(4)], eng="gpsimd")
                for h in range(4):
                    dma(vscr[l, h, u].rearrange("p (b e) -> p b e", b=4), v_bf.t[:, :, h * 128:(h + 1) * 128], [v_bf.k], [v_t[l][h][u]], eng="gpsimd")

            S.phase = 'gla_prep'
            ws = wload(l, "U6")
            w3 = ws.t[:, 0:8 * 528].rearrange("p (kc n) -> p kc n", kc=8)
            p = psn()
            for kc in range(8):
                mm(p.t[0:16, 0:TT], w3[:, kc, 0:16], xn.t[:, kc, 0:TT], kc == 0, kc == 7, [xn.k, ws.k], [p.k])
            evac(gab.t[0:16, 0:TT], p.t[0:16, 0:TT], [p.k], [gab.k])
            cp(lxT.t[:, :, 0:3], lxh[l].t[:, :, :], [lxh[l].k], [lxT.k], eng="gpsimd")
            for c in range(4):
                p = psn()
                for kc in range(8):
                    mm(p.t[:, 0:TT], w3[:, kc, 16 + c * 128:16 + (c + 1) * 128], xn.t[:, kc, 0:TT], kc == 0, kc == 7, [xn.k, ws.k], [p.k])
                evac(lxT.t[:, c, 3:3 + TT], p.t[:, 0:TT], [p.k], [lxT.k])
            cp(lxh[l].t[:, :, :], lxT.t[:, :, TT:TT + 3], [lxT.k], [lxh[l].k], eng="gpsimd")
            w8 = wload(l, "U8")
            w83 = w8.t[:, 0:1024].rearrange("p (c n) -> p c n", c=8)
            w7 = wload(l, "U7")
            w73 = w7.t[:, 0:4096].rearrange("p (kc n) -> p kc n", kc=8)


            S.phase = 'attn'
            acc = PS[0:4]
            sring = PS[4:8]
            sr_i = 0
            nkb_cur = (TT + 127) // 128
            pend_fin = None
            for h in range(4):
                S.phase = 'attn'

                nent = [0]

                def load_unit(u):
                    slot = kv_i[0] % 3
                    kv_i[0] += 1
                    kr, vr = kring[slot], vring[slot]
                    if seq == 'p':
                        dma(kr.t[:, :], ktscr[l, h, :, u * 512:(u + 1) * 512], [kt_t[l][h][u]], [kr.k])
                        dma(vr.t[:, :, :], vscr[l, h, u].rearrange("p (b e) -> p b e", b=4), [v_t[l][h][u]], [vr.k])
                    else:
                        ks_ = kst[0]
                        dma(ks_.t[:, :, :], ck[l, seq, u * 512:(u + 1) * 512, h * 128:(h + 1) * 128].rearrange("(b p) f -> p b f", p=128), [], [ks_.k], eng="gpsimd")
                        dma(vr.t[:, :, :], cv[l, seq, u * 512:(u + 1) * 512, h * 128:(h + 1) * 128].rearrange("(b p) f -> p b f", p=128), [], [vr.k], eng="gpsimd")
                        pt = sring[2 * (nent[0] % 2)]
                        ptb = pt.t[:].bitcast(BF16)
                        for b_ in range(4):
                            tr(ptb[:, b_ * 128:(b_ + 1) * 128], ks_.t[:, b_, :], identb.t[:, :], [ks_.k, identb.k], [pt.k])
                        evac(kr.t[:, :], ptb[:, 0:512], [pt.k], [kr.k])
                    return kr, vr

                loaded = {}

                def ensure(u):
                    if u < units_past and u not in loaded:
                        loaded[u] = load_unit(u)

                nblk = units_past * 4 + nkb_cur

                def gen_blocks():
                    for u in range(units_past):
                        ensure(u)
                        ensure(u + 1)
                        kr, vr = loaded.pop(u)
                        for kb in range(4):
                            yield (kr.t[:, kb * 128:(kb + 1) * 128], vr.t[:, kb, :], 128, None, [kr.k, vr.k])
                    for kb in range(nkb_cur):
                        nk = min(128, TT - kb * 128)
                        yield (KTc.t[:, h, kb * 128:kb * 128 + nk], v_bf.t[0:nk, kb, h * 128:(h + 1) * 128], nk,
                               kb if seq == 'p' else None, [KTc.k, v_bf.k])

                def s1a(blk):
                    ksrc, vsrc, nk, diag, rds = blk
                    par = nent[0] % 2
                    nent[0] += 1
                    sbs = []
                    for m in range(2):
                        sp_ = sring[2 * par + m]
                        mm(sp_.t[0:nk, 0:TT], ksrc[m * 64:(m + 1) * 64, :], QT.t[m * 64:(m + 1) * 64, h, 0:TT], True, True, rds + [QT.k], [sp_.k])
                        sbs.append(sp_)
                    return sbs

                def s1b(blk, sbs):
                    ksrc, vsrc, nk, diag, rds = blk
                    prs = []
                    for m in range(2):
                        sp_ = sbs[m]
                        pr = pring[pr_i[0] % 4]
                        pr_i[0] += 1
                        act(pr.t[0:nk, 0:TT], sp_.t[0:nk, 0:TT], AF.Exp, [sp_.k], [pr.k], scale=0.125)
                        if diag is not None:
                            if diag > 0:
                                mset(pr.t[0:nk, 0:diag * 128], 0.0, [pr.k], [pr.k])
                            if nk > 64:
                                mset(pr.t[64:nk, diag * 128:diag * 128 + 64], 0.0, [pr.k], [pr.k])
                        prs.append(pr)
                    return prs

                def stage2(blk, prs, bidx):
                    ksrc, vsrc, nk, diag, rds = blk
                    first = bidx == 0
                    last = bidx == nblk - 1
                    for m in range(2):
                        pr = prs[m]
                        mm(acc[m].t[:, 0:TT], vsrc, pr.t[0:nk, 0:TT], first, last, rds + [pr.k], [acc[m].k])
                        mm(acc[2 + m].t[:, 0:TT], onesb.t[0:nk, :], pr.t[0:nk, 0:TT], first, last, [onesb.k, pr.k], [acc[2 + m].k])

                def freebank():
                    return sring[2 * (nent[0] % 2)]

                ents = []
                bidx = 0
                for blk in gen_blocks():
                    ents.append([blk, s1a(blk), None])
                    n = len(ents)
                    if n >= 2:
                        e = ents[n - 2]
                        e[2] = s1b(e[0], e[1])
                        if n == 3 and pend_fin is not None:
                            pend_fin(freebank)
                            pend_fin = None
                            S.phase = 'attn'
                    if n >= 3:
                        e = ents[n - 3]
                        stage2(e[0], e[2], bidx)
                        bidx += 1
                n = len(ents)
                e = ents[n - 1]
                e[2] = s1b(e[0], e[1])
                if n >= 2:
                    e2 = ents[n - 2]
                    stage2(e2[0], e2[2], bidx)
                    bidx += 1
                stage2(e[0], e[2], bidx)
                if pend_fin is not None:
                    pend_fin(freebank)
                    pend_fin = None
                S.phase = 'attn_fin'
                r0, r1, t0_, t1_ = tmp(), tmp(), tmp(), tmp()
                act(r0.f[:, 0:TT], acc[2].t[:, 0:TT], AF.Ln, [acc[2].k], [r0.k])
                act(r1.f[:, 0:TT], acc[3].t[:, 0:TT], AF.Ln, [acc[3].k], [r1.k])
                act(r0.f[:, 0:TT], r0.f[:, 0:TT], AF.Exp, [r0.k], [r0.k], scale=-1.0)
                act(r1.f[:, 0:TT], r1.f[:, 0:TT], AF.Exp, [r1.k], [r1.k], scale=-1.0)
                tt(t0_.f[:, 0:TT], acc[0].t[:, 0:TT], r0.f[:, 0:TT], ALU.mult, [acc[0].k, r0.k], [t0_.k])
                tt(t1_.f[:, 0:TT], acc[1].t[:, 0:TT], r1.f[:, 0:TT], ALU.mult, [acc[1].k, r1.k], [t1_.k])
                stt(t0_.f[:, 0:TT], t1_.f[:, 0:TT], lamc[l].t[:, 0:1], t0_.f[:, 0:TT], ALU.mult, ALU.add, [t1_.k, t0_.k, lamc[l].k], [t0_.k])
                sq = tmp()
                act(sq.b[:, 0:TT], t0_.f[:, 0:TT], AF.Square, [t0_.k], [sq.k])

                def fin_b(bank, h=h, t0_=t0_, sq=sq):
                    S.phase = 'attn_fin'
                    pb_ = bank()
                    mm(pb_.t[:, 0:TT], onesb.t[:, :], sq.b[:, 0:TT], True, True, [onesb.k, sq.k], [pb_.k])
                    rs = tmp()
                    act(rs.f[:, 0:TT], pb_.t[:, 0:TT], AF.Ln, [pb_.k], [rs.k], bias=eps_ap, scale=1.0 / 128)
                    act(rs.f[:, 0:TT], rs.f[:, 0:TT], AF.Exp, [rs.k], [rs.k], scale=-0.5)
                    stt(oattn.t[:, h, 0:TT], t0_.f[:, 0:TT], lamc[l].t[:, 1:2], rs.f[:, 0:TT], ALU.mult, ALU.mult, [t0_.k, rs.k, lamc[l].k], [oattn.k])

                pend_fin = fin_b
            pend_fin(lambda: sring[0])
            pend_fin = None

            S.phase = 'gla_prep'
            ws = wload(l, "U4")
            w3 = ws.t[:, 0:4096].rearrange("p (kc n) -> p kc n", kc=8)
            for ck_ in range(NCK):
                p = psn()
                for kc in range(8):
                    mm(p.t[0:L, :], xn.t[:, kc, ck_ * L:(ck_ + 1) * L], w3[:, kc, :], kc == 0, kc == 7, [xn.k, ws.k], [p.k])
                evac(gv_tok.t[0:L, ck_, :], p.t[0:L, :], [p.k], [gv_tok.k])
            for h in range(4):
                p = psn()
                mm(p.t[0:64, 0:TT], w2b[l].t[0:16, h * 64:(h + 1) * 64], gab.t[0:16, 0:TT], True, True, [w2b[l].k, gab.k], [p.k])
                e1 = tmp()
                act(e1.f[0:64, 0:TT], p.t[0:64, 0:TT], AF.Exp, [p.k, pB.k], [e1.k], bias=pB.t[0:64, 46 + h:47 + h], scale=-1.0)
                act(e1.f[0:64, 0:TT], e1.f[0:64, 0:TT], AF.Ln, [e1.k], [e1.k], bias=1.0)
                scan(gcs.t[:, h, 0:TT], cmask.t[0:64, 0:TT], e1.f[0:64, 0:TT], 0.0, [cmask.k, e1.k], [gcs.k])
            ws = wload(l, "U3")
            w3 = ws.t[:, 0:4096].rearrange("p (kc n) -> p kc n", kc=8)
            for h in range(4):
                p = psn()
                for kc in range(8):
                    mm(p.t[0:64, 0:TT], w3[:, kc, h * 64:(h + 1) * 64], xn.t[:, kc, 0:TT], kc == 0, kc == 7, [xn.k, ws.k], [p.k])
                eq = tmp()
                act(eq.f[0:64, 0:TT], gcs.t[:, h, 0:TT], AF.Exp, [gcs.k], [eq.k], scale=-1.0 / 16)
                stt(qtl.t[:, h, 0:TT], p.t[0:64, 0:TT], 0.125, eq.f[0:64, 0:TT], ALU.mult, ALU.mult, [p.k, eq.k], [qtl.k])
                cp(ebl.t[:, h, 0:NCK], eq.f[0:64, 0:TT].rearrange("p (c t) -> p c t", t=L)[:, :, L - 1], [eq.k], [ebl.k])
                p = psn()
                for kc in range(8):
                    mm(p.t[0:64, 0:TT], w3[:, kc, 256 + h * 64:256 + (h + 1) * 64], xn.t[:, kc, 0:TT], kc == 0, kc == 7, [xn.k, ws.k], [p.k])
                ek = tmp()
                act(ek.f[0:64, 0:TT], gcs.t[:, h, 0:TT], AF.Exp, [gcs.k], [ek.k], scale=1.0 / 16)
                tt(ktl.t[:, h, 0:TT], p.t[0:64, 0:TT], ek.f[0:64, 0:TT], ALU.mult, [p.k, ek.k], [ktl.k])
            for ck_ in range(NCK):
                pt = psn()
                ptb = pt.t[:].bitcast(BF16)
                for h in range(4):
                    tr(ptb[0:L, h * 64:(h + 1) * 64], ktl.t[:, h, ck_ * L:(ck_ + 1) * L], identb.t[0:64, 0:64], [ktl.k, identb.k], [pt.k])
                evac(ktok.t[0:L, ck_, :], ptb[0:L, 0:256], [pt.k], [ktok.k])
            S.phase = 'gla_chunks'
            gacc = PS[0:4]
            gring = PS[4:8]
            gr_i = 0
            for ck_ in range(NCK):
                cs_ = slice(ck_ * L, (ck_ + 1) * L)
                pas, ams = [], []
                for h in range(4):
                    pa = gring[h]
                    mm(pa.t[0:L, 0:L], ktl.t[:, h, cs_], qtl.t[:, h, cs_], True, True, [ktl.k, qtl.k], [pa.k])
                    pas.append(pa)
                for h in range(4):
                    am = tmp()
                    tt(am.b[0:L, 0:L], pas[h].t[0:L, 0:L], tri.t[0:L, 0:L], ALU.mult, [pas[h].k, tri.k], [am.k])
                    ams.append(am)
                for h in range(4):
                    mm(gacc[h].t[:, cs_], gv_tok.t[0:L, ck_, h * 128:(h + 1) * 128], ams[h].b[0:L, 0:L], True, False, [gv_tok.k, ams[h].k], [gacc[h].k])
                    mm(gacc[h].t[:, cs_], gSb[l].t[:, h, :], qtl.t[:, h, cs_], False, True, [gSbk[l][h], qtl.k], [gacc[h].k])
                for h in range(4):
                    pu = gring[h]
                    mm(pu.t[0:64, 0:128], ktok.t[0:L, ck_, h * 64:(h + 1) * 64], gv_tok.t[0:L, ck_, h * 128:(h + 1) * 128], True, True, [ktok.k, gv_tok.k], [pu.k])
                for h in range(4):
                    pu = gring[h]
                    tt(gS[l].t[:, h, :], gS[l].t[:, h, :], pu.t[0:64, 0:128], ALU.add, [gSk[l][h], pu.k], [gSk[l][h]])
                    ts(gS[l].t[:, h, :], gS[l].t[:, h, :], ebl.t[:, h, ck_:ck_ + 1], None, ALU.mult, None, [gSk[l][h], ebl.k], [gSk[l][h]])
                    evac(gSb[l].t[:, h, :], gS[l].t[:, h, :], [gSk[l][h]], [gSbk[l][h]])
            gr_i = 0
            S.phase = 'gla_out'
            ws = wload(l, "U5")
            w3 = ws.t[:, 0:4096].rearrange("p (kc n) -> p kc n", kc=8)
            for h in range(4):
                gf = tmp()
                act(gf.f[:, 0:TT], gacc[h].t[:, 0:TT], AF.Copy, [gacc[h].k], [gf.k])
                sq = tmp()
                act(sq.b[:, 0:TT], gf.f[:, 0:TT], AF.Square, [gf.k], [sq.k])
                p = gring[gr_i % 4]
                gr_i += 1
                mm(p.t[:, 0:TT], onesb.t[:, :], sq.b[:, 0:TT], True, True, [onesb.k, sq.k], [p.k])
                rs = tmp()
                act(rs.f[:, 0:TT], p.t[:, 0:TT], AF.Ln, [p.k], [rs.k], bias=eps_ap, scale=1.0 / 128)
                act(rs.f[:, 0:TT], rs.f[:, 0:TT], AF.Exp, [rs.k], [rs.k], scale=-0.5)
                stt(gf.f[:, 0:TT], gf.f[:, 0:TT], pB.t[:, 33:34], rs.f[:, 0:TT], ALU.mult, ALU.mult, [gf.k, rs.k, pB.k], [gf.k])
                p = gring[gr_i % 4]
                gr_i += 1
                for kc in range(8):
                    mm(p.t[:, 0:TT], w3[:, kc, h * 128:(h + 1) * 128], xn.t[:, kc, 0:TT], kc == 0, kc == 7, [xn.k, ws.k], [p.k])
                sl = tmp()
                act(sl.f[:, 0:TT], p.t[:, 0:TT], AF.Silu, [p.k], [sl.k])
                tt(gon.t[:, h, 0:TT], gf.f[:, 0:TT], sl.f[:, 0:TT], ALU.mult, [gf.k, sl.k], [gon.k])

            S.phase = 'lru'
            for half in range(2):
                cs2 = [2 * half, 2 * half + 1]
                T1 = {c: tmp() for c in cs2}
                T2 = {c: tmp() for c in cs2}
                T3 = {c: tmp() for c in cs2}
                T4 = {c: tmp() for c in cs2}
                rk = [lxT.k, pB.k]
                for c in cs2:
                    xc = T1[c]
                    ts(xc.f[:, 0:TT], lxT.t[:, c, 0:TT], pB.t[:, c:c + 1], pB.t[:, 16 + c:17 + c], ALU.mult, ALU.add, rk, [xc.k])
                for j in range(1, 4):
                    for c in cs2:
                        xc = T1[c]
                        stt(xc.f[:, 0:TT], lxT.t[:, c, j:j + TT], pB.t[:, 4 * j + c:4 * j + c + 1], xc.f[:, 0:TT], ALU.mult, ALU.add, rk + [xc.k], [xc.k])
                for c in cs2:
                    cp(T2[c].b[:, 0:TT], T1[c].f[:, 0:TT], [T1[c].k], [T2[c].k], eng="gpsimd")
                prs_, pis_ = {}, {}
                for c in cs2:
                    prs_[c] = psn()
                    mm(prs_[c].t[:, 0:TT], w83[:, c, :], T2[c].b[:, 0:TT], True, True, [w8.k, T2[c].k], [prs_[c].k])
                    pis_[c] = psn()
                    mm(pis_[c].t[:, 0:TT], w83[:, 4 + c, :], T2[c].b[:, 0:TT], True, True, [w8.k, T2[c].k], [pis_[c].k])
                for c in cs2:
                    act(T3[c].f[:, 0:TT], prs_[c].t[:, 0:TT], AF.Sigmoid, [prs_[c].k, pB.k], [T3[c].k], bias=pB.t[:, 20 + c:21 + c])
                    act(T4[c].f[:, 0:TT], pis_[c].t[:, 0:TT], AF.Sigmoid, [pis_[c].k, pB.k], [T4[c].k], bias=pB.t[:, 24 + c:25 + c])
                for c in cs2:
                    act(T3[c].f[:, 0:TT], T3[c].f[:, 0:TT], AF.Exp, [T3[c].k, lamc[l].k], [T3[c].k], scale=lamc[l].t[:, 2 + c:3 + c])
                for c in cs2:
                    tt(T2[c].f[:, 0:TT], T3[c].f[:, 0:TT], T3[c].f[:, 0:TT], ALU.mult, [T3[c].k, T2[c].k], [T2[c].k])
                    ts(T2[c].f[:, 0:TT], T2[c].f[:, 0:TT], -1.0, 1.0, ALU.mult, ALU.add, [T2[c].k], [T2[c].k])
                    ts(T2[c].f[:, 0:TT], T2[c].f[:, 0:TT], 1e-18, None, ALU.max, None, [T2[c].k], [T2[c].k])
                    tt(T4[c].f[:, 0:TT], T4[c].f[:, 0:TT], T1[c].f[:, 0:TT], ALU.mult, [T4[c].k, T1[c].k], [T4[c].k])
                for c in cs2:
                    act(T2[c].f[:, 0:TT], T2[c].f[:, 0:TT], AF.Ln, [T2[c].k], [T2[c].k])
                for c in cs2:
                    act(T2[c].f[:, 0:TT], T2[c].f[:, 0:TT], AF.Exp, [T2[c].k], [T2[c].k], scale=0.5)
                for c in cs2:
                    tt(T4[c].f[:, 0:TT], T4[c].f[:, 0:TT], T2[c].f[:, 0:TT], ALU.mult, [T4[c].k, T2[c].k], [T4[c].k])
                for c in cs2:
                    scan(T1[c].f[:, 0:TT], T3[c].f[:, 0:TT], T4[c].f[:, 0:TT], hst[l].t[:, c:c + 1], [T3[c].k, T4[c].k, hst[l].k, T1[c].k], [T1[c].k])
                    cp(hst[l].t[:, c:c + 1], T1[c].f[:, TT - 1:TT], [T1[c].k], [hst[l].k])
                pgs = {}
                for c in cs2:
                    pgs[c] = psn()
                    for kc in range(8):
                        mm(pgs[c].t[:, 0:TT], w73[:, kc, c * 128:(c + 1) * 128], xn.t[:, kc, 0:TT], kc == 0, kc == 7, [xn.k, w7.k], [pgs[c].k])
                for c in cs2:
                    act(T2[c].f[:, 0:TT], pgs[c].t[:, 0:TT], AF.Gelu_apprx_tanh, [pgs[c].k, T2[c].k], [T2[c].k])
                for c in cs2:
                    tt(hl.t[:, c, 0:TT], T1[c].f[:, 0:TT], T2[c].f[:, 0:TT], ALU.mult, [T1[c].k, T2[c].k], [hl.k])

            S.phase = 'merge'
            for m in range(8):
                ws = wload(l, f"M{m}")
                wsb = wload(l, f"B{m}")
                wm4 = ws.t[:, 0:3072].rearrange("p (kc j f) -> p kc j f", kc=8, j=3)
                wb4 = wsb.t[:, 0:1536].rearrange("p (kc j f) -> p kc j f", kc=4, j=3)
                srcs = [oattn, gon, hl]
                acc_t = None
                for j in range(3):
                    pg = psn()
                    for kc in range(8):
                        mm(pg.t[:, 0:TT], wm4[:, kc, j, :], xn.t[:, kc, 0:TT], kc == 0, kc == 7, [xn.k, ws.k], [pg.k])
                    g = tmp()
                    act(g.f[:, 0:TT], pg.t[:, 0:TT], AF.Sigmoid, [pg.k, pA.k], [g.k], bias=pA.t[:, 88 + j * 8 + m:89 + j * 8 + m])
                    py = psn()
                    for kc in range(4):
                        mm(py.t[:, 0:TT], wb4[:, kc, j, :], srcs[j].t[:, kc, 0:TT], kc == 0, kc == 3, [srcs[j].k, wsb.k], [py.k])
                    if j == 0:
                        acc_t = tmp()
                        tt(acc_t.f[:, 0:TT], g.f[:, 0:TT], py.t[:, 0:TT], ALU.mult, [g.k, py.k], [acc_t.k])
                    else:
                        tt(g.f[:, 0:TT], g.f[:, 0:TT], py.t[:, 0:TT], ALU.mult, [g.k, py.k], [g.k])
                        if j == 1:
                            tt(acc_t.f[:, 0:TT], acc_t.f[:, 0:TT], g.f[:, 0:TT], ALU.add, [g.k, acc_t.k], [acc_t.k])
                        else:
                            tt(mergedb.t[:, m, 0:TT], acc_t.f[:, 0:TT], g.f[:, 0:TT], ALU.add, [g.k, acc_t.k], [mergedb.k])
            S.phase = 'wout'
            for i in range(2):
                ws = wload(l, f"O{i}")
                w3 = ws.t[:, 0:4096].rearrange("p (kc n) -> p kc n", kc=8)
                for mi in range(4):
                    m = i * 4 + mi
                    p = psn()
                    for kc in range(8):
                        mm(p.t[:, 0:TT], w3[:, kc, mi * 128:(mi + 1) * 128], mergedb.t[:, kc, 0:TT], kc == 0, kc == 7, [mergedb.k, ws.k], [p.k])
                    tt(xT.t[:, m, 0:TT], xT.t[:, m, 0:TT], p.t[:, 0:TT], ALU.add, [xT.k, p.k], [xT.k])

            S.phase = 'ffn_gu'
            rmsnorm_to(xn, lambda c: pA.t[:, 120 + c:121 + c])
            for i in range(11):
                ws = wload(l, f"F{i}")
                w4 = ws.t[:, 0:4096].rearrange("p (kc g n) -> p kc g n", kc=8, g=2)
                for s_ in range(2):
                    j = 2 * i + s_
                    pg = psn()
                    for kc in range(8):
                        mm(pg.t[:, 0:TT], w4[:, kc, 0, s_ * 128:(s_ + 1) * 128], xn.t[:, kc, 0:TT], kc == 0, kc == 7, [xn.k, ws.k], [pg.k])
                    pu = psn()
                    for kc in range(8):
                        mm(pu.t[:, 0:TT], w4[:, kc, 1, s_ * 128:(s_ + 1) * 128], xn.t[:, kc, 0:TT], kc == 0, kc == 7, [xn.k, ws.k], [pu.k])
                    gu = tmp()
                    gu2 = tmp()
                    act(gu.f[:, 0:TT], pg.t[:, 0:TT], AF.Copy, [pg.k], [gu.k])
                    gc = gu2
                    act(gc.f[:, 0:TT], pg.t[:, 0:TT], AF.Identity, [pg.k, pA.k], [gc.k], bias=pA.t[:, 66 + j:67 + j], scale=pA.t[:, 44 + j:45 + j])
                    if TT > 1:
                        stt(gc.f[:, 1:TT], gu.f[:, 0:TT - 1], pA.t[:, 22 + j:23 + j], gc.f[:, 1:TT], ALU.mult, ALU.add, [gu.k, gc.k, pA.k], [gc.k])
                    stt(gc.f[:, 0:1], guh[l].t[:, j, 1:2], pA.t[:, 22 + j:23 + j], gc.f[:, 0:1], ALU.mult, ALU.add, [guh[l].k, gc.k, pA.k], [gc.k])
                    stt(gc.f[:, 2:TT], gu.f[:, 0:TT - 2], pA.t[:, j:j + 1], gc.f[:, 2:TT], ALU.mult, ALU.add, [gu.k, gc.k, pA.k], [gc.k])
                    stt(gc.f[:, 0:2], guh[l].t[:, j, 0:2], pA.t[:, j:j + 1], gc.f[:, 0:2], ALU.mult, ALU.add, [guh[l].k, gc.k, pA.k], [gc.k])
                    cp(guh[l].t[:, j, :], gu.f[:, TT - 2:TT], [gu.k, gc.k], [guh[l].k], eng="gpsimd")
                    act(gc.f[:, 0:TT], gc.f[:, 0:TT], AF.Gelu_apprx_tanh, [gc.k], [gc.k])
                    tt(fT_t[:, j, 0:TT], gc.f[:, 0:TT], pu.t[:, 0:TT], ALU.mult, [gc.k, pu.k], [fT[j].k])
            S.phase = 'ffn_down'
            for m in range(8):
                ws = wload(l, f"D{m}")
                w3 = ws.t[:, 0:NFC * 128].rearrange("p (kc n) -> p kc n", kc=NFC)
                p = psn()
                for kc in range(NFC):
                    mm(p.t[:, 0:TT], w3[:, kc, :], fT_t[:, kc, 0:TT], kc == 0, kc == NFC - 1, [fT[kc].k, ws.k], [p.k])
                tt(xT.t[:, m, 0:TT], xT.t[:, m, 0:TT], p.t[:, 0:TT], ALU.add, [xT.k, p.k], [xT.k])

            S.phase = 'stateout'
            if is_last:
                ot = otile()
                dma(outs["gla"][l].rearrange("h k v -> k h v"), gS[l].t[:, :, :], gSk[l], [ot], eng="gpsimd")
                for j in range(3):
                    ot = otile()
                    dma(outs["lc"][l][j].rearrange("(c p) -> p c", p=128), lxh[l].t[:, :, j], [lxh[l].k], [ot], allow_slow_non_contiguous=True)
                ot = otile()
                dma(outs["lh"][l].rearrange("(c p) -> p c", p=128), hst[l].t[:, :], [hst[l].k], [ot], allow_slow_non_contiguous=True)
                for j in range(2):
                    ot = otile()
                    dma(outs["fc"][l][j].rearrange("(c p) -> p c", p=128), guh[l].t[:, :, j], [guh[l].k], [ot], allow_slow_non_contiguous=True)

        S.phase = 'yout'
        p = psn()
        for c in range(8):
            sq = tmp()
            act(sq.b[:, 0:TT], xT.t[:, c, 0:TT], AF.Square, [xT.k], [sq.k])
            mm(p.t[:, 0:TT], onesb.t[:, :], sq.b[:, 0:TT], c == 0, c == 7, [onesb.k, sq.k], [p.k])
        rsf = lxT.t[:, 0, 0:TT]
        act(rsf, p.t[:, 0:TT], AF.Ln, [p.k], [lxT.k], bias=eps_ap, scale=1.0 / D)
        act(rsf, rsf, AF.Exp, [lxT.k], [lxT.k], scale=-0.5)
        base2, yk = tmp8()

        def ytok(bi):
            return pool_t[0:BP, base2 + 2 * bi: base2 + 2 * bi + 2, :].rearrange("p a n -> p (a n)")

        for c in range(8):
            yc = tmp()
            stt(yc.f[:, 0:TT], xT.t[:, c, 0:TT], parB[0].t[:, 38 + c:39 + c], rsf, ALU.mult, ALU.mult, [xT.k, lxT.k, parB[0].k], [yc.k])
            pt = psn()
            for bi in range(NB):
                tr(pt.t[0:BP, bi * 128:(bi + 1) * 128], yc.f[:, bi * BP:(bi + 1) * BP], identf.t[:, :], [yc.k, identf.k], [pt.k])
            for bi in range(NB):
                evac(ytok(bi)[:, c * 128:(c + 1) * 128], pt.t[0:BP, bi * 128:(bi + 1) * 128], [pt.k], [yk[2 * bi], yk[2 * bi + 1]])
        for bi in range(NB):
            ot = otile()
            dma(ysrc[bi * BP:(bi + 1) * BP, :], ytok(bi), [yk[2 * bi], yk[2 * bi + 1]], [ot], eng="gpsimd")

    epsb = mk("epsb", [128, 1], F32)
    mset(epsb.t[:], EPS, [], [epsb.k])
    eps_ap = epsb.t[:, 0:1]
    _orig_act = act

    def act(out, in_, func, reads, writes, bias=None, scale=None):
        if bias is eps_ap:
            reads = list(reads) + [epsb.k]
        _orig_act(out, in_, func, reads, writes, bias=bias, scale=scale)

    if cfg["samples"]:
        for s in range(2):
            outs = {
                "k": [ks_o[l, s] for l in range(DEPTH)], "v": [vs_o[l, s] for l in range(DEPTH)],
                "gla": [glas[l, s] for l in range(DEPTH)], "lc": [lcs[l, s] for l in range(DEPTH)],
                "lh": [lhs_o[l, s] for l in range(DEPTH)], "fc": [fcs[l, s] for l in range(DEPTH)],
            }
            run_tile(s, DSEQ, DSEQ, 1, 32, 1, xs[s], ys[s], ropes, 0, PAST // 512, True, outs)
    npt = cfg["n_ptiles"]
    for ti in range(npt):
        t0 = ti * TTP
        outs = {
            "k": [kp[l, t0:t0 + TTP] for l in range(DEPTH)], "v": [vp[l, t0:t0 + TTP] for l in range(DEPTH)],
            "gla": [glap[l] for l in range(DEPTH)], "lc": [lcp[l] for l in range(DEPTH)],
            "lh": [lhp[l] for l in range(DEPTH)], "fc": [fcp[l] for l in range(DEPTH)],
        }
        run_tile('p', TTP, 128, 4, 64, 8, xp[t0:t0 + TTP], yp[t0:t0 + TTP], ropep[t0:t0 + TTP], t0, ti, ti == npt - 1, outs)

    S.op("sync", lambda e: e.nop(), reads=out_tiles)
    S.emit()
    return nc, S


def _rope_table(pos):
    half = 8
    inv = (np.float32(500000.0) ** (-np.arange(half, dtype=np.float32) / np.float32(half))).astype(np.float32)
    ang = (pos.astype(np.float32)[:, None] * inv[None, :]).astype(np.float32)
    cos = np.cos(ang).astype(np.float32)
    sin = np.sin(ang).astype(np.float32)
    return np.concatenate([np.tile(cos, (1, 8)), np.tile(sin, (1, 8))], axis=1).astype(np.float32)


_WNAMES = ["norm_mix", "w_in", "lambda_qk", "attn_subln", "w_gla_gate2", "b_gla_gate", "gla_norm", "lru_conv_w",
           "lru_conv_b", "lru_wa", "lru_ba", "lru_wx", "lru_bx", "lru_lambda", "w_branch_attn", "w_branch_gla",
           "w_branch_lru", "w_merge", "b_merge", "w_out", "norm_ffn", "w_ffn_gate", "ffn_conv_w", "ffn_conv_b",
           "w_ffn_up", "w_ffn_down", "norm_final"]


def kernel(**inp):
    cfg = dict(CFG)
    nc, S = build_program(cfg)
    f = lambda a: np.ascontiguousarray(np.asarray(a, dtype=np.float32))
    NLc = cfg['depth']
    wts = {n: (f(inp[n]) if n == 'norm_final' else f(np.asarray(inp[n])[:NLc])) for n in _WNAMES}
    consts = {
        "ropep": _rope_table(np.arange(SEQ)),
        "ropes": _rope_table(PAST + np.arange(DSEQ)),
        "cmask": (np.arange(TTP) % 64 != 0).astype(np.float32)[None, :],
        "tri": np.triu(np.ones((64, 64), np.float32)),
        "ident": np.eye(128, dtype=np.float32),
    }
    x_prompt = f(inp["x_prompt"])
    x_sample = f(inp["x_sample"])
    ckf = np.asarray(inp["cache_attn_k"], dtype=np.float32).reshape(DEPTH, 16, PAST, 512)
    cvf = np.asarray(inp["cache_attn_v"], dtype=np.float32).reshape(DEPTH, 16, PAST, 512)
    sgf = f(inp["state_gla"])
    slcf = f(inp["state_lru_conv"])
    slhf = f(inp["state_lru_h"])
    sfcf = f(inp["state_ffn_conv"])
    PCORES = [0, 1, 4, 5]
    zero_x = np.zeros((SEQ, D), np.float32)
    in_maps = []
    for c in range(8):
        s0 = 2 * c
        m = {
            "xp": x_prompt[PCORES.index(c)] if c in PCORES else zero_x,
            "xs": x_sample[s0:s0 + 2],
            "ck": np.ascontiguousarray(ckf[:, s0:s0 + 2]),
            "cv": np.ascontiguousarray(cvf[:, s0:s0 + 2]),
            "sg": np.ascontiguousarray(sgf[:, s0:s0 + 2]),
            "slc": np.ascontiguousarray(slcf[:, s0:s0 + 2]),
            "slh": np.ascontiguousarray(slhf[:, s0:s0 + 2]),
            "sfc": np.ascontiguousarray(sfcf[:, s0:s0 + 2]),
        }
        m.update(wts)
        m.update(consts)
        in_maps.append(m)
    res = run_bass_kernel_spmd(nc, in_maps, core_ids=list(range(8))).results
    B = 4
    y_prompt = np.stack([res[b]["yp"] for b in PCORES])
    k_p = np.stack([res[b]["kp"] for b in PCORES], axis=1).reshape(DEPTH, B, SEQ, 4, 2, 64)
    v_p = np.stack([res[b]["vp"] for b in PCORES], axis=1).reshape(DEPTH, B, SEQ, 4, 128)
    gla_p = np.stack([res[b]["glap"] for b in PCORES], axis=1)
    lc_p = np.stack([res[b]["lcp"] for b in PCORES], axis=1)
    lh_p = np.stack([res[b]["lhp"] for b in PCORES], axis=1)
    fc_p = np.stack([res[b]["fcp"] for b in PCORES], axis=1)
    y_sample = np.concatenate([res[c]["ys"] for c in range(8)], axis=0)
    k_s = np.concatenate([res[c]["ks"] for c in range(8)], axis=1).reshape(DEPTH, 16, DSEQ, 4, 2, 64)
    v_s = np.concatenate([res[c]["vs"] for c in range(8)], axis=1).reshape(DEPTH, 16, DSEQ, 4, 128)
    gla_s = np.concatenate([res[c]["glas"] for c in range(8)], axis=1)
    lc_s = np.concatenate([res[c]["lcs"] for c in range(8)], axis=1)
    lh_s = np.concatenate([res[c]["lhs"] for c in range(8)], axis=1)
    fc_s = np.concatenate([res[c]["fcs"] for c in range(8)], axis=1)
    outs = (y_prompt, y_sample, k_p, v_p, gla_p, lc_p, lh_p, fc_p, k_s, v_s, gla_s, lc_s, lh_s, fc_s)
    return tuple(np.ascontiguousarray(o, dtype=np.float32) for o in outs)
```

```python
import math
from contextlib import ExitStack

import numpy as np
import concourse.bass as bass
import concourse.mybir as mybir
from concourse.bass_utils import run_bass_kernel_spmd

F32 = mybir.dt.float32
BF16 = mybir.dt.bfloat16
AF = mybir.ActivationFunctionType
ALU = mybir.AluOpType

ENGS = ["tensor", "vector", "scalar", "gpsimd", "sync"]

CFG = {"n_ptiles": 16, "depth": 4, "samples": True}

D = 1024
DEPTH = 4
SEQ = 8192
TTP = 512
PAST = 2048
DSEQ = 32
EPS = 1e-6
AQ, AK, AV, GQ, GK, GV, GR, GA, LX, LG, PW = 0, 512, 1024, 1536, 1792, 2048, 2560, 3072, 3088, 3600, 4112
DFF = 2816
NFC = 22
WSLOT = 4224


class Tile:
    __slots__ = ("name", "last_w", "readers", "psum")

    def __init__(self, name):
        self.name = name
        self.last_w = None
        self.readers = []
        self.psum = False


class Op:
    __slots__ = ("eng", "fn", "deps", "dma", "signal", "sem", "val", "idx", "phase")


class Sched:
    RING = 12

    def __init__(self, nc):
        self.nc = nc
        self.ops = []
        self.stack = ExitStack()
        self.ntile = 0
        self.phase = 'init'

    def sbuf(self, name, shape, dtype):
        return self.stack.enter_context(self.nc.sbuf_tensor("sb_" + name, list(shape), dtype))

    def psum(self, name, shape, dtype):
        return self.stack.enter_context(self.nc.psum_tensor("pp_" + name, list(shape), dtype))

    def tile(self, name=None):
        self.ntile += 1
        return Tile(name or f"t{self.ntile}")

    def op(self, eng, fn, reads=(), writes=(), dma=False):
        o = Op()
        o.eng = eng
        o.fn = fn
        o.dma = dma
        o.signal = False
        o.sem = None
        o.val = 0
        o.idx = len(self.ops)
        o.phase = self.phase
        deps = set()
        ops = self.ops
        for t in reads:
            if t.last_w is not None:
                deps.add(t.last_w)
            if t.psum:
                for r in t.readers:
                    if ops[r].eng != eng:
                        deps.add(r)
        for t in writes:
            if t.last_w is not None:
                deps.add(t.last_w)
            for r in t.readers:
                ro = ops[r]
                if (not dma) and (not ro.dma) and ro.eng == eng:
                    continue
                deps.add(r)
        if eng == "tensor" and not dma:
            deps = {d for d in deps if ops[d].dma or ops[d].eng != "tensor"}
        o.deps = deps
        for t in reads:
            if not dma:
                t.readers = [r for r in t.readers if ops[r].dma or ops[r].eng != eng]
            t.readers.append(o.idx)
        for t in writes:
            t.last_w = o.idx
            t.readers = []
        ops.append(o)
        return o

    def emit(self):
        nc = self.nc
        ops = self.ops
        stack = self.stack
        dma_count = {e: 0 for e in ENGS}
        dma_hist = {e: [] for e in ENGS}
        for o in ops:
            if o.dma:
                j = dma_count[o.eng]
                if j >= self.RING:
                    o.deps.add(dma_hist[o.eng][j - self.RING])
                dma_hist[o.eng].append(o.idx)
                dma_count[o.eng] += 1
        for o in ops:
            for d in o.deps:
                ops[d].signal = True
        eng_sem = {e: stack.enter_context(nc.semaphore(f"s_{e}")) for e in ENGS}
        ring_sems = {e: [stack.enter_context(nc.semaphore(f"d_{e}_{i}")) for i in range(self.RING)]
                     for e in ENGS if dma_count[e] > 0}
        cnt = {e: 0 for e in ENGS}
        dj = {e: 0 for e in ENGS}
        for o in ops:
            if o.dma:
                j = dj[o.eng]
                o.sem = ring_sems[o.eng][j % self.RING]
                o.val = 16 * (j // self.RING + 1)
                dj[o.eng] += 1
            elif o.signal:
                cnt[o.eng] += 1
                o.sem = eng_sem[o.eng]
                o.val = cnt[o.eng]
        per_eng = {e: [] for e in ENGS}
        for o in ops:
            per_eng[o.eng].append(o)
        self.stats = {e: len(per_eng[e]) for e in ENGS}
        block = stack.enter_context(nc.Block())

        def make(ename):
            def body(eng):
                waited = {}
                for o in per_eng[ename]:
                    need = {}
                    for d in o.deps:
                        do = ops[d]
                        k = id(do.sem)
                        if k not in need or need[k][1] < do.val:
                            need[k] = (do.sem, do.val)
                    for k, (sem, val) in need.items():
                        if waited.get(k, 0) >= val:
                            continue
                        eng.wait_ge(sem, val)
                        waited[k] = val
                    ins = o.fn(eng)
                    if o.dma:
                        ins.then_inc(o.sem, 16)
                    elif o.signal:
                        ins.then_inc(o.sem, 1)
            return body

        for e in ENGS:
            if per_eng[e]:
                getattr(block, e)(make(e))
        stack.close()


class Buf:
    __slots__ = ("t", "k")

    def __init__(self, t, k):
        self.t = t
        self.k = k


def build_program(cfg):
    nc = bass.Bass("TRN2", target_bir_lowering=False)
    S = Sched(nc)
    NL = cfg["depth"]

    def din(name, shape, dt=F32):
        return nc.dram_tensor(name, list(shape), dt, kind="ExternalInput").ap()

    def dout(name, shape, dt=F32):
        return nc.dram_tensor(name, list(shape), dt, kind="ExternalOutput").ap()

    def dscr(name, shape, dt=BF16):
        return nc.dram_tensor(name, list(shape), dt).ap()

    xp = din("xp", [SEQ, D])
    xs = din("xs", [2, DSEQ, D])
    ck = din("ck", [DEPTH, 2, PAST, 512])
    cv = din("cv", [DEPTH, 2, PAST, 512])
    sg = din("sg", [DEPTH, 2, 4, 64, 128])
    slc = din("slc", [DEPTH, 2, 3, 512])
    slh = din("slh", [DEPTH, 2, 512])
    sfc = din("sfc", [DEPTH, 2, 2, DFF])
    W = {}
    for name, shape in [
        ("norm_mix", [NL, D]), ("w_in", [NL, D, PW]), ("lambda_qk", [NL, 4, 64]),
        ("attn_subln", [NL, 128]), ("w_gla_gate2", [NL, 16, 256]), ("b_gla_gate", [NL, 256]),
        ("gla_norm", [NL, 128]), ("lru_conv_w", [NL, 4, 512]), ("lru_conv_b", [NL, 512]),
        ("lru_wa", [NL, 8, 64, 64]), ("lru_ba", [NL, 512]), ("lru_wx", [NL, 8, 64, 64]),
        ("lru_bx", [NL, 512]), ("lru_lambda", [NL, 512]), ("w_branch_attn", [NL, 512, D]),
        ("w_branch_gla", [NL, 512, D]), ("w_branch_lru", [NL, 512, D]), ("w_merge", [NL, D, 3 * D]),
        ("b_merge", [NL, 3 * D]), ("w_out", [NL, D, D]), ("norm_ffn", [NL, D]),
        ("w_ffn_gate", [NL, D, DFF]), ("ffn_conv_w", [NL, 3, DFF]), ("ffn_conv_b", [NL, DFF]),
        ("w_ffn_up", [NL, D, DFF]), ("w_ffn_down", [NL, DFF, D]), ("norm_final", [D]),
    ]:
        W[name] = din(name, shape)
    ropep = din("ropep", [SEQ, 128])
    ropes = din("ropes", [DSEQ, 128])
    cmask_d = din("cmask", [1, TTP])
    tri_d = din("tri", [64, 64])
    ident_d = din("ident", [128, 128])

    yp = dout("yp", [SEQ, D])
    ys = dout("ys", [2, DSEQ, D])
    kp = dout("kp", [DEPTH, SEQ, 512])
    vp = dout("vp", [DEPTH, SEQ, 512])
    glap = dout("glap", [DEPTH, 4, 64, 128])
    lcp = dout("lcp", [DEPTH, 3, 512])
    lhp = dout("lhp", [DEPTH, 512])
    fcp = dout("fcp", [DEPTH, 2, DFF])
    ks_o = dout("ks", [DEPTH, 2, DSEQ, 512])
    vs_o = dout("vs", [DEPTH, 2, DSEQ, 512])
    glas = dout("glas", [DEPTH, 2, 4, 64, 128])
    lcs = dout("lcs", [DEPTH, 2, 3, 512])
    lhs_o = dout("lhs", [DEPTH, 2, 512])
    fcs = dout("fcs", [DEPTH, 2, 2, DFF])
    out_tiles = []

    def otile():
        t = S.tile()
        out_tiles.append(t)
        return t

    ktscr = dscr("ktscr", [DEPTH, 4, 128, SEQ])
    vscr = dscr("vscr", [DEPTH, 4, SEQ // 512, 128, 512])
    kt_t = [[[S.tile() for u in range(16)] for h in range(4)] for l in range(DEPTH)]
    v_t = [[[S.tile() for u in range(16)] for h in range(4)] for l in range(DEPTH)]

    units = {}

    def mk(name, shape, dt):
        return Buf(S.sbuf(name, shape, dt), S.tile(name))

    xT = mk("xT", [128, 8, TTP], F32)
    NP = 16
    pool_t = S.sbuf("pool", [128, NP, TTP], F32)
    pool_k = [S.tile(f"pool{i}") for i in range(NP)]
    pool_i = [0]

    class Tmp:
        __slots__ = ("f", "b", "k")

    def tmp():
        i = pool_i[0] % NP
        pool_i[0] += 1
        r = Tmp()
        r.f = pool_t[:, i, :]
        r.b = pool_t[:, i, :].bitcast(BF16)
        r.k = pool_k[i]
        return r

    def tmp8():
        if (pool_i[0] % NP) + 8 > NP:
            pool_i[0] += NP - (pool_i[0] % NP)
        base = pool_i[0] % NP
        ks_ = [tmp().k for _ in range(8)]
        return base, ks_

    xn = mk("xn", [128, 8, TTP], BF16)
    NWR = 5
    wring = [mk(f"wring{i}", [128, WSLOT], BF16) for i in range(NWR)]
    wr_i = [0]
    QT = mk("QT", [128, 4, TTP], BF16)
    KTc = mk("KTc", [128, 4, TTP], BF16)
    qtl = mk("qtl", [64, 4, TTP], BF16)
    ktl = mk("ktl", [64, 4, TTP], BF16)
    v_bf = mk("v_bf", [128, 4, 512], BF16)
    gv_tok = mk("gv_tok", [64, 8, 512], BF16)
    ktok = mk("ktok", [64, 8, 256], BF16)
    fT = [Buf(None, S.tile(f"fT{j}")) for j in range(NFC)]
    fT_t = S.sbuf("fT", [128, NFC, TTP], BF16)
    mergedb = Buf(fT_t[:, 0:8, :], S.tile("mergedb"))
    oattn = Buf(fT_t[:, 8:12, :], S.tile("oattn"))
    gon = Buf(fT_t[:, 12:16, :], S.tile("gon"))
    hl = Buf(fT_t[:, 16:20, :], S.tile("hl"))
    for j in range(NFC):
        if j < 8:
            fT[j].k = mergedb.k
        elif j < 12:
            fT[j].k = oattn.k
        elif j < 16:
            fT[j].k = gon.k
        elif j < 20:
            fT[j].k = hl.k
    pring = [mk(f"pring{i}", [128, TTP], BF16) for i in range(4)]
    pr_i = [0]
    kring = [mk(f"kring{i}", [128, 512], BF16) for i in range(3)]
    vring = [mk(f"vring{i}", [128, 4, 128], BF16) for i in range(3)]
    kv_i = [0]
    kst = [mk(f"kst{i}", [128, 4, 128], BF16) for i in range(1)]
    kst_i = [0]
    gS = [mk(f"gS{l}", [64, 4, 128], F32) for l in range(DEPTH)]
    gSb = [mk(f"gSb{l}", [64, 4, 128], BF16) for l in range(DEPTH)]
    gSk = [[S.tile(f"gSk{l}_{h}") for h in range(4)] for l in range(DEPTH)]
    gSbk = [[S.tile(f"gSbk{l}_{h}") for h in range(4)] for l in range(DEPTH)]
    hst = [mk(f"hst{l}", [128, 4], F32) for l in range(DEPTH)]
    lxh = [mk(f"lxh{l}", [128, 4, 3], F32) for l in range(DEPTH)]
    guh = [mk(f"guh{l}", [128, NFC, 2], F32) for l in range(DEPTH)]
    parA = [mk(f"parA{l}", [128, 128], F32) for l in range(DEPTH)]
    parB = [mk(f"parB{l}", [128, 64], F32) for l in range(DEPTH)]
    lamc = [mk(f"lamc{l}", [128, 8], F32) for l in range(DEPTH)]
    w2b = [mk(f"w2b{l}", [16, 256], BF16) for l in range(DEPTH)]
    identf = mk("identf", [128, 128], F32)
    identb = mk("identb", [128, 128], BF16)
    onesb = mk("onesb", [128, 128], BF16)
    tri = mk("tri", [64, 64], F32)
    cmask = mk("cmask_sb", [128, TTP], F32)
    rope_sb = mk("rope_sb", [128, 4, 128], F32)
    lxT = mk("lxT", [128, 4, 3 + TTP], F32)
    gcs = mk("gcs", [64, 4, TTP], F32)
    ropet = mk("ropet", [128, 256], F32)
    gab = mk("gab", [16, TTP], BF16)
    ebl = mk("ebl", [64, 4, 8], F32)

    PS = [Buf(S.psum(f"ps{i}", [128, 512], F32), S.tile(f"ps{i}")) for i in range(8)]
    for b_ in PS:
        b_.k.psum = True
    ps_i = [0]

    def psn():
        i = ps_i[0] % 8
        ps_i[0] += 1
        return PS[i]

    def dma(out, in_, reads, writes, eng="sync", **kw):
        S.op(eng, lambda e: e.dma_start(out=out, in_=in_, **kw), reads=reads, writes=writes, dma=True)

    def mm(out, lhsT, rhs, start, stop, reads, writes):
        S.op("tensor", lambda e: e.matmul(out, lhsT=lhsT, rhs=rhs, start=start, stop=stop), reads=reads, writes=writes)

    def tr(out, in_, ident, reads, writes):
        S.op("tensor", lambda e: e.transpose(out=out, in_=in_, identity=ident), reads=reads, writes=writes)

    def act(out, in_, func, reads, writes, bias=None, scale=None):
        kw = {}
        if bias is not None:
            kw["bias"] = bias
        if scale is not None:
            kw["scale"] = scale
        S.op("scalar", lambda e: e.activation(out=out, in_=in_, func=func, **kw), reads=reads, writes=writes)

    def tt(out, in0, in1, op, reads, writes, eng="vector"):
        S.op(eng, lambda e: e.tensor_tensor(out=out, in0=in0, in1=in1, op=op), reads=reads, writes=writes)

    def ts(out, in0, s1, s2, op0, op1, reads, writes, eng="vector"):
        if op1 is None:
            S.op(eng, lambda e: e.tensor_scalar(out=out, in0=in0, scalar1=s1, scalar2=None, op0=op0), reads=reads, writes=writes)
        else:
            S.op(eng, lambda e: e.tensor_scalar(out=out, in0=in0, scalar1=s1, scalar2=s2, op0=op0, op1=op1), reads=reads, writes=writes)

    def stt(out, in0, scalar, in1, op0, op1, reads, writes):
        S.op("vector", lambda e: e.scalar_tensor_tensor(out=out, in0=in0, scalar=scalar, in1=in1, op0=op0, op1=op1), reads=reads, writes=writes)

    def cp(out, in_, reads, writes, eng="vector"):
        S.op(eng, lambda e: e.tensor_copy(out=out, in_=in_), reads=reads, writes=writes)

    def mset(ap, val, reads, writes, eng="gpsimd"):
        S.op(eng, lambda e: e.memset(ap, val), reads=reads, writes=writes)

    def scan(out, d0, d1, init, reads, writes):
        S.op("vector", lambda e: e.tensor_tensor_scan(out=out, data0=d0, data1=d1, initial=init, op0=ALU.mult, op1=ALU.add), reads=reads, writes=writes)

    def recip(out, in_, reads, writes):
        S.op("vector", lambda e: e.reciprocal(out=out, in_=in_), reads=reads, writes=writes)

    ev_i = [0]

    def evac(out, in_, reads, writes):
        ev_i[0] += 1
        if ev_i[0] % 2:
            act(out, in_, AF.Copy, reads, writes)
        else:
            cp(out, in_, reads, writes)

    dma(identf.t[:], ident_d[:, :], [], [identf.k])
    cp(identb.t[:], identf.t[:], [identf.k], [identb.k])
    mset(onesb.t[:], 1.0, [], [onesb.k])
    dma(tri.t[:], tri_d[:, :], [], [tri.k])
    dma(cmask.t[:], cmask_d.rearrange("a b -> (a b)").partition_broadcast(128), [], [cmask.k])

    def wview(name, l):
        return W[name][l].rearrange("(kc p) n -> p kc n", p=128)

    cast_done = set()

    def cast_layer(l):
        if l in cast_done or l >= NL:
            return
        cast_done.add(l)
        zt = tmp()
        mset(zt.f, 0.0, [], [zt.k])
        ul = {}

        def unit(name, width):
            ap = dscr(f"wu_{l}_{name}", [128, width])
            k = S.tile(f"wu_{l}_{name}")
            ul[name] = (ap, k, width)
            return ap, k

        win = wview("w_in", l)
        for name, c0, c1 in [("U0", 0, 512), ("U1", 512, 1024), ("U2", 1024, 1536), ("U3", 1536, 2048),
                             ("U4", 2048, 2560), ("U5", 2560, 3072), ("U6", 3072, 3600), ("U7", 3600, 4112)]:
            w = c1 - c0
            ap, k = unit(name, 8 * w)
            dma(ap.rearrange("p (kc n) -> p kc n", kc=8), win[:, :, c0:c1], [], [k], eng="gpsimd")
        ap, k = unit("U8", 8 * 128)
        dma(ap[:, :], zt.b[:, 0:1024], [zt.k], [k], eng="gpsimd")
        a3 = ap.rearrange("p (c n) -> p c n", c=8)
        for gi, wn in enumerate(["lru_wa", "lru_wx"]):
            for c in range(4):
                for half in range(2):
                    dma(a3[half * 64:(half + 1) * 64, gi * 4 + c, half * 64:(half + 1) * 64],
                        W[wn][l, 2 * c + half], [], [k], eng="gpsimd")
        wm = W["w_merge"][l].rearrange("(kc p) (j m f) -> p kc j m f", p=128, j=3, m=8)
        wbr = [W[n][l].rearrange("(kc p) (m f) -> p kc m f", p=128, m=8) for n in ("w_branch_attn", "w_branch_gla", "w_branch_lru")]
        for m in range(8):
            ap, k = unit(f"M{m}", 3072)
            a_m = ap[:, 0:3072].rearrange("p (kc j f) -> p kc j f", kc=8, j=3)
            for j in range(3):
                dma(a_m[:, :, j, :], wm[:, :, j, m, :], [], [k], eng="gpsimd")
            ap, k = unit(f"B{m}", 1536)
            a_b = ap[:, 0:1536].rearrange("p (kc j f) -> p kc j f", kc=4, j=3)
            for j in range(3):
                dma(a_b[:, :, j, :], wbr[j][:, :, m, :], [], [k], eng="gpsimd")
        wo = wview("w_out", l)
        for i in range(2):
            ap, k = unit(f"O{i}", 4096)
            dma(ap.rearrange("p (kc n) -> p kc n", kc=8), wo[:, :, i * 512:(i + 1) * 512], [], [k], eng="gpsimd")
        wg = wview("w_ffn_gate", l)
        wu = wview("w_ffn_up", l)
        for i in range(11):
            ap, k = unit(f"F{i}", 4096)
            a4 = ap.rearrange("p (kc g n) -> p kc g n", kc=8, g=2)
            dma(a4[:, :, 0, :], wg[:, :, i * 256:(i + 1) * 256], [], [k], eng="gpsimd")
            dma(a4[:, :, 1, :], wu[:, :, i * 256:(i + 1) * 256], [], [k], eng="gpsimd")
        wd = wview("w_ffn_down", l)
        for m in range(8):
            ap, k = unit(f"D{m}", NFC * 128)
            dma(ap.rearrange("p (kc n) -> p kc n", kc=NFC), wd[:, :, m * 128:(m + 1) * 128], [], [k], eng="gpsimd")
        units[l] = ul


    cast_layer(0)

    for l in range(NL):
        st = tmp()
        st_t = st.f[:, 0:128]
        mset(st_t, 0.0, [], [st.k])
        rows = [
            (0, 66, W["ffn_conv_w"][l].rearrange("j (c f) -> (j c) f", f=128)),
            (66, 22, W["ffn_conv_b"][l].rearrange("(c f) -> c f", f=128)),
            (88, 24, W["b_merge"][l].rearrange("(c f) -> c f", f=128)),
            (112, 8, W["norm_mix"][l].rearrange("(c f) -> c f", f=128)),
            (120, 8, W["norm_ffn"][l].rearrange("(c f) -> c f", f=128)),
        ]
        for r0, n, src in rows:
            dma(st_t[r0:r0 + n, :], src, [], [st.k])
        p = psn()
        tr(p.t[:, 0:128], st_t, identf.t[:], [st.k, identf.k], [p.k])
        cp(parA[l].t[:], p.t[:, 0:128], [p.k], [parA[l].k])
        st = tmp()
        st_t = st.f[:, 0:128]
        mset(st_t, 0.0, [], [st.k])
        rows = [
            (0, 16, W["lru_conv_w"][l].rearrange("j (c f) -> (j c) f", f=128), 128),
            (16, 4, W["lru_conv_b"][l].rearrange("(c f) -> c f", f=128), 128),
            (20, 4, W["lru_ba"][l].rearrange("(c f) -> c f", f=128), 128),
            (24, 4, W["lru_bx"][l].rearrange("(c f) -> c f", f=128), 128),
            (28, 4, W["lru_lambda"][l].rearrange("(c f) -> c f", f=128), 128),
            (32, 1, W["attn_subln"][l].rearrange("(c f) -> c f", f=128), 128),
            (33, 1, W["gla_norm"][l].rearrange("(c f) -> c f", f=128), 128),
            (34, 4, W["b_gla_gate"][l].rearrange("(c f) -> c f", f=64), 64),
            (38, 8, W["norm_final"].rearrange("(c f) -> c f", f=128), 128),
        ]
        for r0, n, src, wd_ in rows:
            dma(st_t[r0:r0 + n, 0:wd_], src, [], [st.k])
        p = psn()
        tr(p.t[:, 0:128], st_t, identf.t[:], [st.k, identf.k], [p.k])
        cp(parB[l].t[:], p.t[:, 0:64], [p.k], [parB[l].k])
        lam_init = 0.8 - 0.6 * math.exp(-0.3 * l)
        lq = tmp()
        dma(lq.f[:, 0:256], W["lambda_qk"][l].rearrange("a b -> (a b)").partition_broadcast(128), [], [lq.k])
        t1 = tmp()
        tt(t1.f[:, 0:64], lq.f[:, 0:64], lq.f[:, 64:128], ALU.mult, [lq.k], [t1.k])
        tt(t1.f[:, 64:128], lq.f[:, 128:192], lq.f[:, 192:256], ALU.mult, [lq.k, t1.k], [t1.k])
        S.op("vector", lambda e, t1=t1: e.tensor_reduce(out=t1.f[:, 128:130], in_=t1.f[:, 0:128].rearrange("p (a b) -> p a b", a=2), axis=mybir.AxisListType.X, op=ALU.add), reads=[t1.k], writes=[t1.k])
        act(t1.f[:, 130:132], t1.f[:, 128:130], AF.Exp, [t1.k], [t1.k])
        tt(t1.f[:, 132:133], t1.f[:, 131:132], t1.f[:, 130:131], ALU.subtract, [t1.k], [t1.k])
        ts(lamc[l].t[:, 0:1], t1.f[:, 132:133], -lam_init, None, ALU.add, None, [t1.k], [lamc[l].k])
        ts(lamc[l].t[:, 1:2], parB[l].t[:, 32:33], 1.0 - lam_init, None, ALU.mult, None, [parB[l].k, lamc[l].k], [lamc[l].k])
        act(t1.f[:, 140:144], parB[l].t[:, 28:32], AF.Exp, [parB[l].k, t1.k], [t1.k], scale=-1.0)
        act(t1.f[:, 144:148], t1.f[:, 140:144], AF.Ln, [t1.k], [t1.k], bias=1.0)
        ts(lamc[l].t[:, 2:6], t1.f[:, 144:148], -8.0, None, ALU.mult, None, [t1.k, lamc[l].k], [lamc[l].k])
        ts(parB[l].t[:, 46:50], parB[l].t[:, 34:38], -1.0, None, ALU.mult, None, [parB[l].k], [parB[l].k])
        t2 = tmp()
        dma(t2.f[0:16, 0:256], W["w_gla_gate2"][l], [], [t2.k])
        cp(w2b[l].t[:], t2.f[0:16, 0:256], [t2.k], [w2b[l].k])

    def wload(l, name):
        ap, k, width = units[l][name]
        slot = wring[wr_i[0] % NWR]
        wr_i[0] += 1
        dma(slot.t[:, 0:width], ap[:, :], [k], [slot.k])
        return slot

    def run_tile(seq, TT, BP, NB, L, NCK, xsrc, ysrc, rope_src, t0, units_past, is_last, outs):
        S.phase = 'xload'
        base, xk = tmp8()

        def xtok(bi):
            return pool_t[0:BP, base + 2 * bi: base + 2 * bi + 2, :].rearrange("p a n -> p (a n)")

        for bi in range(NB):
            dma(xtok(bi), xsrc[bi * BP:(bi + 1) * BP, :], [], [xk[2 * bi], xk[2 * bi + 1]])
        dma(rope_sb.t[0:BP, 0:NB, :], rope_src.rearrange("(b p) f -> p b f", p=BP), [], [rope_sb.k])
        for c in range(8):
            p = psn()
            for bi in range(NB):
                tr(p.t[:, bi * BP:(bi + 1) * BP], xtok(bi)[:, c * 128:(c + 1) * 128], identf.t[0:BP, 0:BP],
                   [xk[2 * bi], xk[2 * bi + 1], identf.k], [p.k])
            evac(xT.t[:, c, 0:TT], p.t[:, 0:TT], [p.k], [xT.k])

        def rmsnorm_to(dst, gcol):
            p = psn()
            for c in range(8):
                sq = tmp()
                act(sq.b[:, 0:TT], xT.t[:, c, 0:TT], AF.Square, [xT.k], [sq.k])
                mm(p.t[:, 0:TT], onesb.t[:, :], sq.b[:, 0:TT], c == 0, c == 7, [onesb.k, sq.k], [p.k])
            rs = tmp()
            act(rs.f[:, 0:TT], p.t[:, 0:TT], AF.Ln, [p.k], [rs.k], bias=eps_ap, scale=1.0 / D)
            act(rs.f[:, 0:TT], rs.f[:, 0:TT], AF.Exp, [rs.k], [rs.k], scale=-0.5)
            for c in range(8):
                stt(dst.t[:, c, 0:TT], xT.t[:, c, 0:TT], gcol(c), rs.f[:, 0:TT], ALU.mult, ALU.mult, [xT.k, rs.k], [dst.k])

        for l in range(NL):
            cast_layer(l + 1)
            pA, pB = parA[l], parB[l]
            if t0 == 0:
                if seq == 'p':
                    mset(gS[l].t[:], 0.0, [], gSk[l])
                    mset(gSb[l].t[:], 0.0, [], gSbk[l])
                    mset(hst[l].t[:], 0.0, [], [hst[l].k])
                    mset(lxh[l].t[:], 0.0, [], [lxh[l].k])
                    mset(guh[l].t[:], 0.0, [], [guh[l].k])
                else:
                    dma(gS[l].t[:], sg[l, seq].rearrange("h k v -> k h v"), [], gSk[l])
                    cp(gSb[l].t[:], gS[l].t[:], gSk[l], gSbk[l], eng="gpsimd")
                    dma(hst[l].t[:], slh[l, seq].rearrange("(c p) -> p c", p=128), [], [hst[l].k], allow_slow_non_contiguous=True)
                    for j in range(3):
                        dma(lxh[l].t[:, :, j], slc[l, seq, j].rearrange("(c p) -> p c", p=128), [], [lxh[l].k], allow_slow_non_contiguous=True)
                    for j in range(2):
                        dma(guh[l].t[:, :, j], sfc[l, seq, j].rearrange("(c p) -> p c", p=128), [], [guh[l].k], allow_slow_non_contiguous=True)

            S.phase = 'norm1'
            rmsnorm_to(xn, lambda c: pA.t[:, 112 + c:113 + c])
            xn_r = [xn.k, pA.k]

            S.phase = 'qkv'
            pend_tr = []
            for ui, uname in [(2, "U2"), (0, "U0"), (1, "U1")]:
                ws = wload(l, uname)
                w3 = ws.t[:, 0:4096].rearrange("p (kc n) -> p kc n", kc=8)
                tbs = []
                for bi in range(NB):
                    p = psn()
                    for kc in range(8):
                        mm(p.t[0:BP, :], xn.t[:, kc, bi * BP:(bi + 1) * BP], w3[:, kc, :], kc == 0, kc == 7, [xn.k, ws.k], [p.k])
                    tk = tmp()
                    act(tk.f[0:BP, :], p.t[0:BP, :], AF.Copy, [p.k], [tk.k])
                    if ui < 2:
                        x4 = tk.f[0:BP, :].rearrange("p (a d) -> p a d", d=64)
                        cs4 = rope_sb.t[0:BP, bi, 0:64].rearrange("p (a d) -> p a d", d=8)
                        sn4 = rope_sb.t[0:BP, bi, 64:128].rearrange("p (a d) -> p a d", d=8)
                        tq = ropet
                        q4 = tq.t[0:BP, 0:256].rearrange("p (j a d) -> p j a d", j=4, d=8)
                        rk = [tk.k, rope_sb.k, tq.k]
                        tt(q4[:, 0], x4[:, :, 0:8], cs4, ALU.mult, rk, [tq.k])
                        tt(q4[:, 1], x4[:, :, 8:16], sn4, ALU.mult, rk, [tq.k])
                        tt(q4[:, 2], x4[:, :, 8:16], cs4, ALU.mult, rk, [tq.k])
                        tt(q4[:, 3], x4[:, :, 0:8], sn4, ALU.mult, rk, [tq.k])
                        tt(x4[:, :, 0:8], q4[:, 0], q4[:, 1], ALU.subtract, [tq.k, tk.k], [tk.k])
                        tt(x4[:, :, 8:16], q4[:, 2], q4[:, 3], ALU.add, [tq.k, tk.k], [tk.k])
                        tb = tmp()
                        cp(tb.b[0:BP, 0:512], tk.f[0:BP, :], [tk.k], [tb.k])
                        tbs.append(tb)
                        if ui == 1:
                            ot = otile()
                            dma(outs["k"][l][bi * BP:(bi + 1) * BP, :], tk.f[0:BP, :], [tk.k], [ot], eng="gpsimd")
                    else:
                        ot = otile()
                        dma(outs["v"][l][bi * BP:(bi + 1) * BP, :], tk.f[0:BP, :], [tk.k], [ot], eng="gpsimd")
                        cp(v_bf.t[0:BP, bi, :], tk.f[0:BP, :], [tk.k], [v_bf.k], eng="gpsimd")
                if tbs:
                    pend_tr.append((QT if ui == 0 else KTc, tbs))
            for dst, tbs in pend_tr:
                for bi, tb in enumerate(tbs):
                    pt = psn()
                    ptb = pt.t[:].bitcast(BF16)
                    for hc in range(4):
                        tr(ptb[:, hc * BP:(hc + 1) * BP], tb.b[0:BP, hc * 128:(hc + 1) * 128], identb.t[0:BP, 0:BP], [tb.k, identb.k], [pt.k])
                    evac(dst.t[:, :, bi * BP:(bi + 1) * BP], ptb[:, 0:4 * BP].rearrange("p (h t) -> p h t", h=4), [pt.k], [dst.k])
            if seq == 'p' and not is_last:
                u = t0 // 512
                dma(ktscr[l].rearrange("h p t -> p h t")[:, :, t0:t0 + TT], KTc.t[:, :, 0:TT], [KTc.k], [kt_t[l][h][u] for h in range(4)], eng="gpsimd")
                for h in range(4):
                    dma(vscr[l, h, u].rearrange("p (b e) -> p b e", b=4), v_bf.t[:, :, h * 128:(h + 1) * 128], [v_bf.k], [v_t[l][h][u]], eng="gpsimd")

            S.phase = 'gla_prep'
            ws = wload(l, "U6")
            w3 = ws.t[:, 0:8 * 528].rearrange("p (kc n) -> p kc n", kc=8)
            p = psn()
            for kc in range(8):
                mm(p.t[0:16, 0:TT], w3[:, kc, 0:16], xn.t[:, kc, 0:TT], kc == 0, kc == 7, [xn.k, ws.k], [p.k])
            evac(gab.t[0:16, 0:TT], p.t[0:16, 0:TT], [p.k], [gab.k])
            cp(lxT.t[:, :, 0:3], lxh[l].t[:, :, :], [lxh[l].k], [lxT.k], eng="gpsimd")
            for c in range(4):
                p = psn()
                for kc in range(8):
                    mm(p.t[:, 0:TT], w3[:, kc, 16 + c * 128:16 + (c + 1) * 128], xn.t[:, kc, 0:TT], kc == 0, kc == 7, [xn.k, ws.k], [p.k])
                evac(lxT.t[:, c, 3:3 + TT], p.t[:, 0:TT], [p.k], [lxT.k])
            cp(lxh[l].t[:, :, :], lxT.t[:, :, TT:TT + 3], [lxT.k], [lxh[l].k], eng="gpsimd")
            w8 = wload(l, "U8")
            w83 = w8.t[:, 0:1024].rearrange("p (c n) -> p c n", c=8)
            w7 = wload(l, "U7")
            w73 = w7.t[:, 0:4096].rearrange("p (kc n) -> p kc n", kc=8)


            S.phase = 'attn'
            acc = PS[0:4]
            sring = PS[4:8]
            sr_i = 0
            nkb_cur = (TT + 127) // 128
            pend_fin = None
            for h in range(4):
                S.phase = 'attn'

                nent = [0]

                def load_unit(u):
                    slot = kv_i[0] % 3
                    kv_i[0] += 1
                    kr, vr = kring[slot], vring[slot]
                    if seq == 'p':
                        dma(kr.t[:, :], ktscr[l, h, :, u * 512:(u + 1) * 512], [kt_t[l][h][u]], [kr.k])
                        dma(vr.t[:, :, :], vscr[l, h, u].rearrange("p (b e) -> p b e", b=4), [v_t[l][h][u]], [vr.k])
                    else:
                        ks_ = kst[0]
                        dma(ks_.t[:, :, :], ck[l, seq, u * 512:(u + 1) * 512, h * 128:(h + 1) * 128].rearrange("(b p) f -> p b f", p=128), [], [ks_.k], eng="gpsimd")
                        dma(vr.t[:, :, :], cv[l, seq, u * 512:(u + 1) * 512, h * 128:(h + 1) * 128].rearrange("(b p) f -> p b f", p=128), [], [vr.k], eng="gpsimd")
                        pt = sring[2 * (nent[0] % 2)]
                        ptb = pt.t[:].bitcast(BF16)
                        for b_ in range(4):
                            tr(ptb[:, b_ * 128:(b_ + 1) * 128], ks_.t[:, b_, :], identb.t[:, :], [ks_.k, identb.k], [pt.k])
                        evac(kr.t[:, :], ptb[:, 0:512], [pt.k], [kr.k])
                    return kr, vr

                loaded = {}

                def ensure(u):
                    if u < units_past and u not in loaded:
                        loaded[u] = load_unit(u)

                nblk = units_past * 4 + nkb_cur

                def gen_blocks():
                    for u in range(units_past):
                        ensure(u)
                        ensure(u + 1)
                        kr, vr = loaded.pop(u)
                        for kb in range(4):
                            yield (kr.t[:, kb * 128:(kb + 1) * 128], vr.t[:, kb, :], 128, None, [kr.k, vr.k])
                    for kb in range(nkb_cur):
                        nk = min(128, TT - kb * 128)
                        yield (KTc.t[:, h, kb * 128:kb * 128 + nk], v_bf.t[0:nk, kb, h * 128:(h + 1) * 128], nk,
                               kb if seq == 'p' else None, [KTc.k, v_bf.k])

                def s1a(blk):
                    ksrc, vsrc, nk, diag, rds = blk
                    par = nent[0] % 2
                    nent[0] += 1
                    sbs = []
                    for m in range(2):
                        sp_ = sring[2 * par + m]
                        mm(sp_.t[0:nk, 0:TT], ksrc[m * 64:(m + 1) * 64, :], QT.t[m * 64:(m + 1) * 64, h, 0:TT], True, True, rds + [QT.k], [sp_.k])
                        sbs.append(sp_)
                    return sbs

                def s1b(blk, sbs):
                    ksrc, vsrc, nk, diag, rds = blk
                    prs = []
                    for m in range(2):
                        sp_ = sbs[m]
                        pr = pring[pr_i[0] % 4]
                        pr_i[0] += 1
                        act(pr.t[0:nk, 0:TT], sp_.t[0:nk, 0:TT], AF.Exp, [sp_.k], [pr.k], scale=0.125)
                        if diag is not None:
                            if diag > 0:
                                mset(pr.t[0:nk, 0:diag * 128], 0.0, [pr.k], [pr.k])
                            if nk > 64:
                                mset(pr.t[64:nk, diag * 128:diag * 128 + 64], 0.0, [pr.k], [pr.k])
                        prs.append(pr)
                    return prs

                def stage2(blk, prs, bidx):
                    ksrc, vsrc, nk, diag, rds = blk
                    first = bidx == 0
                    last = bidx == nblk - 1
                    for m in range(2):
                        pr = prs[m]
                        mm(acc[m].t[:, 0:TT], vsrc, pr.t[0:nk, 0:TT], first, last, rds + [pr.k], [acc[m].k])
                        mm(acc[2 + m].t[:, 0:TT], onesb.t[0:nk, :], pr.t[0:nk, 0:TT], first, last, [onesb.k, pr.k], [acc[2 + m].k])

                def freebank():
                    return sring[2 * (nent[0] % 2)]

                ents = []
                bidx = 0
                for blk in gen_blocks():
                    ents.append([blk, s1a(blk), None])
                    n = len(ents)
                    if n >= 2:
                        e = ents[n - 2]
                        e[2] = s1b(e[0], e[1])
                        if n == 3 and pend_fin is not None:
                            pend_fin(freebank)
                            pend_fin = None
                            S.phase = 'attn'
                    if n >= 3:
                        e = ents[n - 3]
                        stage2(e[0], e[2], bidx)
                        bidx += 1
                n = len(ents)
                e = ents[n - 1]
                e[2] = s1b(e[0], e[1])
                if n >= 2:
                    e2 = ents[n - 2]
                    stage2(e2[0], e2[2], bidx)
                    bidx += 1
                stage2(e[0], e[2], bidx)
                if pend_fin is not None:
                    pend_fin(freebank)
                    pend_fin = None
                S.phase = 'attn_fin'
                r0, r1, t0_, t1_ = tmp(), tmp(), tmp(), tmp()
                act(r0.f[:, 0:TT], acc[2].t[:, 0:TT], AF.Ln, [acc[2].k], [r0.k])
                act(r1.f[:, 0:TT], acc[3].t[:, 0:TT], AF.Ln, [acc[3].k], [r1.k])
                act(r0.f[:, 0:TT], r0.f[:, 0:TT], AF.Exp, [r0.k], [r0.k], scale=-1.0)
                act(r1.f[:, 0:TT], r1.f[:, 0:TT], AF.Exp, [r1.k], [r1.k], scale=-1.0)
                tt(t0_.f[:, 0:TT], acc[0].t[:, 0:TT], r0.f[:, 0:TT], ALU.mult, [acc[0].k, r0.k], [t0_.k])
                tt(t1_.f[:, 0:TT], acc[1].t[:, 0:TT], r1.f[:, 0:TT], ALU.mult, [acc[1].k, r1.k], [t1_.k])
                stt(t0_.f[:, 0:TT], t1_.f[:, 0:TT], lamc[l].t[:, 0:1], t0_.f[:, 0:TT], ALU.mult, ALU.add, [t1_.k, t0_.k, lamc[l].k], [t0_.k])
                sq = tmp()
                act(sq.b[:, 0:TT], t0_.f[:, 0:TT], AF.Square, [t0_.k], [sq.k])

                def fin_b(bank, h=h, t0_=t0_, sq=sq):
                    S.phase = 'attn_fin'
                    pb_ = bank()
                    mm(pb_.t[:, 0:TT], onesb.t[:, :], sq.b[:, 0:TT], True, True, [onesb.k, sq.k], [pb_.k])
                    rs = tmp()
                    act(rs.f[:, 0:TT], pb_.t[:, 0:TT], AF.Ln, [pb_.k], [rs.k], bias=eps_ap, scale=1.0 / 128)
                    act(rs.f[:, 0:TT], rs.f[:, 0:TT], AF.Exp, [rs.k], [rs.k], scale=-0.5)
                    stt(oattn.t[:, h, 0:TT], t0_.f[:, 0:TT], lamc[l].t[:, 1:2], rs.f[:, 0:TT], ALU.mult, ALU.mult, [t0_.k, rs.k, lamc[l].k], [oattn.k])

                pend_fin = fin_b
            pend_fin(lambda: sring[0])
            pend_fin = None

            S.phase = 'gla_prep'
            ws = wload(l, "U4")
            w3 = ws.t[:, 0:4096].rearrange("p (kc n) -> p kc n", kc=8)
            for ck_ in range(NCK):
                p = psn()
                for kc in range(8):
                    mm(p.t[0:L, :], xn.t[:, kc, ck_ * L:(ck_ + 1) * L], w3[:, kc, :], kc == 0, kc == 7, [xn.k, ws.k], [p.k])
                evac(gv_tok.t[0:L, ck_, :], p.t[0:L, :], [p.k], [gv_tok.k])
            for h in range(4):
                p = psn()
                mm(p.t[0:64, 0:TT], w2b[l].t[0:16, h * 64:(h + 1) * 64], gab.t[0:16, 0:TT], True, True, [w2b[l].k, gab.k], [p.k])
                e1 = tmp()
                act(e1.f[0:64, 0:TT], p.t[0:64, 0:TT], AF.Exp, [p.k, pB.k], [e1.k], bias=pB.t[0:64, 46 + h:47 + h], scale=-1.0)
                act(e1.f[0:64, 0:TT], e1.f[0:64, 0:TT], AF.Ln, [e1.k], [e1.k], bias=1.0)
                scan(gcs.t[:, h, 0:TT], cmask.t[0:64, 0:TT], e1.f[0:64, 0:TT], 0.0, [cmask.k, e1.k], [gcs.k])
            ws = wload(l, "U3")
            w3 = ws.t[:, 0:4096].rearrange("p (kc n) -> p kc n", kc=8)
            for h in range(4):
                p = psn()
                for kc in range(8):
                    mm(p.t[0:64, 0:TT], w3[:, kc, h * 64:(h + 1) * 64], xn.t[:, kc, 0:TT], kc == 0, kc == 7, [xn.k, ws.k], [p.k])
                eq = tmp()
                act(eq.f[0:64, 0:TT], gcs.t[:, h, 0:TT], AF.Exp, [gcs.k], [eq.k], scale=-1.0 / 16)
                stt(qtl.t[:, h, 0:TT], p.t[0:64, 0:TT], 0.125, eq.f[0:64, 0:TT], ALU.mult, ALU.mult, [p.k, eq.k], [qtl.k])
                cp(ebl.t[:, h, 0:NCK], eq.f[0:64, 0:TT].rearrange("p (c t) -> p c t", t=L)[:, :, L - 1], [eq.k], [ebl.k])
                p = psn()
                for kc in range(8):
                    mm(p.t[0:64, 0:TT], w3[:, kc, 256 + h * 64:256 + (h + 1) * 64], xn.t[:, kc, 0:TT], kc == 0, kc == 7, [xn.k, ws.k], [p.k])
                ek = tmp()
                act(ek.f[0:64, 0:TT], gcs.t[:, h, 0:TT], AF.Exp, [gcs.k], [ek.k], scale=1.0 / 16)
                tt(ktl.t[:, h, 0:TT], p.t[0:64, 0:TT], ek.f[0:64, 0:TT], ALU.mult, [p.k, ek.k], [ktl.k])
            for ck_ in range(NCK):
                pt = psn()
                ptb = pt.t[:].bitcast(BF16)
                for h in range(4):
                    tr(ptb[0:L, h * 64:(h + 1) * 64], ktl.t[:, h, ck_ * L:(ck_ + 1) * L], identb.t[0:64, 0:64], [ktl.k, identb.k], [pt.k])
                evac(ktok.t[0:L, ck_, :], ptb[0:L, 0:256], [pt.k], [ktok.k])
            S.phase = 'gla_chunks'
            gacc = PS[0:4]
            gring = PS[4:8]
            gr_i = 0
            for ck_ in range(NCK):
                cs_ = slice(ck_ * L, (ck_ + 1) * L)
                pas, ams = [], []
                for h in range(4):
                    pa = gring[h]
                    mm(pa.t[0:L, 0:L], ktl.t[:, h, cs_], qtl.t[:, h, cs_], True, True, [ktl.k, qtl.k], [pa.k])
                    pas.append(pa)
                for h in range(4):
                    am = tmp()
                    tt(am.b[0:L, 0:L], pas[h].t[0:L, 0:L], tri.t[0:L, 0:L], ALU.mult, [pas[h].k, tri.k], [am.k])
                    ams.append(am)
                for h in range(4):
                    mm(gacc[h].t[:, cs_], gv_tok.t[0:L, ck_, h * 128:(h + 1) * 128], ams[h].b[0:L, 0:L], True, False, [gv_tok.k, ams[h].k], [gacc[h].k])
                    mm(gacc[h].t[:, cs_], gSb[l].t[:, h, :], qtl.t[:, h, cs_], False, True, [gSbk[l][h], qtl.k], [gacc[h].k])
                for h in range(4):
                    pu = gring[h]
                    mm(pu.t[0:64, 0:128], ktok.t[0:L, ck_, h * 64:(h + 1) * 64], gv_tok.t[0:L, ck_, h * 128:(h + 1) * 128], True, True, [ktok.k, gv_tok.k], [pu.k])
                for h in range(4):
                    pu = gring[h]
                    tt(gS[l].t[:, h, :], gS[l].t[:, h, :], pu.t[0:64, 0:128], ALU.add, [gSk[l][h], pu.k], [gSk[l][h]])
                    ts(gS[l].t[:, h, :], gS[l].t[:, h, :], ebl.t[:, h, ck_:ck_ + 1], None, ALU.mult, None, [gSk[l][h], ebl.k], [gSk[l][h]])
                    evac(gSb[l].t[:, h, :], gS[l].t[:, h, :], [gSk[l][h]], [gSbk[l][h]])
            gr_i = 0
            S.phase = 'gla_out'
            ws = wload(l, "U5")
            w3 = ws.t[:, 0:4096].rearrange("p (kc n) -> p kc n", kc=8)
            gfs, sqs, sls = [], [], []
            for h in range(4):
                gf = tmp()
                act(gf.f[:, 0:TT], gacc[h].t[:, 0:TT], AF.Copy, [gacc[h].k], [gf.k])
                gfs.append(gf)
            for h in range(4):
                sq = tmp()
                act(sq.b[:, 0:TT], gfs[h].f[:, 0:TT], AF.Square, [gfs[h].k], [sq.k])
                sqs.append(sq)
            pss = []
            for h in range(4):
                p = gring[h]
                mm(p.t[:, 0:TT], onesb.t[:, :], sqs[h].b[:, 0:TT], True, True, [onesb.k, sqs[h].k], [p.k])
                pss.append(p)
            for h in range(4):
                rs = sqs[h]
                act(rs.f[:, 0:TT], pss[h].t[:, 0:TT], AF.Ln, [pss[h].k, rs.k], [rs.k], bias=eps_ap, scale=1.0 / 128)
            for h in range(4):
                rs = sqs[h]
                act(rs.f[:, 0:TT], rs.f[:, 0:TT], AF.Exp, [rs.k], [rs.k], scale=-0.5)
            for h in range(4):
                stt(gfs[h].f[:, 0:TT], gfs[h].f[:, 0:TT], pB.t[:, 33:34], sqs[h].f[:, 0:TT], ALU.mult, ALU.mult, [gfs[h].k, sqs[h].k, pB.k], [gfs[h].k])
            for h in range(4):
                p = gring[h]
                for kc in range(8):
                    mm(p.t[:, 0:TT], w3[:, kc, h * 128:(h + 1) * 128], xn.t[:, kc, 0:TT], kc == 0, kc == 7, [xn.k, ws.k], [p.k])
            for h in range(4):
                sl = tmp()
                act(sl.f[:, 0:TT], gring[h].t[:, 0:TT], AF.Silu, [gring[h].k], [sl.k])
                sls.append(sl)
            for h in range(4):
                tt(gon.t[:, h, 0:TT], gfs[h].f[:, 0:TT], sls[h].f[:, 0:TT], ALU.mult, [gfs[h].k, sls[h].k], [gon.k])

            S.phase = 'lru'
            for half in range(2):
                cs2 = [2 * half, 2 * half + 1]
                T1 = {c: tmp() for c in cs2}
                T2 = {c: tmp() for c in cs2}
                T3 = {c: tmp() for c in cs2}
                T4 = {c: tmp() for c in cs2}
                rk = [lxT.k, pB.k]
                for c in cs2:
                    xc = T1[c]
                    ts(xc.f[:, 0:TT], lxT.t[:, c, 0:TT], pB.t[:, c:c + 1], pB.t[:, 16 + c:17 + c], ALU.mult, ALU.add, rk, [xc.k])
                for j in range(1, 4):
                    for c in cs2:
                        xc = T1[c]
                        stt(xc.f[:, 0:TT], lxT.t[:, c, j:j + TT], pB.t[:, 4 * j + c:4 * j + c + 1], xc.f[:, 0:TT], ALU.mult, ALU.add, rk + [xc.k], [xc.k])
                for c in cs2:
                    cp(T2[c].b[:, 0:TT], T1[c].f[:, 0:TT], [T1[c].k], [T2[c].k], eng="gpsimd")
                prs_, pis_ = {}, {}
                for c in cs2:
                    prs_[c] = psn()
                    mm(prs_[c].t[:, 0:TT], w83[:, c, :], T2[c].b[:, 0:TT], True, True, [w8.k, T2[c].k], [prs_[c].k])
                    pis_[c] = psn()
                    mm(pis_[c].t[:, 0:TT], w83[:, 4 + c, :], T2[c].b[:, 0:TT], True, True, [w8.k, T2[c].k], [pis_[c].k])
                for c in cs2:
                    act(T3[c].f[:, 0:TT], prs_[c].t[:, 0:TT], AF.Sigmoid, [prs_[c].k, pB.k], [T3[c].k], bias=pB.t[:, 20 + c:21 + c])
                    act(T4[c].f[:, 0:TT], pis_[c].t[:, 0:TT], AF.Sigmoid, [pis_[c].k, pB.k], [T4[c].k], bias=pB.t[:, 24 + c:25 + c])
                for c in cs2:
                    act(T3[c].f[:, 0:TT], T3[c].f[:, 0:TT], AF.Exp, [T3[c].k, lamc[l].k], [T3[c].k], scale=lamc[l].t[:, 2 + c:3 + c])
                for c in cs2:
                    tt(T2[c].f[:, 0:TT], T3[c].f[:, 0:TT], T3[c].f[:, 0:TT], ALU.mult, [T3[c].k, T2[c].k], [T2[c].k])
                    ts(T2[c].f[:, 0:TT], T2[c].f[:, 0:TT], -1.0, 1.0, ALU.mult, ALU.add, [T2[c].k], [T2[c].k])
                    ts(T2[c].f[:, 0:TT], T2[c].f[:, 0:TT], 1e-18, None, ALU.max, None, [T2[c].k], [T2[c].k])
                    tt(T4[c].f[:, 0:TT], T4[c].f[:, 0:TT], T1[c].f[:, 0:TT], ALU.mult, [T4[c].k, T1[c].k], [T4[c].k])
                for c in cs2:
                    act(T2[c].f[:, 0:TT], T2[c].f[:, 0:TT], AF.Ln, [T2[c].k], [T2[c].k])
                for c in cs2:
                    act(T2[c].f[:, 0:TT], T2[c].f[:, 0:TT], AF.Exp, [T2[c].k], [T2[c].k], scale=0.5)
                for c in cs2:
                    tt(T4[c].f[:, 0:TT], T4[c].f[:, 0:TT], T2[c].f[:, 0:TT], ALU.mult, [T4[c].k, T2[c].k], [T4[c].k])
                for c in cs2:
                    scan(T1[c].f[:, 0:TT], T3[c].f[:, 0:TT], T4[c].f[:, 0:TT], hst[l].t[:, c:c + 1], [T3[c].k, T4[c].k, hst[l].k, T1[c].k], [T1[c].k])
                    cp(hst[l].t[:, c:c + 1], T1[c].f[:, TT - 1:TT], [T1[c].k], [hst[l].k])
                pgs = {}
                for c in cs2:
                    pgs[c] = psn()
                    for kc in range(8):
                        mm(pgs[c].t[:, 0:TT], w73[:, kc, c * 128:(c + 1) * 128], xn.t[:, kc, 0:TT], kc == 0, kc == 7, [xn.k, w7.k], [pgs[c].k])
                for c in cs2:
                    act(T2[c].f[:, 0:TT], pgs[c].t[:, 0:TT], AF.Gelu_apprx_tanh, [pgs[c].k, T2[c].k], [T2[c].k])
                for c in cs2:
                    tt(hl.t[:, c, 0:TT], T1[c].f[:, 0:TT], T2[c].f[:, 0:TT], ALU.mult, [T1[c].k, T2[c].k], [hl.k])

            S.phase = 'merge'
            for m in range(8):
                ws = wload(l, f"M{m}")
                wsb = wload(l, f"B{m}")
                wm4 = ws.t[:, 0:3072].rearrange("p (kc j f) -> p kc j f", kc=8, j=3)
                wb4 = wsb.t[:, 0:1536].rearrange("p (kc j f) -> p kc j f", kc=4, j=3)
                srcs = [oattn, gon, hl]
                acc_t = None
                for j in range(3):
                    pg = psn()
                    for kc in range(8):
                        mm(pg.t[:, 0:TT], wm4[:, kc, j, :], xn.t[:, kc, 0:TT], kc == 0, kc == 7, [xn.k, ws.k], [pg.k])
                    g = tmp()
                    act(g.f[:, 0:TT], pg.t[:, 0:TT], AF.Sigmoid, [pg.k, pA.k], [g.k], bias=pA.t[:, 88 + j * 8 + m:89 + j * 8 + m])
                    py = psn()
                    for kc in range(4):
                        mm(py.t[:, 0:TT], wb4[:, kc, j, :], srcs[j].t[:, kc, 0:TT], kc == 0, kc == 3, [srcs[j].k, wsb.k], [py.k])
                    if j == 0:
                        acc_t = tmp()
                        tt(acc_t.f[:, 0:TT], g.f[:, 0:TT], py.t[:, 0:TT], ALU.mult, [g.k, py.k], [acc_t.k])
                    else:
                        tt(g.f[:, 0:TT], g.f[:, 0:TT], py.t[:, 0:TT], ALU.mult, [g.k, py.k], [g.k])
                        if j == 1:
                            tt(acc_t.f[:, 0:TT], acc_t.f[:, 0:TT], g.f[:, 0:TT], ALU.add, [g.k, acc_t.k], [acc_t.k])
                        else:
                            tt(mergedb.t[:, m, 0:TT], acc_t.f[:, 0:TT], g.f[:, 0:TT], ALU.add, [g.k, acc_t.k], [mergedb.k])
            S.phase = 'wout'
            for i in range(2):
                ws = wload(l, f"O{i}")
                w3 = ws.t[:, 0:4096].rearrange("p (kc n) -> p kc n", kc=8)
                for mi in range(4):
                    m = i * 4 + mi
                    p = psn()
                    for kc in range(8):
                        mm(p.t[:, 0:TT], w3[:, kc, mi * 128:(mi + 1) * 128], mergedb.t[:, kc, 0:TT], kc == 0, kc == 7, [mergedb.k, ws.k], [p.k])
                    tt(xT.t[:, m, 0:TT], xT.t[:, m, 0:TT], p.t[:, 0:TT], ALU.add, [xT.k, p.k], [xT.k])

            S.phase = 'ffn_gu'
            rmsnorm_to(xn, lambda c: pA.t[:, 120 + c:121 + c])
            for i in range(11):
                ws = wload(l, f"F{i}")
                w4 = ws.t[:, 0:4096].rearrange("p (kc g n) -> p kc g n", kc=8, g=2)
                for s_ in range(2):
                    j = 2 * i + s_
                    pg = psn()
                    for kc in range(8):
                        mm(pg.t[:, 0:TT], w4[:, kc, 0, s_ * 128:(s_ + 1) * 128], xn.t[:, kc, 0:TT], kc == 0, kc == 7, [xn.k, ws.k], [pg.k])
                    pu = psn()
                    for kc in range(8):
                        mm(pu.t[:, 0:TT], w4[:, kc, 1, s_ * 128:(s_ + 1) * 128], xn.t[:, kc, 0:TT], kc == 0, kc == 7, [xn.k, ws.k], [pu.k])
                    gu = tmp()
                    gu2 = tmp()
                    act(gu.f[:, 0:TT], pg.t[:, 0:TT], AF.Copy, [pg.k], [gu.k])
                    gc = gu2
                    act(gc.f[:, 0:TT], pg.t[:, 0:TT], AF.Identity, [pg.k, pA.k], [gc.k], bias=pA.t[:, 66 + j:67 + j], scale=pA.t[:, 44 + j:45 + j])
                    if TT > 1:
                        stt(gc.f[:, 1:TT], gu.f[:, 0:TT - 1], pA.t[:, 22 + j:23 + j], gc.f[:, 1:TT], ALU.mult, ALU.add, [gu.k, gc.k, pA.k], [gc.k])
                    stt(gc.f[:, 0:1], guh[l].t[:, j, 1:2], pA.t[:, 22 + j:23 + j], gc.f[:, 0:1], ALU.mult, ALU.add, [guh[l].k, gc.k, pA.k], [gc.k])
                    stt(gc.f[:, 2:TT], gu.f[:, 0:TT - 2], pA.t[:, j:j + 1], gc.f[:, 2:TT], ALU.mult, ALU.add, [gu.k, gc.k, pA.k], [gc.k])
                    stt(gc.f[:, 0:2], guh[l].t[:, j, 0:2], pA.t[:, j:j + 1], gc.f[:, 0:2], ALU.mult, ALU.add, [guh[l].k, gc.k, pA.k], [gc.k])
                    cp(guh[l].t[:, j, :], gu.f[:, TT - 2:TT], [gu.k, gc.k], [guh[l].k], eng="gpsimd")
                    act(gc.f[:, 0:TT], gc.f[:, 0:TT], AF.Gelu_apprx_tanh, [gc.k], [gc.k])
                    tt(fT_t[:, j, 0:TT], gc.f[:, 0:TT], pu.t[:, 0:TT], ALU.mult, [gc.k, pu.k], [fT[j].k])
            S.phase = 'ffn_down'
            for m in range(8):
                ws = wload(l, f"D{m}")
                w3 = ws.t[:, 0:NFC * 128].rearrange("p (kc n) -> p kc n", kc=NFC)
                p = psn()
                for kc in range(NFC):
                    mm(p.t[:, 0:TT], w3[:, kc, :], fT_t[:, kc, 0:TT], kc == 0, kc == NFC - 1, [fT[kc].k, ws.k], [p.k])
                tt(xT.t[:, m, 0:TT], xT.t[:, m, 0:TT], p.t[:, 0:TT], ALU.add, [xT.k, p.k], [xT.k])

            S.phase = 'stateout'
            if is_last:
                ot = otile()
                dma(outs["gla"][l].rearrange("h k v -> k h v"), gS[l].t[:, :, :], gSk[l], [ot], eng="gpsimd")
                for j in range(3):
                    ot = otile()
                    dma(outs["lc"][l][j].rearrange("(c p) -> p c", p=128), lxh[l].t[:, :, j], [lxh[l].k], [ot], allow_slow_non_contiguous=True)
                ot = otile()
                dma(outs["lh"][l].rearrange("(c p) -> p c", p=128), hst[l].t[:, :], [hst[l].k], [ot], allow_slow_non_contiguous=True)
                for j in range(2):
                    ot = otile()
                    dma(outs["fc"][l][j].rearrange("(c p) -> p c", p=128), guh[l].t[:, :, j], [guh[l].k], [ot], allow_slow_non_contiguous=True)

        S.phase = 'yout'
        p = psn()
        for c in range(8):
            sq = tmp()
            act(sq.b[:, 0:TT], xT.t[:, c, 0:TT], AF.Square, [xT.k], [sq.k])
            mm(p.t[:, 0:TT], onesb.t[:, :], sq.b[:, 0:TT], c == 0, c == 7, [onesb.k, sq.k], [p.k])
        rsf = lxT.t[:, 0, 0:TT]
        act(rsf, p.t[:, 0:TT], AF.Ln, [p.k], [lxT.k], bias=eps_ap, scale=1.0 / D)
        act(rsf, rsf, AF.Exp, [lxT.k], [lxT.k], scale=-0.5)
        base2, yk = tmp8()

        def ytok(bi):
            return pool_t[0:BP, base2 + 2 * bi: base2 + 2 * bi + 2, :].rearrange("p a n -> p (a n)")

        for c in range(8):
            yc = tmp()
            stt(yc.f[:, 0:TT], xT.t[:, c, 0:TT], parB[0].t[:, 38 + c:39 + c], rsf, ALU.mult, ALU.mult, [xT.k, lxT.k, parB[0].k], [yc.k])
            pt = psn()
            for bi in range(NB):
                tr(pt.t[0:BP, bi * 128:(bi + 1) * 128], yc.f[:, bi * BP:(bi + 1) * BP], identf.t[:, :], [yc.k, identf.k], [pt.k])
            for bi in range(NB):
                evac(ytok(bi)[:, c * 128:(c + 1) * 128], pt.t[0:BP, bi * 128:(bi + 1) * 128], [pt.k], [yk[2 * bi], yk[2 * bi + 1]])
        for bi in range(NB):
            ot = otile()
            dma(ysrc[bi * BP:(bi + 1) * BP, :], ytok(bi), [yk[2 * bi], yk[2 * bi + 1]], [ot], eng="gpsimd")

    epsb = mk("epsb", [128, 1], F32)
    mset(epsb.t[:], EPS, [], [epsb.k])
    eps_ap = epsb.t[:, 0:1]
    _orig_act = act

    def act(out, in_, func, reads, writes, bias=None, scale=None):
        if bias is eps_ap:
            reads = list(reads) + [epsb.k]
        _orig_act(out, in_, func, reads, writes, bias=bias, scale=scale)

    if cfg["samples"]:
        for s in range(2):
            outs = {
                "k": [ks_o[l, s] for l in range(DEPTH)], "v": [vs_o[l, s] for l in range(DEPTH)],
                "gla": [glas[l, s] for l in range(DEPTH)], "lc": [lcs[l, s] for l in range(DEPTH)],
                "lh": [lhs_o[l, s] for l in range(DEPTH)], "fc": [fcs[l, s] for l in range(DEPTH)],
            }
            run_tile(s, DSEQ, DSEQ, 1, 32, 1, xs[s], ys[s], ropes, 0, PAST // 512, True, outs)
    npt = cfg["n_ptiles"]
    for ti in range(npt):
        t0 = ti * TTP
        outs = {
            "k": [kp[l, t0:t0 + TTP] for l in range(DEPTH)], "v": [vp[l, t0:t0 + TTP] for l in range(DEPTH)],
            "gla": [glap[l] for l in range(DEPTH)], "lc": [lcp[l] for l in range(DEPTH)],
            "lh": [lhp[l] for l in range(DEPTH)], "fc": [fcp[l] for l in range(DEPTH)],
        }
        run_tile('p', TTP, 128, 4, 64, 8, xp[t0:t0 + TTP], yp[t0:t0 + TTP], ropep[t0:t0 + TTP], t0, ti, ti == npt - 1, outs)

    S.op("sync", lambda e: e.nop(), reads=out_tiles)
    S.emit()
    return nc, S


def _rope_table(pos):
    half = 8
    inv = (np.float32(500000.0) ** (-np.arange(half, dtype=np.float32) / np.float32(half))).astype(np.float32)
    ang = (pos.astype(np.float32)[:, None] * inv[None, :]).astype(np.float32)
    cos = np.cos(ang).astype(np.float32)
    sin = np.sin(ang).astype(np.float32)
    return np.concatenate([np.tile(cos, (1, 8)), np.tile(sin, (1, 8))], axis=1).astype(np.float32)


_WNAMES = ["norm_mix", "w_in", "lambda_qk", "attn_subln", "w_gla_gate2", "b_gla_gate", "gla_norm", "lru_conv_w",
           "lru_conv_b", "lru_wa", "lru_ba", "lru_wx", "lru_bx", "lru_lambda", "w_branch_attn", "w_branch_gla",
           "w_branch_lru", "w_merge", "b_merge", "w_out", "norm_ffn", "w_ffn_gate", "ffn_conv_w", "ffn_conv_b",
           "w_ffn_up", "w_ffn_down", "norm_final"]


def kernel(**inp):
    cfg = dict(CFG)
    nc, S = build_program(cfg)
    f = lambda a: np.ascontiguousarray(np.asarray(a, dtype=np.float32))
    NLc = cfg['depth']
    wts = {n: (f(inp[n]) if n == 'norm_final' else f(np.asarray(inp[n])[:NLc])) for n in _WNAMES}
    consts = {
        "ropep": _rope_table(np.arange(SEQ)),
        "ropes": _rope_table(PAST + np.arange(DSEQ)),
        "cmask": (np.arange(TTP) % 64 != 0).astype(np.float32)[None, :],
        "tri": np.triu(np.ones((64, 64), np.float32)),
        "ident": np.eye(128, dtype=np.float32),
    }
    x_prompt = f(inp["x_prompt"])
    x_sample = f(inp["x_sample"])
    ckf = np.asarray(inp["cache_attn_k"], dtype=np.float32).reshape(DEPTH, 16, PAST, 512)
    cvf = np.asarray(inp["cache_attn_v"], dtype=np.float32).reshape(DEPTH, 16, PAST, 512)
    sgf = f(inp["state_gla"])
    slcf = f(inp["state_lru_conv"])
    slhf = f(inp["state_lru_h"])
    sfcf = f(inp["state_ffn_conv"])
    PCORES = [0, 1, 4, 5]
    zero_x = np.zeros((SEQ, D), np.float32)
    in_maps = []
    for c in range(8):
        s0 = 2 * c
        m = {
            "xp": x_prompt[PCORES.index(c)] if c in PCORES else zero_x,
            "xs": x_sample[s0:s0 + 2],
            "ck": np.ascontiguousarray(ckf[:, s0:s0 + 2]),
            "cv": np.ascontiguousarray(cvf[:, s0:s0 + 2]),
            "sg": np.ascontiguousarray(sgf[:, s0:s0 + 2]),
            "slc": np.ascontiguousarray(slcf[:, s0:s0 + 2]),
            "slh": np.ascontiguousarray(slhf[:, s0:s0 + 2]),
            "sfc": np.ascontiguousarray(sfcf[:, s0:s0 + 2]),
        }
        m.update(wts)
        m.update(consts)
        in_maps.append(m)
    res = run_bass_kernel_spmd(nc, in_maps, core_ids=list(range(8))).results
    B = 4
    y_prompt = np.stack([res[b]["yp"] for b in PCORES])
    k_p = np.stack([res[b]["kp"] for b in PCORES], axis=1).reshape(DEPTH, B, SEQ, 4, 2, 64)
    v_p = np.stack([res[b]["vp"] for b in PCORES], axis=1).reshape(DEPTH, B, SEQ, 4, 128)
    gla_p = np.stack([res[b]["glap"] for b in PCORES], axis=1)
    lc_p = np.stack([res[b]["lcp"] for b in PCORES], axis=1)
    lh_p = np.stack([res[b]["lhp"] for b in PCORES], axis=1)
    fc_p = np.stack([res[b]["fcp"] for b in PCORES], axis=1)
    y_sample = np.concatenate([res[c]["ys"] for c in range(8)], axis=0)
    k_s = np.concatenate([res[c]["ks"] for c in range(8)], axis=1).reshape(DEPTH, 16, DSEQ, 4, 2, 64)
    v_s = np.concatenate([res[c]["vs"] for c in range(8)], axis=1).reshape(DEPTH, 16, DSEQ, 4, 128)
    gla_s = np.concatenate([res[c]["glas"] for c in range(8)], axis=1)
    lc_s = np.concatenate([res[c]["lcs"] for c in range(8)], axis=1)
    lh_s = np.concatenate([res[c]["lhs"] for c in range(8)], axis=1)
    fc_s = np.concatenate([res[c]["fcs"] for c in range(8)], axis=1)
    outs = (y_prompt, y_sample, k_p, v_p, gla_p, lc_p, lh_p, fc_p, k_s, v_s, gla_s, lc_s, lh_s, fc_s)
    return tuple(np.ascontiguousarray(o, dtype=np.float32) for o in outs)
```

```python
import math
from contextlib import ExitStack

import numpy as np
import concourse.bass as bass
import concourse.mybir as mybir
from concourse.bass_utils import run_bass_kernel_spmd

F32 = mybir.dt.float32
BF16 = mybir.dt.bfloat16
AF = mybir.ActivationFunctionType
ALU = mybir.AluOpType

ENGS = ["tensor", "vector", "scalar", "gpsimd", "sync"]

CFG = {"n_ptiles": 16, "depth": 4, "samples": True}

D = 1024
DEPTH = 4
SEQ = 8192
TTP = 512
PAST = 2048
DSEQ = 32
EPS = 1e-6
AQ, AK, AV, GQ, GK, GV, GR, GA, LX, LG, PW = 0, 512, 1024, 1536, 1792, 2048, 2560, 3072, 3088, 3600, 4112
DFF = 2816
NFC = 22
WSLOT = 4224


class Tile:
    __slots__ = ("name", "last_w", "readers", "psum")

    def __init__(self, name):
        self.name = name
        self.last_w = None
        self.readers = []
        self.psum = False


class Op:
    __slots__ = ("eng", "fn", "deps", "dma", "signal", "sem", "val", "idx", "phase")


class Sched:
    RING = 12

    def __init__(self, nc):
        self.nc = nc
        self.ops = []
        self.stack = ExitStack()
        self.ntile = 0
        self.phase = 'init'

    def sbuf(self, name, shape, dtype):
        return self.stack.enter_context(self.nc.sbuf_tensor("sb_" + name, list(shape), dtype))

    def psum(self, name, shape, dtype):
        return self.stack.enter_context(self.nc.psum_tensor("pp_" + name, list(shape), dtype))

    def tile(self, name=None):
        self.ntile += 1
        return Tile(name or f"t{self.ntile}")

    def op(self, eng, fn, reads=(), writes=(), dma=False):
        o = Op()
        o.eng = eng
        o.fn = fn
        o.dma = dma
        o.signal = False
        o.sem = None
        o.val = 0
        o.idx = len(self.ops)
        o.phase = self.phase
        deps = set()
        ops = self.ops
        for t in reads:
            if t.last_w is not None:
                deps.add(t.last_w)
            if t.psum:
                for r in t.readers:
                    if ops[r].eng != eng:
                        deps.add(r)
        for t in writes:
            if t.last_w is not None:
                deps.add(t.last_w)
            for r in t.readers:
                ro = ops[r]
                if (not dma) and (not ro.dma) and ro.eng == eng:
                    continue
                deps.add(r)
        if eng == "tensor" and not dma:
            deps = {d for d in deps if ops[d].dma or ops[d].eng != "tensor"}
        o.deps = deps
        for t in reads:
            if not dma:
                t.readers = [r for r in t.readers if ops[r].dma or ops[r].eng != eng]
            t.readers.append(o.idx)
        for t in writes:
            t.last_w = o.idx
            t.readers = []
        ops.append(o)
        return o

    def emit(self):
        nc = self.nc
        ops = self.ops
        stack = self.stack
        dma_count = {e: 0 for e in ENGS}
        dma_hist = {e: [] for e in ENGS}
        for o in ops:
            if o.dma:
                j = dma_count[o.eng]
                if j >= self.RING:
                    o.deps.add(dma_hist[o.eng][j - self.RING])
                dma_hist[o.eng].append(o.idx)
                dma_count[o.eng] += 1
        for o in ops:
            for d in o.deps:
                ops[d].signal = True
        eng_sem = {e: stack.enter_context(nc.semaphore(f"s_{e}")) for e in ENGS}
        ring_sems = {e: [stack.enter_context(nc.semaphore(f"d_{e}_{i}")) for i in range(self.RING)]
                     for e in ENGS if dma_count[e] > 0}
        cnt = {e: 0 for e in ENGS}
        dj = {e: 0 for e in ENGS}
        for o in ops:
            if o.dma:
                j = dj[o.eng]
                o.sem = ring_sems[o.eng][j % self.RING]
                o.val = 16 * (j // self.RING + 1)
                dj[o.eng] += 1
            elif o.signal:
                cnt[o.eng] += 1
                o.sem = eng_sem[o.eng]
                o.val = cnt[o.eng]
        per_eng = {e: [] for e in ENGS}
        for o in ops:
            per_eng[o.eng].append(o)
        self.stats = {e: len(per_eng[e]) for e in ENGS}
        block = stack.enter_context(nc.Block())

        def make(ename):
            def body(eng):
                waited = {}
                for o in per_eng[ename]:
                    need = {}
                    for d in o.deps:
                        do = ops[d]
                        k = id(do.sem)
                        if k not in need or need[k][1] < do.val:
                            need[k] = (do.sem, do.val)
                    for k, (sem, val) in need.items():
                        if waited.get(k, 0) >= val:
                            continue
                        eng.wait_ge(sem, val)
                        waited[k] = val
                    ins = o.fn(eng)
                    if o.dma:
                        ins.then_inc(o.sem, 16)
                    elif o.signal:
                        ins.then_inc(o.sem, 1)
            return body

        for e in ENGS:
            if per_eng[e]:
                getattr(block, e)(make(e))
        stack.close()


class Buf:
    __slots__ = ("t", "k")

    def __init__(self, t, k):
        self.t = t
        self.k = k


def build_program(cfg):
    nc = bass.Bass("TRN2", target_bir_lowering=False)
    S = Sched(nc)
    NL = cfg["depth"]

    def din(name, shape, dt=F32):
        return nc.dram_tensor(name, list(shape), dt, kind="ExternalInput").ap()

    def dout(name, shape, dt=F32):
        return nc.dram_tensor(name, list(shape), dt, kind="ExternalOutput").ap()

    def dscr(name, shape, dt=BF16):
        return nc.dram_tensor(name, list(shape), dt).ap()

    xp = din("xp", [SEQ, D])
    xs = din("xs", [2, DSEQ, D])
    ck = din("ck", [DEPTH, 2, PAST, 512])
    cv = din("cv", [DEPTH, 2, PAST, 512])
    sg = din("sg", [DEPTH, 2, 4, 64, 128])
    slc = din("slc", [DEPTH, 2, 3, 512])
    slh = din("slh", [DEPTH, 2, 512])
    sfc = din("sfc", [DEPTH, 2, 2, DFF])
    W = {}
    for name, shape in [
        ("norm_mix", [NL, D]), ("w_in", [NL, D, PW]), ("lambda_qk", [NL, 4, 64]),
        ("attn_subln", [NL, 128]), ("w_gla_gate2", [NL, 16, 256]), ("b_gla_gate", [NL, 256]),
        ("gla_norm", [NL, 128]), ("lru_conv_w", [NL, 4, 512]), ("lru_conv_b", [NL, 512]),
        ("lru_wa", [NL, 8, 64, 64]), ("lru_ba", [NL, 512]), ("lru_wx", [NL, 8, 64, 64]),
        ("lru_bx", [NL, 512]), ("lru_lambda", [NL, 512]), ("w_branch_attn", [NL, 512, D]),
        ("w_branch_gla", [NL, 512, D]), ("w_branch_lru", [NL, 512, D]), ("w_merge", [NL, D, 3 * D]),
        ("b_merge", [NL, 3 * D]), ("w_out", [NL, D, D]), ("norm_ffn", [NL, D]),
        ("w_ffn_gate", [NL, D, DFF]), ("ffn_conv_w", [NL, 3, DFF]), ("ffn_conv_b", [NL, DFF]),
        ("w_ffn_up", [NL, D, DFF]), ("w_ffn_down", [NL, DFF, D]), ("norm_final", [D]),
    ]:
        W[name] = din(name, shape)
    ropep = din("ropep", [SEQ, 128])
    ropes = din("ropes", [DSEQ, 128])
    cmask_d = din("cmask", [1, TTP])
    tri_d = din("tri", [64, 64])
    ident_d = din("ident", [128, 128])

    yp = dout("yp", [SEQ, D])
    ys = dout("ys", [2, DSEQ, D])
    kp = dout("kp", [DEPTH, SEQ, 512])
    vp = dout("vp", [DEPTH, SEQ, 512])
    glap = dout("glap", [DEPTH, 4, 64, 128])
    lcp = dout("lcp", [DEPTH, 3, 512])
    lhp = dout("lhp", [DEPTH, 512])
    fcp = dout("fcp", [DEPTH, 2, DFF])
    ks_o = dout("ks", [DEPTH, 2, DSEQ, 512])
    vs_o = dout("vs", [DEPTH, 2, DSEQ, 512])
    glas = dout("glas", [DEPTH, 2, 4, 64, 128])
    lcs = dout("lcs", [DEPTH, 2, 3, 512])
    lhs_o = dout("lhs", [DEPTH, 2, 512])
    fcs = dout("fcs", [DEPTH, 2, 2, DFF])
    out_tiles = []

    def otile():
        t = S.tile()
        out_tiles.append(t)
        return t

    ktscr = dscr("ktscr", [DEPTH, 4, 128, SEQ])
    vscr = dscr("vscr", [DEPTH, 4, SEQ // 512, 128, 512])
    kt_t = [[[S.tile() for u in range(16)] for h in range(4)] for l in range(DEPTH)]
    v_t = [[[S.tile() for u in range(16)] for h in range(4)] for l in range(DEPTH)]

    units = {}

    def mk(name, shape, dt):
        return Buf(S.sbuf(name, shape, dt), S.tile(name))

    xT = mk("xT", [128, 8, TTP], F32)
    NP = 16
    pool_t = S.sbuf("pool", [128, NP, TTP], F32)
    pool_k = [S.tile(f"pool{i}") for i in range(NP)]
    pool_i = [0]

    class Tmp:
        __slots__ = ("f", "b", "k")

    def tmp():
        i = pool_i[0] % NP
        pool_i[0] += 1
        r = Tmp()
        r.f = pool_t[:, i, :]
        r.b = pool_t[:, i, :].bitcast(BF16)
        r.k = pool_k[i]
        return r

    def tmp8():
        if (pool_i[0] % NP) + 8 > NP:
            pool_i[0] += NP - (pool_i[0] % NP)
        base = pool_i[0] % NP
        ks_ = [tmp().k for _ in range(8)]
        return base, ks_

    xn = mk("xn", [128, 8, TTP], BF16)
    NWR = 5
    wring = [mk(f"wring{i}", [128, WSLOT], BF16) for i in range(NWR)]
    wr_i = [0]
    QT = mk("QT", [128, 4, TTP], BF16)
    KTc = mk("KTc", [128, 4, TTP], BF16)
    qtl = mk("qtl", [64, 4, TTP], BF16)
    ktl = mk("ktl", [64, 4, TTP], BF16)
    v_bf = mk("v_bf", [128, 4, 512], BF16)
    gv_tok = mk("gv_tok", [64, 8, 512], BF16)
    ktok = mk("ktok", [64, 8, 256], BF16)
    fT = [Buf(None, S.tile(f"fT{j}")) for j in range(NFC)]
    fT_t = S.sbuf("fT", [128, NFC, TTP], BF16)
    mergedb = Buf(fT_t[:, 0:8, :], S.tile("mergedb"))
    oattn = Buf(fT_t[:, 8:12, :], S.tile("oattn"))
    gon = Buf(fT_t[:, 12:16, :], S.tile("gon"))
    hl = Buf(fT_t[:, 16:20, :], S.tile("hl"))
    for j in range(NFC):
        if j < 8:
            fT[j].k = mergedb.k
        elif j < 12:
            fT[j].k = oattn.k
        elif j < 16:
            fT[j].k = gon.k
        elif j < 20:
            fT[j].k = hl.k
    pring = [mk(f"pring{i}", [128, TTP], BF16) for i in range(4)]
    pr_i = [0]
    kring = [mk(f"kring{i}", [128, 512], BF16) for i in range(3)]
    vring = [mk(f"vring{i}", [128, 4, 128], BF16) for i in range(3)]
    kv_i = [0]
    kst = [mk(f"kst{i}", [128, 4, 128], BF16) for i in range(1)]
    kst_i = [0]
    gS = [mk(f"gS{l}", [64, 4, 128], F32) for l in range(DEPTH)]
    gSb = [mk(f"gSb{l}", [64, 4, 128], BF16) for l in range(DEPTH)]
    gSk = [[S.tile(f"gSk{l}_{h}") for h in range(4)] for l in range(DEPTH)]
    gSbk = [[S.tile(f"gSbk{l}_{h}") for h in range(4)] for l in range(DEPTH)]
    hst = [mk(f"hst{l}", [128, 4], F32) for l in range(DEPTH)]
    lxh = [mk(f"lxh{l}", [128, 4, 3], F32) for l in range(DEPTH)]
    guh = [mk(f"guh{l}", [128, NFC, 2], F32) for l in range(DEPTH)]
    parA = [mk(f"parA{l}", [128, 128], F32) for l in range(DEPTH)]
    parB = [mk(f"parB{l}", [128, 64], F32) for l in range(DEPTH)]
    lamc = [mk(f"lamc{l}", [128, 8], F32) for l in range(DEPTH)]
    w2b = [mk(f"w2b{l}", [16, 256], BF16) for l in range(DEPTH)]
    identf = mk("identf", [128, 128], F32)
    identb = mk("identb", [128, 128], BF16)
    onesb = mk("onesb", [128, 128], BF16)
    tri = mk("tri", [64, 64], F32)
    cmask = mk("cmask_sb", [128, TTP], F32)
    rope_sb = mk("rope_sb", [128, 4, 128], F32)
    lxT = mk("lxT", [128, 4, 3 + TTP], F32)
    gcs = mk("gcs", [64, 4, TTP], F32)
    ropet = mk("ropet", [128, 256], F32)
    gab = mk("gab", [16, TTP], BF16)
    ebl = mk("ebl", [64, 4, 8], F32)

    PS = [Buf(S.psum(f"ps{i}", [128, 512], F32), S.tile(f"ps{i}")) for i in range(8)]
    for b_ in PS:
        b_.k.psum = True
    ps_i = [0]

    def psn():
        i = ps_i[0] % 8
        ps_i[0] += 1
        return PS[i]

    def dma(out, in_, reads, writes, eng="sync", **kw):
        S.op(eng, lambda e: e.dma_start(out=out, in_=in_, **kw), reads=reads, writes=writes, dma=True)

    def mm(out, lhsT, rhs, start, stop, reads, writes):
        S.op("tensor", lambda e: e.matmul(out, lhsT=lhsT, rhs=rhs, start=start, stop=stop), reads=reads, writes=writes)

    def tr(out, in_, ident, reads, writes):
        S.op("tensor", lambda e: e.transpose(out=out, in_=in_, identity=ident), reads=reads, writes=writes)

    def act(out, in_, func, reads, writes, bias=None, scale=None):
        kw = {}
        if bias is not None:
            kw["bias"] = bias
        if scale is not None:
            kw["scale"] = scale
        S.op("scalar", lambda e: e.activation(out=out, in_=in_, func=func, **kw), reads=reads, writes=writes)

    def tt(out, in0, in1, op, reads, writes, eng="vector"):
        S.op(eng, lambda e: e.tensor_tensor(out=out, in0=in0, in1=in1, op=op), reads=reads, writes=writes)

    def ts(out, in0, s1, s2, op0, op1, reads, writes, eng="vector"):
        if op1 is None:
            S.op(eng, lambda e: e.tensor_scalar(out=out, in0=in0, scalar1=s1, scalar2=None, op0=op0), reads=reads, writes=writes)
        else:
            S.op(eng, lambda e: e.tensor_scalar(out=out, in0=in0, scalar1=s1, scalar2=s2, op0=op0, op1=op1), reads=reads, writes=writes)

    def stt(out, in0, scalar, in1, op0, op1, reads, writes):
        S.op("vector", lambda e: e.scalar_tensor_tensor(out=out, in0=in0, scalar=scalar, in1=in1, op0=op0, op1=op1), reads=reads, writes=writes)

    def cp(out, in_, reads, writes, eng="vector"):
        S.op(eng, lambda e: e.tensor_copy(out=out, in_=in_), reads=reads, writes=writes)

    def mset(ap, val, reads, writes, eng="gpsimd"):
        S.op(eng, lambda e: e.memset(ap, val), reads=reads, writes=writes)

    def scan(out, d0, d1, init, reads, writes):
        S.op("vector", lambda e: e.tensor_tensor_scan(out=out, data0=d0, data1=d1, initial=init, op0=ALU.mult, op1=ALU.add), reads=reads, writes=writes)

    def recip(out, in_, reads, writes):
        S.op("vector", lambda e: e.reciprocal(out=out, in_=in_), reads=reads, writes=writes)

    ev_i = [0]

    def evac(out, in_, reads, writes):
        ev_i[0] += 1
        if ev_i[0] % 2:
            act(out, in_, AF.Copy, reads, writes)
        else:
            cp(out, in_, reads, writes)

    dma(identf.t[:], ident_d[:, :], [], [identf.k])
    cp(identb.t[:], identf.t[:], [identf.k], [identb.k])
    mset(onesb.t[:], 1.0, [], [onesb.k])
    dma(tri.t[:], tri_d[:, :], [], [tri.k])
    dma(cmask.t[:], cmask_d.rearrange("a b -> (a b)").partition_broadcast(128), [], [cmask.k])

    def wview(name, l):
        return W[name][l].rearrange("(kc p) n -> p kc n", p=128)

    cast_done = set()

    def cast_layer(l):
        if l in cast_done or l >= NL:
            return
        cast_done.add(l)
        zt = tmp()
        mset(zt.f, 0.0, [], [zt.k])
        ul = {}

        def unit(name, width):
            ap = dscr(f"wu_{l}_{name}", [128, width])
            k = S.tile(f"wu_{l}_{name}")
            ul[name] = (ap, k, width)
            return ap, k

        win = wview("w_in", l)
        for name, c0, c1 in [("U0", 0, 512), ("U1", 512, 1024), ("U2", 1024, 1536), ("U3", 1536, 2048),
                             ("U4", 2048, 2560), ("U5", 2560, 3072), ("U6", 3072, 3600), ("U7", 3600, 4112)]:
            w = c1 - c0
            ap, k = unit(name, 8 * w)
            dma(ap.rearrange("p (kc n) -> p kc n", kc=8), win[:, :, c0:c1], [], [k], eng="gpsimd")
        ap, k = unit("U8", 8 * 128)
        dma(ap[:, :], zt.b[:, 0:1024], [zt.k], [k], eng="gpsimd")
        a3 = ap.rearrange("p (c n) -> p c n", c=8)
        for gi, wn in enumerate(["lru_wa", "lru_wx"]):
            for c in range(4):
                for half in range(2):
                    dma(a3[half * 64:(half + 1) * 64, gi * 4 + c, half * 64:(half + 1) * 64],
                        W[wn][l, 2 * c + half], [], [k], eng="gpsimd")
        wm = W["w_merge"][l].rearrange("(kc p) (j m f) -> p kc j m f", p=128, j=3, m=8)
        wbr = [W[n][l].rearrange("(kc p) (m f) -> p kc m f", p=128, m=8) for n in ("w_branch_attn", "w_branch_gla", "w_branch_lru")]
        for m in range(8):
            ap, k = unit(f"M{m}", 3072)
            a_m = ap[:, 0:3072].rearrange("p (kc j f) -> p kc j f", kc=8, j=3)
            for j in range(3):
                dma(a_m[:, :, j, :], wm[:, :, j, m, :], [], [k], eng="gpsimd")
            ap, k = unit(f"B{m}", 1536)
            a_b = ap[:, 0:1536].rearrange("p (kc j f) -> p kc j f", kc=4, j=3)
            for j in range(3):
                dma(a_b[:, :, j, :], wbr[j][:, :, m, :], [], [k], eng="gpsimd")
        wo = wview("w_out", l)
        for i in range(2):
            ap, k = unit(f"O{i}", 4096)
            dma(ap.rearrange("p (kc n) -> p kc n", kc=8), wo[:, :, i * 512:(i + 1) * 512], [], [k], eng="gpsimd")
        wg = wview("w_ffn_gate", l)
        wu = wview("w_ffn_up", l)
        for i in range(11):
            ap, k = unit(f"F{i}", 4096)
            a4 = ap.rearrange("p (kc g n) -> p kc g n", kc=8, g=2)
            dma(a4[:, :, 0, :], wg[:, :, i * 256:(i + 1) * 256], [], [k], eng="gpsimd")
            dma(a4[:, :, 1, :], wu[:, :, i * 256:(i + 1) * 256], [], [k], eng="gpsimd")
        wd = wview("w_ffn_down", l)
        for m in range(8):
            ap, k = unit(f"D{m}", NFC * 128)
            dma(ap.rearrange("p (kc n) -> p kc n", kc=NFC), wd[:, :, m * 128:(m + 1) * 128], [], [k], eng="gpsimd")
        units[l] = ul


    cast_layer(0)

    for l in range(NL):
        st = tmp()
        st_t = st.f[:, 0:128]
        mset(st_t, 0.0, [], [st.k])
        rows = [
            (0, 66, W["ffn_conv_w"][l].rearrange("j (c f) -> (j c) f", f=128)),
            (66, 22, W["ffn_conv_b"][l].rearrange("(c f) -> c f", f=128)),
            (88, 24, W["b_merge"][l].rearrange("(c f) -> c f", f=128)),
            (112, 8, W["norm_mix"][l].rearrange("(c f) -> c f", f=128)),
            (120, 8, W["norm_ffn"][l].rearrange("(c f) -> c f", f=128)),
        ]
        for r0, n, src in rows:
            dma(st_t[r0:r0 + n, :], src, [], [st.k])
        p = psn()
        tr(p.t[:, 0:128], st_t, identf.t[:], [st.k, identf.k], [p.k])
        cp(parA[l].t[:], p.t[:, 0:128], [p.k], [parA[l].k])
        st = tmp()
        st_t = st.f[:, 0:128]
        mset(st_t, 0.0, [], [st.k])
        rows = [
            (0, 16, W["lru_conv_w"][l].rearrange("j (c f) -> (j c) f", f=128), 128),
            (16, 4, W["lru_conv_b"][l].rearrange("(c f) -> c f", f=128), 128),
            (20, 4, W["lru_ba"][l].rearrange("(c f) -> c f", f=128), 128),
            (24, 4, W["lru_bx"][l].rearrange("(c f) -> c f", f=128), 128),
            (28, 4, W["lru_lambda"][l].rearrange("(c f) -> c f", f=128), 128),
            (32, 1, W["attn_subln"][l].rearrange("(c f) -> c f", f=128), 128),
            (33, 1, W["gla_norm"][l].rearrange("(c f) -> c f", f=128), 128),
            (34, 4, W["b_gla_gate"][l].rearrange("(c f) -> c f", f=64), 64),
            (38, 8, W["norm_final"].rearrange("(c f) -> c f", f=128), 128),
        ]
        for r0, n, src, wd_ in rows:
            dma(st_t[r0:r0 + n, 0:wd_], src, [], [st.k])
        p = psn()
        tr(p.t[:, 0:128], st_t, identf.t[:], [st.k, identf.k], [p.k])
        cp(parB[l].t[:], p.t[:, 0:64], [p.k], [parB[l].k])
        lam_init = 0.8 - 0.6 * math.exp(-0.3 * l)
        lq = tmp()
        dma(lq.f[:, 0:256], W["lambda_qk"][l].rearrange("a b -> (a b)").partition_broadcast(128), [], [lq.k])
        t1 = tmp()
        tt(t1.f[:, 0:64], lq.f[:, 0:64], lq.f[:, 64:128], ALU.mult, [lq.k], [t1.k])
        tt(t1.f[:, 64:128], lq.f[:, 128:192], lq.f[:, 192:256], ALU.mult, [lq.k, t1.k], [t1.k])
        S.op("vector", lambda e, t1=t1: e.tensor_reduce(out=t1.f[:, 128:130], in_=t1.f[:, 0:128].rearrange("p (a b) -> p a b", a=2), axis=mybir.AxisListType.X, op=ALU.add), reads=[t1.k], writes=[t1.k])
        act(t1.f[:, 130:132], t1.f[:, 128:130], AF.Exp, [t1.k], [t1.k])
        tt(t1.f[:, 132:133], t1.f[:, 131:132], t1.f[:, 130:131], ALU.subtract, [t1.k], [t1.k])
        ts(lamc[l].t[:, 0:1], t1.f[:, 132:133], -lam_init, None, ALU.add, None, [t1.k], [lamc[l].k])
        ts(lamc[l].t[:, 1:2], parB[l].t[:, 32:33], 1.0 - lam_init, None, ALU.mult, None, [parB[l].k, lamc[l].k], [lamc[l].k])
        act(t1.f[:, 140:144], parB[l].t[:, 28:32], AF.Exp, [parB[l].k, t1.k], [t1.k], scale=-1.0)
        act(t1.f[:, 144:148], t1.f[:, 140:144], AF.Ln, [t1.k], [t1.k], bias=1.0)
        ts(lamc[l].t[:, 2:6], t1.f[:, 144:148], -8.0, None, ALU.mult, None, [t1.k, lamc[l].k], [lamc[l].k])
        ts(parB[l].t[:, 46:50], parB[l].t[:, 34:38], -1.0, None, ALU.mult, None, [parB[l].k], [parB[l].k])
        t2 = tmp()
        dma(t2.f[0:16, 0:256], W["w_gla_gate2"][l], [], [t2.k])
        cp(w2b[l].t[:], t2.f[0:16, 0:256], [t2.k], [w2b[l].k])

    def wload(l, name):
        ap, k, width = units[l][name]
        slot = wring[wr_i[0] % NWR]
        wr_i[0] += 1
        dma(slot.t[:, 0:width], ap[:, :], [k], [slot.k])
        return slot

    def run_tile(seq, TT, BP, NB, L, NCK, xsrc, ysrc, rope_src, t0, units_past, is_last, outs):
        S.phase = 'xload'
        base, xk = tmp8()

        def xtok(bi):
            return pool_t[0:BP, base + 2 * bi: base + 2 * bi + 2, :].rearrange("p a n -> p (a n)")

        for bi in range(NB):
            dma(xtok(bi), xsrc[bi * BP:(bi + 1) * BP, :], [], [xk[2 * bi], xk[2 * bi + 1]])
        dma(rope_sb.t[0:BP, 0:NB, :], rope_src.rearrange("(b p) f -> p b f", p=BP), [], [rope_sb.k])
        for c in range(8):
            p = psn()
            for bi in range(NB):
                tr(p.t[:, bi * BP:(bi + 1) * BP], xtok(bi)[:, c * 128:(c + 1) * 128], identf.t[0:BP, 0:BP],
                   [xk[2 * bi], xk[2 * bi + 1], identf.k], [p.k])
            evac(xT.t[:, c, 0:TT], p.t[:, 0:TT], [p.k], [xT.k])

        def rmsnorm_to(dst, gcol):
            p = psn()
            for c in range(8):
                sq = tmp()
                act(sq.b[:, 0:TT], xT.t[:, c, 0:TT], AF.Square, [xT.k], [sq.k])
                mm(p.t[:, 0:TT], onesb.t[:, :], sq.b[:, 0:TT], c == 0, c == 7, [onesb.k, sq.k], [p.k])
            rs = tmp()
            act(rs.f[:, 0:TT], p.t[:, 0:TT], AF.Ln, [p.k], [rs.k], bias=eps_ap, scale=1.0 / D)
            act(rs.f[:, 0:TT], rs.f[:, 0:TT], AF.Exp, [rs.k], [rs.k], scale=-0.5)
            for c in range(8):
                stt(dst.t[:, c, 0:TT], xT.t[:, c, 0:TT], gcol(c), rs.f[:, 0:TT], ALU.mult, ALU.mult, [xT.k, rs.k], [dst.k])

        for l in range(NL):
            cast_layer(l + 1)
            pA, pB = parA[l], parB[l]
            if t0 == 0:
                if seq == 'p':
                    mset(gS[l].t[:], 0.0, [], gSk[l])
                    mset(gSb[l].t[:], 0.0, [], gSbk[l])
                    mset(hst[l].t[:], 0.0, [], [hst[l].k])
                    mset(lxh[l].t[:], 0.0, [], [lxh[l].k])
                    mset(guh[l].t[:], 0.0, [], [guh[l].k])
                else:
                    dma(gS[l].t[:], sg[l, seq].rearrange("h k v -> k h v"), [], gSk[l])
                    cp(gSb[l].t[:], gS[l].t[:], gSk[l], gSbk[l], eng="gpsimd")
                    dma(hst[l].t[:], slh[l, seq].rearrange("(c p) -> p c", p=128), [], [hst[l].k], allow_slow_non_contiguous=True)
                    for j in range(3):
                        dma(lxh[l].t[:, :, j], slc[l, seq, j].rearrange("(c p) -> p c", p=128), [], [lxh[l].k], allow_slow_non_contiguous=True)
                    for j in range(2):
                        dma(guh[l].t[:, :, j], sfc[l, seq, j].rearrange("(c p) -> p c", p=128), [], [guh[l].k], allow_slow_non_contiguous=True)

            S.phase = 'norm1'
            rmsnorm_to(xn, lambda c: pA.t[:, 112 + c:113 + c])
            xn_r = [xn.k, pA.k]

            S.phase = 'qkv'
            pend_tr = []
            for ui, uname in [(2, "U2"), (0, "U0"), (1, "U1")]:
                ws = wload(l, uname)
                w3 = ws.t[:, 0:4096].rearrange("p (kc n) -> p kc n", kc=8)
                tbs = []
                for bi in range(NB):
                    p = psn()
                    for kc in range(8):
                        mm(p.t[0:BP, :], xn.t[:, kc, bi * BP:(bi + 1) * BP], w3[:, kc, :], kc == 0, kc == 7, [xn.k, ws.k], [p.k])
                    tk = tmp()
                    act(tk.f[0:BP, :], p.t[0:BP, :], AF.Copy, [p.k], [tk.k])
                    if ui < 2:
                        x4 = tk.f[0:BP, :].rearrange("p (a d) -> p a d", d=64)
                        cs4 = rope_sb.t[0:BP, bi, 0:64].rearrange("p (a d) -> p a d", d=8)
                        sn4 = rope_sb.t[0:BP, bi, 64:128].rearrange("p (a d) -> p a d", d=8)
                        tq = ropet
                        q4 = tq.t[0:BP, 0:256].rearrange("p (j a d) -> p j a d", j=4, d=8)
                        rk = [tk.k, rope_sb.k, tq.k]
                        tt(q4[:, 0], x4[:, :, 0:8], cs4, ALU.mult, rk, [tq.k])
                        tt(q4[:, 1], x4[:, :, 8:16], sn4, ALU.mult, rk, [tq.k])
                        tt(q4[:, 2], x4[:, :, 8:16], cs4, ALU.mult, rk, [tq.k])
                        tt(q4[:, 3], x4[:, :, 0:8], sn4, ALU.mult, rk, [tq.k])
                        tt(x4[:, :, 0:8], q4[:, 0], q4[:, 1], ALU.subtract, [tq.k, tk.k], [tk.k])
                        tt(x4[:, :, 8:16], q4[:, 2], q4[:, 3], ALU.add, [tq.k, tk.k], [tk.k])
                        tb = tmp()
                        cp(tb.b[0:BP, 0:512], tk.f[0:BP, :], [tk.k], [tb.k])
                        tbs.append(tb)
                        if ui == 1:
                            ot = otile()
                            dma(outs["k"][l][bi * BP:(bi + 1) * BP, :], tk.f[0:BP, :], [tk.k], [ot], eng="gpsimd")
                    else:
                        ot = otile()
                        dma(outs["v"][l][bi * BP:(bi + 1) * BP, :], tk.f[0:BP, :], [tk.k], [ot], eng="gpsimd")
                        cp(v_bf.t[0:BP, bi, :], tk.f[0:BP, :], [tk.k], [v_bf.k], eng="gpsimd")
                if tbs:
                    pend_tr.append((QT if ui == 0 else KTc, tbs))
            for dst, tbs in pend_tr:
                for bi, tb in enumerate(tbs):
                    pt = psn()
                    ptb = pt.t[:].bitcast(BF16)
                    for hc in range(4):
                        tr(ptb[:, hc * BP:(hc + 1) * BP], tb.b[0:BP, hc * 128:(hc + 1) * 128], identb.t[0:BP, 0:BP], [tb.k, identb.k], [pt.k])
                    evac(dst.t[:, :, bi * BP:(bi + 1) * BP], ptb[:, 0:4 * BP].rearrange("p (h t) -> p h t", h=4), [pt.k], [dst.k])
            if seq == 'p' and not is_last:
                u = t0 // 512
                dma(ktscr[l].rearrange("h p t -> p h t")[:, :, t0:t0 + TT], KTc.t[:, :, 0:TT], [KTc.k], [kt_t[l][h][u] for h in range(4)], eng="gpsimd")
                for h in range(4):
                    dma(vscr[l, h, u].rearrange("p (b e) -> p b e", b=4), v_bf.t[:, :, h * 128:(h + 1) * 128], [v_bf.k], [v_t[l][h][u]], eng="gpsimd")

            S.phase = 'gla_prep'
            ws = wload(l, "U6")
            w3 = ws.t[:, 0:8 * 528].rearrange("p (kc n) -> p kc n", kc=8)
            p = psn()
            for kc in range(8):
                mm(p.t[0:16, 0:TT], w3[:, kc, 0:16], xn.t[:, kc, 0:TT], kc == 0, kc == 7, [xn.k, ws.k], [p.k])
            evac(gab.t[0:16, 0:TT], p.t[0:16, 0:TT], [p.k], [gab.k])
            cp(lxT.t[:, :, 0:3], lxh[l].t[:, :, :], [lxh[l].k], [lxT.k], eng="gpsimd")
            for c in range(4):
                p = psn()
                for kc in range(8):
                    mm(p.t[:, 0:TT], w3[:, kc, 16 + c * 128:16 + (c + 1) * 128], xn.t[:, kc, 0:TT], kc == 0, kc == 7, [xn.k, ws.k], [p.k])
                evac(lxT.t[:, c, 3:3 + TT], p.t[:, 0:TT], [p.k], [lxT.k])
            cp(lxh[l].t[:, :, :], lxT.t[:, :, TT:TT + 3], [lxT.k], [lxh[l].k], eng="gpsimd")
            w8 = wload(l, "U8")
            w83 = w8.t[:, 0:1024].rearrange("p (c n) -> p c n", c=8)
            w7 = wload(l, "U7")
            w73 = w7.t[:, 0:4096].rearrange("p (kc n) -> p kc n", kc=8)


            S.phase = 'attn'
            acc = PS[0:4]
            sring = PS[4:8]
            sr_i = 0
            nkb_cur = (TT + 127) // 128
            pend_fin = None
            for h in range(4):
                S.phase = 'attn'

                nent = [0]

                def load_unit(u):
                    slot = kv_i[0] % 3
                    kv_i[0] += 1
                    kr, vr = kring[slot], vring[slot]
                    if seq == 'p':
                        dma(kr.t[:, :], ktscr[l, h, :, u * 512:(u + 1) * 512], [kt_t[l][h][u]], [kr.k])
                        dma(vr.t[:, :, :], vscr[l, h, u].rearrange("p (b e) -> p b e", b=4), [v_t[l][h][u]], [vr.k])
                    else:
                        ks_ = kst[0]
                        dma(ks_.t[:, :, :], ck[l, seq, u * 512:(u + 1) * 512, h * 128:(h + 1) * 128].rearrange("(b p) f -> p b f", p=128), [], [ks_.k], eng="gpsimd")
                        dma(vr.t[:, :, :], cv[l, seq, u * 512:(u + 1) * 512, h * 128:(h + 1) * 128].rearrange("(b p) f -> p b f", p=128), [], [vr.k], eng="gpsimd")
                        pt = sring[2 * (nent[0] % 2)]
                        ptb = pt.t[:].bitcast(BF16)
                        for b_ in range(4):
                            tr(ptb[:, b_ * 128:(b_ + 1) * 128], ks_.t[:, b_, :], identb.t[:, :], [ks_.k, identb.k], [pt.k])
                        evac(kr.t[:, :], ptb[:, 0:512], [pt.k], [kr.k])
                    return kr, vr

                loaded = {}

                def ensure(u):
                    if u < units_past and u not in loaded:
                        loaded[u] = load_unit(u)

                nblk = units_past * 4 + nkb_cur

                def gen_blocks():
                    for u in range(units_past):
                        ensure(u)
                        ensure(u + 1)
                        kr, vr = loaded.pop(u)
                        for kb in range(4):
                            yield (kr.t[:, kb * 128:(kb + 1) * 128], vr.t[:, kb, :], 128, None, [kr.k, vr.k])
                    for kb in range(nkb_cur):
                        nk = min(128, TT - kb * 128)
                        yield (KTc.t[:, h, kb * 128:kb * 128 + nk], v_bf.t[0:nk, kb, h * 128:(h + 1) * 128], nk,
                               kb if seq == 'p' else None, [KTc.k, v_bf.k])

                def s1a(blk):
                    ksrc, vsrc, nk, diag, rds = blk
                    par = nent[0] % 2
                    nent[0] += 1
                    sbs = []
                    for m in range(2):
                        sp_ = sring[2 * par + m]
                        mm(sp_.t[0:nk, 0:TT], ksrc[m * 64:(m + 1) * 64, :], QT.t[m * 64:(m + 1) * 64, h, 0:TT], True, True, rds + [QT.k], [sp_.k])
                        sbs.append(sp_)
                    return sbs

                def s1b(blk, sbs):
                    ksrc, vsrc, nk, diag, rds = blk
                    prs = []
                    for m in range(2):
                        sp_ = sbs[m]
                        pr = pring[pr_i[0] % 4]
                        pr_i[0] += 1
                        act(pr.t[0:nk, 0:TT], sp_.t[0:nk, 0:TT], AF.Exp, [sp_.k], [pr.k], scale=0.125)
                        if diag is not None:
                            if diag > 0:
                                mset(pr.t[0:nk, 0:diag * 128], 0.0, [pr.k], [pr.k])
                            if nk > 64:
                                mset(pr.t[64:nk, diag * 128:diag * 128 + 64], 0.0, [pr.k], [pr.k])
                        prs.append(pr)
                    return prs

                def stage2(blk, prs, bidx):
                    ksrc, vsrc, nk, diag, rds = blk
                    first = bidx == 0
                    last = bidx == nblk - 1
                    for m in range(2):
                        pr = prs[m]
                        mm(acc[m].t[:, 0:TT], vsrc, pr.t[0:nk, 0:TT], first, last, rds + [pr.k], [acc[m].k])
                        mm(acc[2 + m].t[:, 0:TT], onesb.t[0:nk, :], pr.t[0:nk, 0:TT], first, last, [onesb.k, pr.k], [acc[2 + m].k])

                def freebank():
                    return sring[2 * (nent[0] % 2)]

                ents = []
                bidx = 0
                for blk in gen_blocks():
                    ents.append([blk, s1a(blk), None])
                    n = len(ents)
                    if n >= 2:
                        e = ents[n - 2]
                        e[2] = s1b(e[0], e[1])
                        if n == 3 and pend_fin is not None:
                            pend_fin(freebank)
                            pend_fin = None
                            S.phase = 'attn'
                    if n >= 3:
                        e = ents[n - 3]
                        stage2(e[0], e[2], bidx)
                        bidx += 1
                n = len(ents)
                e = ents[n - 1]
                e[2] = s1b(e[0], e[1])
                if n >= 2:
                    e2 = ents[n - 2]
                    stage2(e2[0], e2[2], bidx)
                    bidx += 1
                stage2(e[0], e[2], bidx)
                if pend_fin is not None:
                    pend_fin(freebank)
                    pend_fin = None
                S.phase = 'attn_fin'
                r0, r1, t0_, t1_ = tmp(), tmp(), tmp(), tmp()
                cp(t0_.f[:, 0:TT], acc[0].t[:, 0:TT], [acc[0].k], [t0_.k])
                cp(t1_.f[:, 0:TT], acc[1].t[:, 0:TT], [acc[1].k], [t1_.k])
                act(r0.f[:, 0:TT], acc[2].t[:, 0:TT], AF.Ln, [acc[2].k], [r0.k])
                act(r1.f[:, 0:TT], acc[3].t[:, 0:TT], AF.Ln, [acc[3].k], [r1.k])
                act(r0.f[:, 0:TT], r0.f[:, 0:TT], AF.Exp, [r0.k], [r0.k], scale=-1.0)
                act(r1.f[:, 0:TT], r1.f[:, 0:TT], AF.Exp, [r1.k], [r1.k], scale=-1.0)
                tt(t0_.f[:, 0:TT], t0_.f[:, 0:TT], r0.f[:, 0:TT], ALU.mult, [t0_.k, r0.k], [t0_.k])
                tt(t1_.f[:, 0:TT], t1_.f[:, 0:TT], r1.f[:, 0:TT], ALU.mult, [t1_.k, r1.k], [t1_.k])
                stt(t0_.f[:, 0:TT], t1_.f[:, 0:TT], lamc[l].t[:, 0:1], t0_.f[:, 0:TT], ALU.mult, ALU.add, [t1_.k, t0_.k, lamc[l].k], [t0_.k])
                sq = tmp()
                act(sq.b[:, 0:TT], t0_.f[:, 0:TT], AF.Square, [t0_.k], [sq.k])

                def fin_b(bank, h=h, t0_=t0_, sq=sq):
                    S.phase = 'attn_fin'
                    pb_ = bank()
                    mm(pb_.t[:, 0:TT], onesb.t[:, :], sq.b[:, 0:TT], True, True, [onesb.k, sq.k], [pb_.k])
                    rs = tmp()
                    act(rs.f[:, 0:TT], pb_.t[:, 0:TT], AF.Ln, [pb_.k], [rs.k], bias=eps_ap, scale=1.0 / 128)
                    act(rs.f[:, 0:TT], rs.f[:, 0:TT], AF.Exp, [rs.k], [rs.k], scale=-0.5)
                    stt(oattn.t[:, h, 0:TT], t0_.f[:, 0:TT], lamc[l].t[:, 1:2], rs.f[:, 0:TT], ALU.mult, ALU.mult, [t0_.k, rs.k, lamc[l].k], [oattn.k])

                pend_fin = fin_b
            pend_fin(lambda: sring[0])
            pend_fin = None

            S.phase = 'gla_prep'
            ws = wload(l, "U4")
            w3 = ws.t[:, 0:4096].rearrange("p (kc n) -> p kc n", kc=8)
            for ck_ in range(NCK):
                p = psn()
                for kc in range(8):
                    mm(p.t[0:L, :], xn.t[:, kc, ck_ * L:(ck_ + 1) * L], w3[:, kc, :], kc == 0, kc == 7, [xn.k, ws.k], [p.k])
                evac(gv_tok.t[0:L, ck_, :], p.t[0:L, :], [p.k], [gv_tok.k])
            for h in range(4):
                p = psn()
                mm(p.t[0:64, 0:TT], w2b[l].t[0:16, h * 64:(h + 1) * 64], gab.t[0:16, 0:TT], True, True, [w2b[l].k, gab.k], [p.k])
                e1 = tmp()
                act(e1.f[0:64, 0:TT], p.t[0:64, 0:TT], AF.Exp, [p.k, pB.k], [e1.k], bias=pB.t[0:64, 46 + h:47 + h], scale=-1.0)
                act(e1.f[0:64, 0:TT], e1.f[0:64, 0:TT], AF.Ln, [e1.k], [e1.k], bias=1.0)
                scan(gcs.t[:, h, 0:TT], cmask.t[0:64, 0:TT], e1.f[0:64, 0:TT], 0.0, [cmask.k, e1.k], [gcs.k])
            ws = wload(l, "U3")
            w3 = ws.t[:, 0:4096].rearrange("p (kc n) -> p kc n", kc=8)
            for h in range(4):
                p = psn()
                for kc in range(8):
                    mm(p.t[0:64, 0:TT], w3[:, kc, h * 64:(h + 1) * 64], xn.t[:, kc, 0:TT], kc == 0, kc == 7, [xn.k, ws.k], [p.k])
                eq = tmp()
                act(eq.f[0:64, 0:TT], gcs.t[:, h, 0:TT], AF.Exp, [gcs.k], [eq.k], scale=-1.0 / 16)
                stt(qtl.t[:, h, 0:TT], p.t[0:64, 0:TT], 0.125, eq.f[0:64, 0:TT], ALU.mult, ALU.mult, [p.k, eq.k], [qtl.k])
                cp(ebl.t[:, h, 0:NCK], eq.f[0:64, 0:TT].rearrange("p (c t) -> p c t", t=L)[:, :, L - 1], [eq.k], [ebl.k])
                p = psn()
                for kc in range(8):
                    mm(p.t[0:64, 0:TT], w3[:, kc, 256 + h * 64:256 + (h + 1) * 64], xn.t[:, kc, 0:TT], kc == 0, kc == 7, [xn.k, ws.k], [p.k])
                ek = tmp()
                act(ek.f[0:64, 0:TT], gcs.t[:, h, 0:TT], AF.Exp, [gcs.k], [ek.k], scale=1.0 / 16)
                tt(ktl.t[:, h, 0:TT], p.t[0:64, 0:TT], ek.f[0:64, 0:TT], ALU.mult, [p.k, ek.k], [ktl.k])
            for ck_ in range(NCK):
                pt = psn()
                ptb = pt.t[:].bitcast(BF16)
                for h in range(4):
                    tr(ptb[0:L, h * 64:(h + 1) * 64], ktl.t[:, h, ck_ * L:(ck_ + 1) * L], identb.t[0:64, 0:64], [ktl.k, identb.k], [pt.k])
                evac(ktok.t[0:L, ck_, :], ptb[0:L, 0:256], [pt.k], [ktok.k])
            S.phase = 'gla_chunks'
            gacc = PS[0:4]
            gring = PS[4:8]
            gr_i = 0
            for ck_ in range(NCK):
                cs_ = slice(ck_ * L, (ck_ + 1) * L)
                pas, ams = [], []
                for h in range(4):
                    pa = gring[h]
                    mm(pa.t[0:L, 0:L], ktl.t[:, h, cs_], qtl.t[:, h, cs_], True, True, [ktl.k, qtl.k], [pa.k])
                    pas.append(pa)
                for h in range(4):
                    am = tmp()
                    tt(am.b[0:L, 0:L], pas[h].t[0:L, 0:L], tri.t[0:L, 0:L], ALU.mult, [pas[h].k, tri.k], [am.k])
                    ams.append(am)
                for h in range(4):
                    mm(gacc[h].t[:, cs_], gv_tok.t[0:L, ck_, h * 128:(h + 1) * 128], ams[h].b[0:L, 0:L], True, False, [gv_tok.k, ams[h].k], [gacc[h].k])
                    mm(gacc[h].t[:, cs_], gSb[l].t[:, h, :], qtl.t[:, h, cs_], False, True, [gSbk[l][h], qtl.k], [gacc[h].k])
                for h in range(4):
                    pu = gring[h]
                    mm(pu.t[0:64, 0:128], ktok.t[0:L, ck_, h * 64:(h + 1) * 64], gv_tok.t[0:L, ck_, h * 128:(h + 1) * 128], True, True, [ktok.k, gv_tok.k], [pu.k])
                for h in range(4):
                    pu = gring[h]
                    tt(gS[l].t[:, h, :], gS[l].t[:, h, :], pu.t[0:64, 0:128], ALU.add, [gSk[l][h], pu.k], [gSk[l][h]])
                    ts(gS[l].t[:, h, :], gS[l].t[:, h, :], ebl.t[:, h, ck_:ck_ + 1], None, ALU.mult, None, [gSk[l][h], ebl.k], [gSk[l][h]])
                    evac(gSb[l].t[:, h, :], gS[l].t[:, h, :], [gSk[l][h]], [gSbk[l][h]])
            gr_i = 0
            S.phase = 'gla_out'
            ws = wload(l, "U5")
            w3 = ws.t[:, 0:4096].rearrange("p (kc n) -> p kc n", kc=8)
            gfs, sqs, sls = [], [], []
            for h in range(4):
                gf = tmp()
                act(gf.f[:, 0:TT], gacc[h].t[:, 0:TT], AF.Copy, [gacc[h].k], [gf.k])
                gfs.append(gf)
            for h in range(4):
                sq = tmp()
                act(sq.b[:, 0:TT], gfs[h].f[:, 0:TT], AF.Square, [gfs[h].k], [sq.k])
                sqs.append(sq)
            pss = []
            for h in range(4):
                p = gring[h]
                mm(p.t[:, 0:TT], onesb.t[:, :], sqs[h].b[:, 0:TT], True, True, [onesb.k, sqs[h].k], [p.k])
                pss.append(p)
            for h in range(4):
                rs = sqs[h]
                act(rs.f[:, 0:TT], pss[h].t[:, 0:TT], AF.Ln, [pss[h].k, rs.k], [rs.k], bias=eps_ap, scale=1.0 / 128)
            for h in range(4):
                rs = sqs[h]
                act(rs.f[:, 0:TT], rs.f[:, 0:TT], AF.Exp, [rs.k], [rs.k], scale=-0.5)
            for h in range(4):
                stt(gfs[h].f[:, 0:TT], gfs[h].f[:, 0:TT], pB.t[:, 33:34], sqs[h].f[:, 0:TT], ALU.mult, ALU.mult, [gfs[h].k, sqs[h].k, pB.k], [gfs[h].k])
            for h in range(4):
                p = gring[h]
                for kc in range(8):
                    mm(p.t[:, 0:TT], w3[:, kc, h * 128:(h + 1) * 128], xn.t[:, kc, 0:TT], kc == 0, kc == 7, [xn.k, ws.k], [p.k])
            for h in range(4):
                sl = tmp()
                act(sl.f[:, 0:TT], gring[h].t[:, 0:TT], AF.Silu, [gring[h].k], [sl.k])
                sls.append(sl)
            for h in range(4):
                tt(gon.t[:, h, 0:TT], gfs[h].f[:, 0:TT], sls[h].f[:, 0:TT], ALU.mult, [gfs[h].k, sls[h].k], [gon.k])

            S.phase = 'lru'
            for half in range(2):
                cs2 = [2 * half, 2 * half + 1]
                T1 = {c: tmp() for c in cs2}
                T2 = {c: tmp() for c in cs2}
                T3 = {c: tmp() for c in cs2}
                T4 = {c: tmp() for c in cs2}
                rk = [lxT.k, pB.k]
                for c in cs2:
                    xc = T1[c]
                    ts(xc.f[:, 0:TT], lxT.t[:, c, 0:TT], pB.t[:, c:c + 1], pB.t[:, 16 + c:17 + c], ALU.mult, ALU.add, rk, [xc.k])
                for j in range(1, 4):
                    for c in cs2:
                        xc = T1[c]
                        stt(xc.f[:, 0:TT], lxT.t[:, c, j:j + TT], pB.t[:, 4 * j + c:4 * j + c + 1], xc.f[:, 0:TT], ALU.mult, ALU.add, rk + [xc.k], [xc.k])
                for c in cs2:
                    cp(T2[c].b[:, 0:TT], T1[c].f[:, 0:TT], [T1[c].k], [T2[c].k], eng="gpsimd")
                prs_, pis_ = {}, {}
                for c in cs2:
                    prs_[c] = psn()
                    mm(prs_[c].t[:, 0:TT], w83[:, c, :], T2[c].b[:, 0:TT], True, True, [w8.k, T2[c].k], [prs_[c].k])
                    pis_[c] = psn()
                    mm(pis_[c].t[:, 0:TT], w83[:, 4 + c, :], T2[c].b[:, 0:TT], True, True, [w8.k, T2[c].k], [pis_[c].k])
                for c in cs2:
                    act(T3[c].f[:, 0:TT], prs_[c].t[:, 0:TT], AF.Sigmoid, [prs_[c].k, pB.k], [T3[c].k], bias=pB.t[:, 20 + c:21 + c])
                    act(T4[c].f[:, 0:TT], pis_[c].t[:, 0:TT], AF.Sigmoid, [pis_[c].k, pB.k], [T4[c].k], bias=pB.t[:, 24 + c:25 + c])
                for c in cs2:
                    act(T3[c].f[:, 0:TT], T3[c].f[:, 0:TT], AF.Exp, [T3[c].k, lamc[l].k], [T3[c].k], scale=lamc[l].t[:, 2 + c:3 + c])
                for c in cs2:
                    tt(T2[c].f[:, 0:TT], T3[c].f[:, 0:TT], T3[c].f[:, 0:TT], ALU.mult, [T3[c].k, T2[c].k], [T2[c].k])
                    ts(T2[c].f[:, 0:TT], T2[c].f[:, 0:TT], -1.0, 1.0, ALU.mult, ALU.add, [T2[c].k], [T2[c].k])
                    ts(T2[c].f[:, 0:TT], T2[c].f[:, 0:TT], 1e-18, None, ALU.max, None, [T2[c].k], [T2[c].k])
                    tt(T4[c].f[:, 0:TT], T4[c].f[:, 0:TT], T1[c].f[:, 0:TT], ALU.mult, [T4[c].k, T1[c].k], [T4[c].k])
                for c in cs2:
                    act(T2[c].f[:, 0:TT], T2[c].f[:, 0:TT], AF.Ln, [T2[c].k], [T2[c].k])
                for c in cs2:
                    act(T2[c].f[:, 0:TT], T2[c].f[:, 0:TT], AF.Exp, [T2[c].k], [T2[c].k], scale=0.5)
                for c in cs2:
                    tt(T4[c].f[:, 0:TT], T4[c].f[:, 0:TT], T2[c].f[:, 0:TT], ALU.mult, [T4[c].k, T2[c].k], [T4[c].k])
                for c in cs2:
                    scan(T1[c].f[:, 0:TT], T3[c].f[:, 0:TT], T4[c].f[:, 0:TT], hst[l].t[:, c:c + 1], [T3[c].k, T4[c].k, hst[l].k, T1[c].k], [T1[c].k])
                    cp(hst[l].t[:, c:c + 1], T1[c].f[:, TT - 1:TT], [T1[c].k], [hst[l].k])
                pgs = {}
                for c in cs2:
                    pgs[c] = psn()
                    for kc in range(8):
                        mm(pgs[c].t[:, 0:TT], w73[:, kc, c * 128:(c + 1) * 128], xn.t[:, kc, 0:TT], kc == 0, kc == 7, [xn.k, w7.k], [pgs[c].k])
                for c in cs2:
                    act(T2[c].f[:, 0:TT], pgs[c].t[:, 0:TT], AF.Gelu_apprx_tanh, [pgs[c].k, T2[c].k], [T2[c].k])
                for c in cs2:
                    tt(hl.t[:, c, 0:TT], T1[c].f[:, 0:TT], T2[c].f[:, 0:TT], ALU.mult, [T1[c].k, T2[c].k], [hl.k])

            S.phase = 'merge'
            for m in range(8):
                ws = wload(l, f"M{m}")
                wsb = wload(l, f"B{m}")
                wm4 = ws.t[:, 0:3072].rearrange("p (kc j f) -> p kc j f", kc=8, j=3)
                wb4 = wsb.t[:, 0:1536].rearrange("p (kc j f) -> p kc j f", kc=4, j=3)
                srcs = [oattn, gon, hl]
                acc_t = None
                for j in range(3):
                    pg = psn()
                    for kc in range(8):
                        mm(pg.t[:, 0:TT], wm4[:, kc, j, :], xn.t[:, kc, 0:TT], kc == 0, kc == 7, [xn.k, ws.k], [pg.k])
                    g = tmp()
                    act(g.f[:, 0:TT], pg.t[:, 0:TT], AF.Sigmoid, [pg.k, pA.k], [g.k], bias=pA.t[:, 88 + j * 8 + m:89 + j * 8 + m])
                    py = psn()
                    for kc in range(4):
                        mm(py.t[:, 0:TT], wb4[:, kc, j, :], srcs[j].t[:, kc, 0:TT], kc == 0, kc == 3, [srcs[j].k, wsb.k], [py.k])
                    if j == 0:
                        acc_t = tmp()
                        tt(acc_t.f[:, 0:TT], g.f[:, 0:TT], py.t[:, 0:TT], ALU.mult, [g.k, py.k], [acc_t.k])
                    else:
                        tt(g.f[:, 0:TT], g.f[:, 0:TT], py.t[:, 0:TT], ALU.mult, [g.k, py.k], [g.k])
                        if j == 1:
                            tt(acc_t.f[:, 0:TT], acc_t.f[:, 0:TT], g.f[:, 0:TT], ALU.add, [g.k, acc_t.k], [acc_t.k])
                        else:
                            tt(mergedb.t[:, m, 0:TT], acc_t.f[:, 0:TT], g.f[:, 0:TT], ALU.add, [g.k, acc_t.k], [mergedb.k])
            S.phase = 'wout'
            for i in range(2):
                ws = wload(l, f"O{i}")
                w3 = ws.t[:, 0:4096].rearrange("p (kc n) -> p kc n", kc=8)
                for mi in range(4):
                    m = i * 4 + mi
                    p = psn()
                    for kc in range(8):
                        mm(p.t[:, 0:TT], w3[:, kc, mi * 128:(mi + 1) * 128], mergedb.t[:, kc, 0:TT], kc == 0, kc == 7, [mergedb.k, ws.k], [p.k])
                    tt(xT.t[:, m, 0:TT], xT.t[:, m, 0:TT], p.t[:, 0:TT], ALU.add, [xT.k, p.k], [xT.k])

            S.phase = 'ffn_gu'
            rmsnorm_to(xn, lambda c: pA.t[:, 120 + c:121 + c])
            for i in range(11):
                ws = wload(l, f"F{i}")
                w4 = ws.t[:, 0:4096].rearrange("p (kc g n) -> p kc g n", kc=8, g=2)
                for s_ in range(2):
                    j = 2 * i + s_
                    pg = psn()
                    for kc in range(8):
                        mm(pg.t[:, 0:TT], w4[:, kc, 0, s_ * 128:(s_ + 1) * 128], xn.t[:, kc, 0:TT], kc == 0, kc == 7, [xn.k, ws.k], [pg.k])
                    pu = psn()
                    for kc in range(8):
                        mm(pu.t[:, 0:TT], w4[:, kc, 1, s_ * 128:(s_ + 1) * 128], xn.t[:, kc, 0:TT], kc == 0, kc == 7, [xn.k, ws.k], [pu.k])
                    gu = tmp()
                    gu2 = tmp()
                    act(gu.f[:, 0:TT], pg.t[:, 0:TT], AF.Copy, [pg.k], [gu.k])
                    gc = gu2
                    act(gc.f[:, 0:TT], pg.t[:, 0:TT], AF.Identity, [pg.k, pA.k], [gc.k], bias=pA.t[:, 66 + j:67 + j], scale=pA.t[:, 44 + j:45 + j])
                    if TT > 1:
                        stt(gc.f[:, 1:TT], gu.f[:, 0:TT - 1], pA.t[:, 22 + j:23 + j], gc.f[:, 1:TT], ALU.mult, ALU.add, [gu.k, gc.k, pA.k], [gc.k])
                    stt(gc.f[:, 0:1], guh[l].t[:, j, 1:2], pA.t[:, 22 + j:23 + j], gc.f[:, 0:1], ALU.mult, ALU.add, [guh[l].k, gc.k, pA.k], [gc.k])
                    stt(gc.f[:, 2:TT], gu.f[:, 0:TT - 2], pA.t[:, j:j + 1], gc.f[:, 2:TT], ALU.mult, ALU.add, [gu.k, gc.k, pA.k], [gc.k])
                    stt(gc.f[:, 0:2], guh[l].t[:, j, 0:2], pA.t[:, j:j + 1], gc.f[:, 0:2], ALU.mult, ALU.add, [guh[l].k, gc.k, pA.k], [gc.k])
                    cp(guh[l].t[:, j, :], gu.f[:, TT - 2:TT], [gu.k, gc.k], [guh[l].k], eng="gpsimd")
                    act(gc.f[:, 0:TT], gc.f[:, 0:TT], AF.Gelu_apprx_tanh, [gc.k], [gc.k])
                    tt(fT_t[:, j, 0:TT], gc.f[:, 0:TT], pu.t[:, 0:TT], ALU.mult, [gc.k, pu.k], [fT[j].k])
            S.phase = 'ffn_down'
            for m in range(8):
                ws = wload(l, f"D{m}")
                w3 = ws.t[:, 0:NFC * 128].rearrange("p (kc n) -> p kc n", kc=NFC)
                p = psn()
                for kc in range(NFC):
                    mm(p.t[:, 0:TT], w3[:, kc, :], fT_t[:, kc, 0:TT], kc == 0, kc == NFC - 1, [fT[kc].k, ws.k], [p.k])
                tt(xT.t[:, m, 0:TT], xT.t[:, m, 0:TT], p.t[:, 0:TT], ALU.add, [xT.k, p.k], [xT.k])

            S.phase = 'stateout'
            if is_last:
                ot = otile()
                dma(outs["gla"][l].rearrange("h k v -> k h v"), gS[l].t[:, :, :], gSk[l], [ot], eng="gpsimd")
                for j in range(3):
                    ot = otile()
                    dma(outs["lc"][l][j].rearrange("(c p) -> p c", p=128), lxh[l].t[:, :, j], [lxh[l].k], [ot], allow_slow_non_contiguous=True)
                ot = otile()
                dma(outs["lh"][l].rearrange("(c p) -> p c", p=128), hst[l].t[:, :], [hst[l].k], [ot], allow_slow_non_contiguous=True)
                for j in range(2):
                    ot = otile()
                    dma(outs["fc"][l][j].rearrange("(c p) -> p c", p=128), guh[l].t[:, :, j], [guh[l].k], [ot], allow_slow_non_contiguous=True)

        S.phase = 'yout'
        p = psn()
        for c in range(8):
            sq = tmp()
            act(sq.b[:, 0:TT], xT.t[:, c, 0:TT], AF.Square, [xT.k], [sq.k])
            mm(p.t[:, 0:TT], onesb.t[:, :], sq.b[:, 0:TT], c == 0, c == 7, [onesb.k, sq.k], [p.k])
        rsf = lxT.t[:, 0, 0:TT]
        act(rsf, p.t[:, 0:TT], AF.Ln, [p.k], [lxT.k], bias=eps_ap, scale=1.0 / D)
        act(rsf, rsf, AF.Exp, [lxT.k], [lxT.k], scale=-0.5)
        base2, yk = tmp8()

        def ytok(bi):
            return pool_t[0:BP, base2 + 2 * bi: base2 + 2 * bi + 2, :].rearrange("p a n -> p (a n)")

        for c in range(8):
            yc = tmp()
            stt(yc.f[:, 0:TT], xT.t[:, c, 0:TT], parB[0].t[:, 38 + c:39 + c], rsf, ALU.mult, ALU.mult, [xT.k, lxT.k, parB[0].k], [yc.k])
            pt = psn()
            for bi in range(NB):
                tr(pt.t[0:BP, bi * 128:(bi + 1) * 128], yc.f[:, bi * BP:(bi + 1) * BP], identf.t[:, :], [yc.k, identf.k], [pt.k])
            for bi in range(NB):
                evac(ytok(bi)[:, c * 128:(c + 1) * 128], pt.t[0:BP, bi * 128:(bi + 1) * 128], [pt.k], [yk[2 * bi], yk[2 * bi + 1]])
        for bi in range(NB):
            ot = otile()
            dma(ysrc[bi * BP:(bi + 1) * BP, :], ytok(bi), [yk[2 * bi], yk[2 * bi + 1]], [ot], eng="gpsimd")

    epsb = mk("epsb", [128, 1], F32)
    mset(epsb.t[:], EPS, [], [epsb.k])
    eps_ap = epsb.t[:, 0:1]
    _orig_act = act

    def act(out, in_, func, reads, writes, bias=None, scale=None):
        if bias is eps_ap:
            reads = list(reads) + [epsb.k]
        _orig_act(out, in_, func, reads, writes, bias=bias, scale=scale)

    if cfg["samples"]:
        for s in range(2):
            outs = {
                "k": [ks_o[l, s] for l in range(DEPTH)], "v": [vs_o[l, s] for l in range(DEPTH)],
                "gla": [glas[l, s] for l in range(DEPTH)], "lc": [lcs[l, s] for l in range(DEPTH)],
                "lh": [lhs_o[l, s] for l in range(DEPTH)], "fc": [fcs[l, s] for l in range(DEPTH)],
            }
            run_tile(s, DSEQ, DSEQ, 1, 32, 1, xs[s], ys[s], ropes, 0, PAST // 512, True, outs)
    npt = cfg["n_ptiles"]
    for ti in range(npt):
        t0 = ti * TTP
        outs = {
            "k": [kp[l, t0:t0 + TTP] for l in range(DEPTH)], "v": [vp[l, t0:t0 + TTP] for l in range(DEPTH)],
            "gla": [glap[l] for l in range(DEPTH)], "lc": [lcp[l] for l in range(DEPTH)],
            "lh": [lhp[l] for l in range(DEPTH)], "fc": [fcp[l] for l in range(DEPTH)],
        }
        run_tile('p', TTP, 128, 4, 64, 8, xp[t0:t0 + TTP], yp[t0:t0 + TTP], ropep[t0:t0 + TTP], t0, ti, ti == npt - 1, outs)

    S.op("sync", lambda e: e.nop(), reads=out_tiles)
    S.emit()
    return nc, S


def _rope_table(pos):
    half = 8
    inv = (np.float32(500000.0) ** (-np.arange(half, dtype=np.float32) / np.float32(half))).astype(np.float32)
    ang = (pos.astype(np.float32)[:, None] * inv[None, :]).astype(np.float32)
    cos = np.cos(ang).astype(np.float32)
    sin = np.sin(ang).astype(np.float32)
    return np.concatenate([np.tile(cos, (1, 8)), np.tile(sin, (1, 8))], axis=1).astype(np.float32)


_WNAMES = ["norm_mix", "w_in", "lambda_qk", "attn_subln", "w_gla_gate2", "b_gla_gate", "gla_norm", "lru_conv_w",
           "lru_conv_b", "lru_wa", "lru_ba", "lru_wx", "lru_bx", "lru_lambda", "w_branch_attn", "w_branch_gla",
           "w_branch_lru", "w_merge", "b_merge", "w_out", "norm_ffn", "w_ffn_gate", "ffn_conv_w", "ffn_conv_b",
           "w_ffn_up", "w_ffn_down", "norm_final"]


def kernel(**inp):
    cfg = dict(CFG)
    nc, S = build_program(cfg)
    f = lambda a: np.ascontiguousarray(np.asarray(a, dtype=np.float32))
    NLc = cfg['depth']
    wts = {n: (f(inp[n]) if n == 'norm_final' else f(np.asarray(inp[n])[:NLc])) for n in _WNAMES}
    consts = {
        "ropep": _rope_table(np.arange(SEQ)),
        "ropes": _rope_table(PAST + np.arange(DSEQ)),
        "cmask": (np.arange(TTP) % 64 != 0).astype(np.float32)[None, :],
        "tri": np.triu(np.ones((64, 64), np.float32)),
        "ident": np.eye(128, dtype=np.float32),
    }
    x_prompt = f(inp["x_prompt"])
    x_sample = f(inp["x_sample"])
    ckf = np.asarray(inp["cache_attn_k"], dtype=np.float32).reshape(DEPTH, 16, PAST, 512)
    cvf = np.asarray(inp["cache_attn_v"], dtype=np.float32).reshape(DEPTH, 16, PAST, 512)
    sgf = f(inp["state_gla"])
    slcf = f(inp["state_lru_conv"])
    slhf = f(inp["state_lru_h"])
    sfcf = f(inp["state_ffn_conv"])
    PCORES = [0, 1, 4, 5]
    zero_x = np.zeros((SEQ, D), np.float32)
    in_maps = []
    for c in range(8):
        s0 = 2 * c
        m = {
            "xp": x_prompt[PCORES.index(c)] if c in PCORES else zero_x,
            "xs": x_sample[s0:s0 + 2],
            "ck": np.ascontiguousarray(ckf[:, s0:s0 + 2]),
            "cv": np.ascontiguousarray(cvf[:, s0:s0 + 2]),
            "sg": np.ascontiguousarray(sgf[:, s0:s0 + 2]),
            "slc": np.ascontiguousarray(slcf[:, s0:s0 + 2]),
            "slh": np.ascontiguousarray(slhf[:, s0:s0 + 2]),
            "sfc": np.ascontiguousarray(sfcf[:, s0:s0 + 2]),
        }
        m.update(wts)
        m.update(consts)
        in_maps.append(m)
    res = run_bass_kernel_spmd(nc, in_maps, core_ids=list(range(8))).results
    B = 4
    y_prompt = np.stack([res[b]["yp"] for b in PCORES])
    k_p = np.stack([res[b]["kp"] for b in PCORES], axis=1).reshape(DEPTH, B, SEQ, 4, 2, 64)
    v_p = np.stack([res[b]["vp"] for b in PCORES], axis=1).reshape(DEPTH, B, SEQ, 4, 128)
    gla_p = np.stack([res[b]["glap"] for b in PCORES], axis=1)
    lc_p = np.stack([res[b]["lcp"] for b in PCORES], axis=1)
    lh_p = np.stack([res[b]["lhp"] for b in PCORES], axis=1)
    fc_p = np.stack([res[b]["fcp"] for b in PCORES], axis=1)
    y_sample = np.concatenate([res[c]["ys"] for c in range(8)], axis=0)
    k_s = np.concatenate([res[c]["ks"] for c in range(8)], axis=1).reshape(DEPTH, 16, DSEQ, 4, 2, 64)
    v_s = np.concatenate([res[c]["vs"] for c in range(8)], axis=1).reshape(DEPTH, 16, DSEQ, 4, 128)
    gla_s = np.concatenate([res[c]["glas"] for c in range(8)], axis=1)
    lc_s = np.concatenate([res[c]["lcs"] for c in range(8)], axis=1)
    lh_s = np.concatenate([res[c]["lhs"] for c in range(8)], axis=1)
    fc_s = np.concatenate([res[c]["fcs"] for c in range(8)], axis=1)
    outs = (y_prompt, y_sample, k_p, v_p, gla_p, lc_p, lh_p, fc_p, k_s, v_s, gla_s, lc_s, lh_s, fc_s)
    return tuple(np.ascontiguousarray(o, dtype=np.float32) for o in outs)
```
